# Optimizing a Trainium2 kernel written in Bass

```python
import jax, jax.numpy as jnp
from jax import lax
import numpy as np

D_MODEL = 1024
BATCH = 2
SEQ = 8192
DEPTH = 1

CHUNK = 64
N_MEM = 256
EPS = 1e-6

SWA_HEADS = 16
SWA_KV_HEADS = 2
SWA_HEAD_DIM = 64
SWA_GROUP = SWA_HEADS // SWA_KV_HEADS
SWA_WINDOW = 128
SWA_WIN_CHUNKS = SWA_WINDOW // CHUNK
SWA_BLOCK = 128

GLA_HEADS = 4
GLA_KEY_DIM = D_MODEL // 2
GLA_VAL_DIM = D_MODEL
GLA_DK = GLA_KEY_DIM // GLA_HEADS
GLA_DV = GLA_VAL_DIM // GLA_HEADS
GLA_GATE_RANK = 16
GLA_GATE_NORM = 16.0

MEM_HEADS = 4
MEM_HEAD_DIM = 64
MEM_WIDTH = MEM_HEADS * MEM_HEAD_DIM

D_FF = -(-(8 * D_MODEL) // (3 * 256)) * 256

IN_SIZES = (SWA_HEADS * SWA_HEAD_DIM, SWA_KV_HEADS * SWA_HEAD_DIM, SWA_KV_HEADS * SWA_HEAD_DIM,
            GLA_KEY_DIM, GLA_KEY_DIM, GLA_VAL_DIM, GLA_VAL_DIM, GLA_GATE_RANK, 2 * D_MODEL)
IN_COLS = int(sum(IN_SIZES))
IN_OFFSETS = tuple(int(o) for o in np.cumsum(IN_SIZES)[:-1])

kernel_name = 'hybrid_swa_sinks_gla_memory_block'


def _rms(t, w):
    tf = t.astype(jnp.float32)
    y = tf * lax.rsqrt(jnp.mean(tf * tf, axis=-1, keepdims=True) + EPS)
    return y.astype(t.dtype) * w


def _swa_with_sinks(q, k, v, sinks):
    B, S = q.shape[0], q.shape[1]
    nb = S // SWA_BLOCK
    qb = q.reshape(B, nb, SWA_BLOCK, SWA_KV_HEADS, SWA_GROUP, SWA_HEAD_DIM)
    pad = ((0, 0), (SWA_BLOCK, 0), (0, 0), (0, 0))
    kp = jnp.pad(k, pad).reshape(B, nb + 1, SWA_BLOCK, SWA_KV_HEADS, SWA_HEAD_DIM)
    vp = jnp.pad(v, pad).reshape(B, nb + 1, SWA_BLOCK, SWA_KV_HEADS, SWA_HEAD_DIM)
    kb = jnp.concatenate([kp[:, :-1], kp[:, 1:]], axis=2)
    vb = jnp.concatenate([vp[:, :-1], vp[:, 1:]], axis=2)
    scale = SWA_HEAD_DIM ** -0.5
    s = jnp.einsum('bnqhgd,bnshd->bnhgqs', qb, kb).astype(jnp.float32) * scale
    n = jnp.arange(nb)[:, None, None]
    qpos = n * SWA_BLOCK + jnp.arange(SWA_BLOCK)[None, :, None]
    kpos = (n - 1) * SWA_BLOCK + jnp.arange(2 * SWA_BLOCK)[None, None, :]
    qch = qpos // CHUNK
    kch = kpos // CHUNK
    mask = (kpos >= 0) & (kch <= qch) & (kch >= qch - SWA_WIN_CHUNKS)
    s = jnp.where(mask[None, :, None, None], s, -1e30)
    sink = sinks.astype(jnp.float32).reshape(1, 1, SWA_KV_HEADS, SWA_GROUP, 1, 1)
    m = jnp.maximum(jnp.max(s, axis=-1, keepdims=True), sink)
    p = jnp.exp(s - m)
    denom = jnp.sum(p, axis=-1, keepdims=True) + jnp.exp(sink - m)
    probs = (p / denom).astype(v.dtype)
    o = jnp.einsum('bnhgqs,bnshd->bnqhgd', probs, vb)
    return o.reshape(B, S, SWA_HEADS * SWA_HEAD_DIM)


def _gla_chunk_causal(q, k, v, gk):
    B, S = q.shape[0], q.shape[1]
    nc = S // CHUNK
    f32 = jnp.float32
    r = lambda t: t.astype(f32).reshape(B, nc, CHUNK, GLA_HEADS, t.shape[-1])
    qc = r(q) * (GLA_DK ** -0.5)
    kc, vc, gc = r(k), r(v), r(gk)
    b = jnp.cumsum(gc, axis=2)
    b_end = b[:, :, -1:]
    k_dec = kc * jnp.exp(b_end - b)
    a = jnp.exp(b_end[:, :, 0])

    def step(state, xs):
        q_c, k_c, v_c, a_c = xs
        state = a_c[..., None] * state + jnp.einsum('bchk,bchv->bhkv', k_c, v_c)
        o_c = jnp.einsum('bchk,bhkv->bchv', q_c, state)
        return state, o_c

    xs = (jnp.moveaxis(qc, 1, 0), jnp.moveaxis(k_dec, 1, 0), jnp.moveaxis(vc, 1, 0), jnp.moveaxis(a, 1, 0))
    s0 = jnp.zeros((B, GLA_HEADS, GLA_DK, GLA_DV), f32)
    _, o = lax.scan(step, s0, xs)
    return jnp.moveaxis(o, 0, 1).reshape(B, S, GLA_HEADS, GLA_DV).astype(q.dtype)


def _mixer_block(h, w_in, b_gate, attn_sinks, gla_gate_w2, gla_gate_b, gla_norm_w, w_attn_o, w_gla_o, w_mix_o):
    B, S, _ = h.shape
    proj = h @ w_in
    q_a, k_a, v_a, q_g, k_g, v_g, g_g, a_lr, gates = jnp.split(proj, IN_OFFSETS, axis=-1)
    o_a = _swa_with_sinks(q_a.reshape(B, S, SWA_HEADS, SWA_HEAD_DIM),
                          k_a.reshape(B, S, SWA_KV_HEADS, SWA_HEAD_DIM),
                          v_a.reshape(B, S, SWA_KV_HEADS, SWA_HEAD_DIM), attn_sinks)
    gk = jax.nn.log_sigmoid((a_lr @ gla_gate_w2 + gla_gate_b).astype(jnp.float32)) / GLA_GATE_NORM
    o_g = _gla_chunk_causal(q_g.reshape(B, S, GLA_HEADS, GLA_DK),
                            k_g.reshape(B, S, GLA_HEADS, GLA_DK),
                            v_g.reshape(B, S, GLA_HEADS, GLA_DV),
                            gk.reshape(B, S, GLA_HEADS, GLA_DK))
    o_g = _rms(o_g, gla_norm_w) * jax.nn.silu(g_g.reshape(B, S, GLA_HEADS, GLA_DV))
    o_g = o_g.reshape(B, S, GLA_VAL_DIM)
    g_a, g_b = jnp.split(jax.nn.sigmoid(gates + b_gate), 2, axis=-1)
    merged = g_a * (o_a @ w_attn_o) + g_b * (o_g @ w_gla_o)
    return merged @ w_mix_o


def _memory_xattn(h, m, w_mem_q, w_mem_kv, w_mem_o):
    B, S, _ = h.shape
    q = (h @ w_mem_q).reshape(B, S, MEM_HEADS, MEM_HEAD_DIM)
    k, v = jnp.split(m @ w_mem_kv, 2, axis=-1)
    k = k.reshape(B, m.shape[1], MEM_HEADS, MEM_HEAD_DIM)
    v = v.reshape(B, m.shape[1], MEM_HEADS, MEM_HEAD_DIM)
    s = jnp.einsum('bshd,bmhd->bhsm', q, k).astype(jnp.float32) * (MEM_HEAD_DIM ** -0.5)
    p = jax.nn.softmax(s, axis=-1).astype(v.dtype)
    o = jnp.einsum('bhsm,bmhd->bshd', p, v).reshape(B, S, MEM_WIDTH)
    return o @ w_mem_o


def _swiglu(h, w_gate, w_up, w_down):
    return (jax.nn.silu(h @ w_gate) * (h @ w_up)) @ w_down


def setup_inputs(seed: int = 0) -> dict:
    key = jax.random.key(seed)
    ks = jax.random.split(key, 24)
    f32 = jnp.float32
    L = DEPTH

    def nrm(k, shape, fan_in):
        return jax.random.normal(k, shape, f32) * (fan_in ** -0.5)

    def gain(k, shape):
        return 1.0 + 0.02 * jax.random.normal(k, shape, f32)

    return {
        'x': jax.random.normal(ks[0], (BATCH, SEQ, D_MODEL), f32),
        'mem': jax.random.normal(ks[1], (BATCH, N_MEM, D_MODEL), f32),
        'norm_mix_w': gain(ks[2], (L, D_MODEL)),
        'w_in': nrm(ks[3], (L, D_MODEL, IN_COLS), D_MODEL),
        'b_gate': 0.02 * jax.random.normal(ks[4], (L, 2 * D_MODEL), f32),
        'attn_sinks': 0.5 * jax.random.normal(ks[5], (L, SWA_HEADS), f32),
        'gla_gate_w2': nrm(ks[6], (L, GLA_GATE_RANK, GLA_KEY_DIM), GLA_GATE_RANK),
        'gla_gate_b': 0.1 * jax.random.normal(ks[7], (L, GLA_KEY_DIM), f32),
        'gla_norm_w': gain(ks[8], (L, GLA_DV)),
        'w_attn_o': nrm(ks[9], (L, SWA_HEADS * SWA_HEAD_DIM, D_MODEL), SWA_HEADS * SWA_HEAD_DIM),
        'w_gla_o': nrm(ks[10], (L, GLA_VAL_DIM, D_MODEL), GLA_VAL_DIM),
        'w_mix_o': nrm(ks[11], (L, D_MODEL, D_MODEL), D_MODEL),
        'norm_mem_q_w': gain(ks[12], (L, D_MODEL)),
        'norm_mem_kv_w': gain(ks[13], (L, D_MODEL)),
        'w_mem_q': nrm(ks[14], (L, D_MODEL, MEM_WIDTH), D_MODEL),
        'w_mem_kv': nrm(ks[15], (L, D_MODEL, 2 * MEM_WIDTH), D_MODEL),
        'w_mem_o': nrm(ks[16], (L, MEM_WIDTH, D_MODEL), MEM_WIDTH),
        'norm_ffn_w': gain(ks[17], (L, D_MODEL)),
        'w_ffn_gate': nrm(ks[18], (L, D_MODEL, D_FF), D_MODEL),
        'w_ffn_up': nrm(ks[19], (L, D_MODEL, D_FF), D_MODEL),
        'w_ffn_down': nrm(ks[20], (L, D_FF, D_MODEL), D_FF),
        'norm_final_w': gain(ks[21], (D_MODEL,)),
    }


def reference(x, mem, norm_mix_w, w_in, b_gate, attn_sinks, gla_gate_w2, gla_gate_b, gla_norm_w,
              w_attn_o, w_gla_o, w_mix_o, norm_mem_q_w, norm_mem_kv_w, w_mem_q, w_mem_kv, w_mem_o,
              norm_ffn_w, w_ffn_gate, w_ffn_up, w_ffn_down, norm_final_w):
    for l in range(DEPTH):
        h = _rms(x, norm_mix_w[l])
        x = x + _mixer_block(h, w_in[l], b_gate[l], attn_sinks[l], gla_gate_w2[l], gla_gate_b[l],
                             gla_norm_w[l], w_attn_o[l], w_gla_o[l], w_mix_o[l])
        x = x + _memory_xattn(_rms(x, norm_mem_q_w[l]), _rms(mem, norm_mem_kv_w[l]),
                              w_mem_q[l], w_mem_kv[l], w_mem_o[l])
        x = x + _swiglu(_rms(x, norm_ffn_w[l]), w_ffn_gate[l], w_ffn_up[l], w_ffn_down[l])
    return _rms(x, norm_final_w)
```

```python
import contextlib
import numpy as np
import concourse.bass as bass
import concourse.mybir as mybir
from concourse.bass_utils import run_bass_kernel_spmd

F32 = mybir.dt.float32
BF16 = mybir.dt.bfloat16
AF = mybir.ActivationFunctionType
ALU = mybir.AluOpType

D = 1024
NCORE = 8
TOK_CORE = 2048
DFF = 2816
INC = 6416
O_QA, O_KA, O_VA, O_QG, O_KG, O_VG, O_GG, O_AL, O_GT = 0, 1024, 1152, 1280, 1792, 2304, 3328, 4352, 4368


class _Stop(Exception):
    pass


class Sched:
    ENGS = ("pe", "act", "dve", "pool", "sp")

    def __init__(self, nc, same_engine_sync=True):
        self.nc = nc
        self.ops = []
        self.last_write = {}
        self.reads_since = {}
        self.dma_count = {}
        self.same_engine_sync = same_engine_sync

    def _add(self, eng, fn, reads, writes, dma_key=None):
        idx = len(self.ops)
        deps = set()
        for r in reads:
            lw = self.last_write.get(r)
            if lw is not None:
                deps.add(lw)
        for w in writes:
            lw = self.last_write.get(w)
            if lw is not None:
                deps.add(lw)
            for rd in self.reads_since.get(w, ()):
                deps.add(rd)
        for r in reads:
            self.reads_since.setdefault(r, []).append(idx)
        for w in writes:
            self.last_write[w] = idx
            self.reads_since[w] = []
        deps.discard(idx)
        op = dict(eng=eng, fn=fn, deps=deps, dma_key=dma_key, signal=False, idx=idx)
        if dma_key is not None:
            c = self.dma_count.get(dma_key, 0) + 16
            self.dma_count[dma_key] = c
            op["dma_val"] = c
        self.ops.append(op)
        return idx

    def pe(self, fn, reads=(), writes=()):
        return self._add("pe", fn, reads, writes)

    def act(self, fn, reads=(), writes=()):
        return self._add("act", fn, reads, writes)

    def dve(self, fn, reads=(), writes=()):
        return self._add("dve", fn, reads, writes)

    def pool(self, fn, reads=(), writes=()):
        return self._add("pool", fn, reads, writes)

    def dma(self, eng, fn, key, reads=(), writes=()):
        return self._add(eng, fn, reads, writes, dma_key=key)

    def emit(self, final_wait_keys=()):
        nc = self.nc
        ops = self.ops
        need = []
        for op in ops:
            nd = []
            for d in sorted(op["deps"]):
                y = ops[d]
                if y["dma_key"] is None and y["eng"] == op["eng"] and op["dma_key"] is None:
                    if op["eng"] == "pe" or not self.same_engine_sync:
                        continue
                nd.append(d)
                if y["dma_key"] is None:
                    y["signal"] = True
            need.append(nd)
        cnt = {e: 0 for e in self.ENGS}
        for op in ops:
            if op["dma_key"] is None and op["signal"]:
                cnt[op["eng"]] += 1
                op["sig_val"] = cnt[op["eng"]]
        with contextlib.ExitStack() as st:
            esem = {e: st.enter_context(nc.semaphore("s_" + e)) for e in self.ENGS}
            dsem = {k: st.enter_context(nc.semaphore("d_%d" % i)) for i, k in enumerate(self.dma_count)}
            block = st.enter_context(nc.Block())
            per = {e: [] for e in self.ENGS}
            for op, nd in zip(ops, need):
                per[op["eng"]].append((op, nd))

            def body(ename, handle):
                waited = {}
                for op, nd in per[ename]:
                    for d in nd:
                        y = ops[d]
                        if y["dma_key"] is not None:
                            sem, val, k = dsem[y["dma_key"]], y["dma_val"], ("d", y["dma_key"])
                        else:
                            sem, val, k = esem[y["eng"]], y["sig_val"], ("e", y["eng"])
                        if waited.get(k, 0) >= val:
                            continue
                        waited[k] = val
                        handle.wait_ge(sem, val)
                    ins = op["fn"](handle)
                    if op["dma_key"] is not None:
                        ins.then_inc(dsem[op["dma_key"]], 16)
                    elif op["signal"]:
                        ins.then_inc(esem[ename], 1)
                if ename == "sp":
                    for k in final_wait_keys:
                        handle.wait_ge(dsem[k], self.dma_count[k])

            @block.tensor
            def _(e):
                body("pe", e)

            @block.scalar
            def _(e):
                body("act", e)

            @block.vector
            def _(e):
                body("dve", e)

            @block.gpsimd
            def _(e):
                body("pool", e)

            @block.sync
            def _(e):
                body("sp", e)


def build(NPRE=48, NGRP=4, NSLOT=3, same_engine_sync=True, dbg=False, stop_after=None):
    nc = bass.Bass("TRN2", target_bir_lowering=False)
    NT_ALL = NPRE + NGRP * 4

    def din(name, shape):
        return nc.dram_tensor(name, list(shape), F32, kind="ExternalInput").ap()

    xall = din("xall", [NT_ALL * 128, D])
    mem = din("mem", [256, D])
    flagb = din("flagb", [128, 1])
    ident_d = din("ident", [128, 128])
    ucum64_d = din("ucum64", [128, 128])
    ucum128_d = din("ucum128", [128, 128])
    ind2_d = din("ind2", [128, 2])
    ind1_d = din("ind1", [128, 2])
    norm_mix_w = din("norm_mix_w", [D])
    w_in = din("w_in", [D, INC])
    b_gate = din("b_gate", [2 * D])
    attn_sinks = din("attn_sinks", [16])
    gla_gate_w2 = din("gla_gate_w2", [16, 512])
    gla_gate_b = din("gla_gate_b", [512])
    gla_norm_w = din("gla_norm_w", [256])
    w_attn_o = din("w_attn_o", [D, D])
    w_gla_o = din("w_gla_o", [D, D])
    w_mix_o = din("w_mix_o", [D, D])
    norm_mem_q_w = din("norm_mem_q_w", [D])
    norm_mem_kv_w = din("norm_mem_kv_w", [D])
    w_mem_q = din("w_mem_q", [D, 256])
    w_mem_kv = din("w_mem_kv", [D, 512])
    w_mem_o = din("w_mem_o", [256, D])
    norm_ffn_w = din("norm_ffn_w", [D])
    w_ffn_gate = din("w_ffn_gate", [D, DFF])
    w_ffn_up = din("w_ffn_up", [D, DFF])
    w_ffn_down = din("w_ffn_down", [DFF, D])
    norm_final_w = din("norm_final_w", [D])
    yout = nc.dram_tensor("y", [NGRP * 512, D], F32, kind="ExternalOutput").ap()

    S = Sched(nc, same_engine_sync=same_engine_sync)
    st = contextlib.ExitStack()
    with st:
        def sb(name, shape, dt=F32):
            return st.enter_context(nc.sbuf_tensor("sb_" + name, list(shape), dt))

        psb = [st.enter_context(nc.psum_tensor("ps%d" % i, [128, 512], F32)) for i in range(8)]
        pcnt = {"any": 0, "r0": 0, "r1": 0}
        PSETS = {"any": [0, 1, 2, 3, 4, 5, 6, 7], "r0": [0, 1, 4, 5], "r1": [2, 3, 6, 7]}

        def newps(kind="any"):
            lst = PSETS[kind]
            i = lst[pcnt[kind] % len(lst)]
            pcnt[kind] += 1
            return psb[i], "ps%d" % i

        idt = sb("idt", [128, 128])
        uc64 = sb("uc64", [128, 128])
        uc128 = sb("uc128", [128, 128])
        ind2 = sb("ind2", [128, 2])
        ind1 = sb("ind1", [128, 2])
        flg = sb("flg", [128, 1])
        wn1 = sb("wn1", [128, 8])
        wn2 = sb("wn2", [128, 8])
        wn3 = sb("wn3", [128, 8])
        wnkv = sb("wnkv", [128, 8])
        gnw8 = sb("gnw8", [128, 8])
        bgate = sb("bgate", [128, 16])
        exps = sb("exps", [128, 16])
        wfin = sb("wfin", [128, D])
        w2aug = sb("w2aug", [32, 512], BF16)
        wkdup = sb("wkdup", [128, 8, 2, 128], BF16)
        wva = sb("wva", [128, 8, 128], BF16)
        walr = sb("walr", [128, 8, 16], BF16)
        wmq = sb("wmq", [128, 8, 256], BF16)
        wmo = sb("wmo", [128, 2, D], BF16)
        big = sb("big", [128, 8 * 1536], BF16)
        wpre = big[:, :].rearrange("p (k c) -> p k c", k=8)
        actT = big[:, 0:22 * 512].rearrange("p (f t) -> p f t", f=22)
        kmT = sb("kmT", [128, 2, 256], BF16)
        vmaug = sb("vmaug", [128, 2, 4, 65], BF16)
        wsl = [sb("wsl%d" % i, [128, 8, 512], BF16) for i in range(NSLOT)]
        xg = sb("xg", [128, 4, D])
        xs = sb("xs", [128, D])
        junk = sb("junk", [128, D], BF16)
        ss = sb("ss", [128, 8])
        rs = sb("rs", [128, 8])
        hT = sb("hT", [128, 8, 512], BF16)
        qT = sb("qT", [128, 8, 512], BF16)
        kT = sb("kT", [128, 2, 640], BF16)
        vaug = sb("vaug", [128, 5, 2, 65], BF16)
        PT = [sb("PT%d" % i, [128, 2, 4, 128], BF16) for i in range(2)]
        oa = sb("oa", [128, D])
        oaT = sb("oaT", [128, 8, 512], BF16)
        den = sb("den", [128, 16])
        qgT = sb("qgT", [128, 4, 512], BF16)
        alrT = sb("alrT", [32, 512], BF16)
        kg = sb("kg", [128, 4, 512])
        vg = sb("vg", [128, 4, D], BF16)
        sg = sb("sg", [128, 4, D], BF16)
        Gf = sb("Gf", [128, 512])
        Dm = sb("Dm", [128, 512])
        av = sb("av", [128, 8])
        kdec = sb("kdec", [128, 512], BF16)
        Sst = sb("Sst", [128, D])
        Sbf = [sb("Sbf%d" % i, [128, D], BF16) for i in range(2)]
        og = sb("og", [128, D])
        ogT = sb("ogT", [128, 8, 512], BF16)
        gT = sb("gT", [128, 16, 512], BF16)
        m1 = sg[:, :, :].rearrange("p a (b c) -> p (a b) c", b=2)
        mtmp = Gf
        mT = qT
        qmT = vg[:, 0, :].rearrange("p (a b) -> p a b", a=2)
        PmT = vg[:, 1, :].rearrange("p (r m c q) -> p r m c q", r=2, m=2, c=2)
        omT = vg[:, 2, :].rearrange("p (a b) -> p a b", a=2)
        om = sb("om", [128, 256])
        sl = [sb("sl%d" % i, [128, 512], BF16) for i in range(2)]
        yst = xs
        xm = oa

        HT_ALL = ["hT0", "hT1", "hT2", "hT3"]

        def chk(name):
            if stop_after == name:
                raise _Stop()

        dbg_keys = []

        def dump(name, ap, ncols, reads):
            if not dbg:
                return
            d_ = nc.dram_tensor("dbg_" + name, [128, ncols], F32, kind="ExternalOutput").ap()
            k_ = "dbg_" + name
            dbg_keys.append(k_)
            S.dma("sp", lambda e: e.dma_start(out=d_, in_=ap), k_, reads=reads)

        ukey = [0]

        def ld(eng, out_ap, in_ap, key, writes, reads=(), slow=False):
            if key in ("c", "cw"):
                ukey[0] += 1
                if key == "cw":
                    reads = list(reads) + ["cwtok%d" % (ukey[0] % 2)]
                    writes = list(writes) + ["cwtok%d" % (ukey[0] % 2)]
                key = "%s%d" % (key, ukey[0])
            if slow:
                S.dma(eng, lambda e: e.dma_start(out=out_ap, in_=in_ap, allow_slow_non_contiguous=True), key, reads=reads, writes=writes)
            else:
                S.dma(eng, lambda e: e.dma_start(out=out_ap, in_=in_ap), key, reads=reads, writes=writes)

        slot_i = [0]

        def wload(W, k0, nk, c0, ncols):
            i = slot_i[0] % NSLOT
            slot_i[0] += 1
            src = W[k0 * 128:(k0 + nk) * 128, c0:c0 + ncols].rearrange("(k p) c -> p k c", p=128)
            ld("pool", wsl[i][:, 0:nk, 0:ncols], src, "wsl%d" % i, writes=["wsl%d" % i])
            return wsl[i], "wsl%d" % i

        def mm(out_ap, lhsT, rhs, start, stop, reads, writes):
            S.pe(lambda e: e.matmul(out=out_ap, lhsT=lhsT, rhs=rhs, start=start, stop=stop), reads=reads, writes=writes)

        def tr(out_ap, in_ap, reads, writes):
            S.pe(lambda e: e.transpose(out=out_ap, in_=in_ap, identity=idt[:]), reads=list(reads) + ["idt"], writes=writes)

        def act(out_ap, in_ap, func, reads, writes, **kw):
            S.act(lambda e: e.activation(out=out_ap, in_=in_ap, func=func, **kw), reads=reads, writes=writes)

        def rstd_from_ss(col, n_inv, ncol=1):
            act(rs[:, col:col + ncol], ss[:, col:col + ncol], AF.Sqrt, ["ss"], ["rs"], scale=n_inv, bias=1e-6)
            S.dve(lambda e: e.reciprocal(out=rs[:, col:col + ncol], in_=rs[:, col:col + ncol]), reads=["rs"], writes=["rs"])

        def norm_T(x_ap, xres, wn, wnres, dst, dstres):
            act(junk[:], x_ap, AF.Square, [xres], ["junk", "ss"], accum_out=ss[:, 0:1])
            rstd_from_ss(0, 1.0 / D)
            S.dve(lambda e: e.tensor_scalar(out=xs[:], in0=x_ap, scalar1=rs[:, 0:1], scalar2=None, op0=ALU.mult),
                  reads=[xres, "rs"], writes=["xs"])
            for half in range(2):
                pp, pr = newps()
                for k in range(4):
                    kk = half * 4 + k
                    tr(pp[:, k * 128:(k + 1) * 128], xs[:, kk * 128:(kk + 1) * 128], ["xs"], [pr])
                S.dve(lambda e, pp=pp, half=half: e.tensor_tensor(
                    out=dst[:, half * 4:(half + 1) * 4, :], in0=pp[:, :].rearrange("p (a b) -> p a b", a=4),
                    in1=wn[:, half * 4:(half + 1) * 4].unsqueeze(2).to_broadcast([128, 4, 128]), op=ALU.mult),
                    reads=[pr, wnres], writes=[dstres])

        def tm_proj(lhs_tile, lhs_res, wt, wres, c0, ncols, evac):
            pp, pr = newps()
            for k in range(8):
                mm(pp[:, 0:ncols], lhs_tile[:, k, :], wt[:, k, c0:c0 + ncols], k == 0, k == 7, [lhs_res, wres], [pr])
            evac(pp, pr)

        def fm_proj(wt, wres, c0, m, rhs, rhs_res, ntok, evac, nk=8):
            pp, pr = newps()
            rr = list(rhs_res) if isinstance(rhs_res, (list, tuple)) else [rhs_res]
            for k in range(nk):
                mm(pp[0:m, 0:ntok], wt[:, k, c0:c0 + m], rhs[:, k, 0:ntok], k == 0, k == nk - 1, rr + [wres], [pr])
            evac(pp, pr)

        def gla_gate(tokc0, umat, ures, indt, indres):
            pz, pzr = newps()
            mm(pz[:, :], alrT[0:17, tokc0:tokc0 + 128], w2aug[0:17, :], True, True, ["alrT", "w2aug"], [pzr])
            act(Gf[:], pz[:, :], AF.Exp, [pzr], ["Gf"], scale=-1.0)
            act(Gf[:], Gf[:], AF.Ln, ["Gf"], ["Gf"], bias=1.0)
            pR, pRr = newps()
            mm(pR[:, :], umat[:], Gf[:], True, True, [ures, "Gf"], [pRr])
            pa, par = newps()
            for h in range(4):
                mm(pa[:, 2 * h:2 * h + 2], Gf[:, h * 128:(h + 1) * 128], indt[:], True, True, ["Gf", indres], [par])
            act(Dm[:], pR[:, :], AF.Exp, [pRr], ["Dm"])
            act(av[:], pa[:, 0:8], AF.Exp, [par], ["av"])

        def state_update(krows, j, vsrc, vres, kind):
            r0, r1 = krows
            pA, pAr = newps(kind)
            pB, pBr = newps(kind)
            for h in range(4):
                pp, pr = (pA, pAr) if h < 2 else (pB, pBr)
                mm(pp[:, (h % 2) * 256:(h % 2 + 1) * 256], kdec[r0:r1, h * 128:(h + 1) * 128],
                   vsrc[r0:r1, h * 256:(h + 1) * 256], True, True, ["kdec", vres], [pr])
            for h in range(4):
                pp, pr = (pA, pAr) if h < 2 else (pB, pBr)
                S.dve(lambda e, h=h, pp=pp: e.scalar_tensor_tensor(
                    out=Sst[:, h * 256:(h + 1) * 256], in0=Sst[:, h * 256:(h + 1) * 256], scalar=av[:, 2 * h + j:2 * h + j + 1],
                    in1=pp[:, (h % 2) * 256:(h % 2 + 1) * 256], op0=ALU.mult, op1=ALU.add),
                    reads=["Sst", "av", pr], writes=["Sst"])

        try:
            ld("sp", idt[:], ident_d, "c", ["idt"])
            ld("sp", uc64[:], ucum64_d, "c", ["uc64"])
            ld("sp", uc128[:], ucum128_d, "c", ["uc128"])
            ld("sp", ind2[:], ind2_d, "c", ["ind2"])
            ld("sp", ind1[:], ind1_d, "c", ["ind1"])
            ld("sp", flg[:], flagb, "c", ["flg"])
            for t_, src_ in ((wn1, norm_mix_w), (wn2, norm_mem_q_w), (wn3, norm_ffn_w), (wnkv, norm_mem_kv_w)):
                ld("sp", t_[:], src_.rearrange("(k p) -> p k", p=128), "c", ["wn"], slow=True)
            for i in range(4):
                ld("sp", gnw8[:, 2 * i:2 * i + 2], gla_norm_w.rearrange("(k p) -> p k", p=128), "c", ["gnw8"], slow=True)
            ld("sp", bgate[:], b_gate.rearrange("(k p) -> p k", p=128), "c", ["bgate"], slow=True)
            ld("sp", exps[:], attn_sinks.partition_broadcast(128), "c", ["exps"])
            ld("sp", wfin[:], norm_final_w.partition_broadcast(128), "c", ["wfin"])
            CONSTS = ["idt", "uc64", "uc128", "ind2", "ind1", "flg", "wn", "gnw8", "bgate", "exps", "wfin"]
            act(exps[:], exps[:], AF.Exp, ["exps"], ["exps"])
            ld("pool", w2aug[0:16, :], gla_gate_w2, "cw", ["w2aug"])
            ld("pool", w2aug[16:17, :], gla_gate_b.rearrange("(o c) -> o c", o=1), "cw", ["w2aug"])
            for g in range(2):
                for r in range(2):
                    ld("pool", wkdup[:, :, g, 64 * r:64 * r + 64],
                       w_in[:, O_KA + 64 * g:O_KA + 64 * g + 64].rearrange("(k p) c -> p k c", p=128), "cw", ["wkdup"])
            ld("pool", wva[:], w_in[:, O_VA:O_VA + 128].rearrange("(k p) c -> p k c", p=128), "cw", ["wva"])
            ld("pool", walr[:], w_in[:, O_AL:O_AL + 16].rearrange("(k p) c -> p k c", p=128), "cw", ["walr"])
            ld("pool", wmq[:], w_mem_q.rearrange("(k p) c -> p k c", p=128), "cw", ["wmq"])
            ld("pool", wmo[:], w_mem_o.rearrange("(k p) c -> p k c", p=128), "cw", ["wmo"])
            for i in range(3):
                ld("pool", wpre[:, :, i * 512:(i + 1) * 512],
                   w_in[:, O_KG + i * 512:O_KG + (i + 1) * 512].rearrange("(k p) c -> p k c", p=128), "cw", ["big"])
            S.dve(lambda e: e.memset(Sst[:], 0.0), writes=["Sst"])
            S.dve(lambda e: e.memset(alrT[:], 1.0), writes=["alrT"])
            S.dve(lambda e: e.memset(vaug[:], 1.0), writes=["vaug"])
            S.dve(lambda e: e.memset(vmaug[:], 1.0), writes=["vmaug"])
            for i in range(2):
                S.dve(lambda e, i=i: e.memset(PT[i][:], 0.0), writes=["PT%d" % i])

            for mt in range(2):
                ld("sp", xm[:], mem[mt * 128:(mt + 1) * 128, :], "xm", ["oa"])
                norm_T(xm[:], "oa", wnkv, "wn", hT[:, :, mt * 128:(mt + 1) * 128], "hT%d" % mt)
            wt, wr = wload(w_mem_kv, 0, 8, 0, 512)
            for c in range(2):
                fm_proj(wt, wr, c * 128, 128, hT, ["hT0", "hT1"], 256,
                        lambda pp, pr, c=c: act(kmT[:, c, :], pp[:, 0:256], AF.Copy, [pr], ["kmT"]))
            for mt in range(2):
                tm_proj(hT[:, :, mt * 128:(mt + 1) * 128], "hT%d" % mt, wt, wr, 256, 256,
                        lambda pp, pr, mt=mt: act(vmaug[:, mt, :, 0:64], pp[:, 0:256].rearrange("p (h d) -> p h d", h=4), AF.Copy, [pr], ["vmaug"]))

            chk("setup")
            for t in range(NPRE):
                ld("sp", xm[:], xall[t * 128:(t + 1) * 128, :], "xm", ["oa"])
                hTt = hT[:, :, (t % 2) * 128:(t % 2 + 1) * 128]
                hres = "hT%d" % (t % 2)
                norm_T(xm[:], "oa", wn1, "wn", hTt, hres)
                tm_proj(hTt, hres, wpre, "big", 0, 512,
                        lambda pp, pr: act(kg[:, 0, :], pp[:, :], AF.Copy, [pr], ["kg"]))
                for i in range(2):
                    tm_proj(hTt, hres, wpre, "big", 512 + i * 512, 512,
                            lambda pp, pr, i=i: act(vg[:, 0, i * 512:(i + 1) * 512], pp[:, :], AF.Copy, [pr], ["vg", "qmT", "PmT", "omT"]))
                fm_proj(walr, "walr", 0, 16, hTt, hres, 128,
                        lambda pp, pr: act(alrT[0:16, 0:128], pp[0:16, 0:128], AF.Copy, [pr], ["alrT"]))
                gla_gate(0, uc128, "uc128", ind1, "ind1")
                S.dve(lambda e: e.tensor_tensor(out=kdec[:], in0=kg[:, 0, :], in1=Dm[:], op=ALU.mult), reads=["kg", "Dm"], writes=["kdec"])
                state_update((0, 128), 0, vg[:, 0, :], "vg", "any")
                if t == NPRE - 1:
                    for g in range(2):
                        fm_proj(wkdup[:, :, g, :], "wkdup", 0, 128, hTt, hres, 128,
                                lambda pp, pr, g=g: act(kT[:, g, 0:128], pp[:, 0:128], AF.Copy, [pr], ["kT"]))
                    tm_proj(hTt, hres, wva, "wva", 0, 128,
                            lambda pp, pr: act(vaug[:, 0, :, 0:64], pp[:, 0:128].rearrange("p (g d) -> p g d", g=2), AF.Copy, [pr], ["vaug"]))

            chk("prefix")
            for grp in range(NGRP):
                for t in range(4):
                    row0 = (NPRE + grp * 4 + t) * 128
                    ld("sp", xg[:, t, :], xall[row0:row0 + 128, :], "xg%d" % t, ["xg%d" % t])
                    norm_T(xg[:, t, :], "xg%d" % t, wn1, "wn", hT[:, :, t * 128:(t + 1) * 128], "hT%d" % t)
                chk("norm1")
                for i in range(2):
                    wt, wr = wload(w_in, 0, 8, O_QA + i * 512, 512)
                    for c in range(4):
                        fm_proj(wt, wr, c * 128, 128, hT, HT_ALL, 512,
                                lambda pp, pr, cc=i * 4 + c: act(qT[:, cc, :], pp[:, :], AF.Copy, [pr], ["qT"]))
                for g in range(2):
                    fm_proj(wkdup[:, :, g, :], "wkdup", 0, 128, hT, HT_ALL, 512,
                            lambda pp, pr, g=g: act(kT[:, g, 128:640], pp[:, :], AF.Copy, [pr], ["kT"]))
                for t in range(4):
                    tm_proj(hT[:, :, t * 128:(t + 1) * 128], "hT%d" % t, wva, "wva", 0, 128,
                            lambda pp, pr, t=t: act(vaug[:, 1 + t, :, 0:64], pp[:, 0:128].rearrange("p (g d) -> p g d", g=2), AF.Copy, [pr], ["vaug"]))
                wt, wr = wload(w_in, 0, 8, O_QG, 512)
                for c in range(4):
                    fm_proj(wt, wr, c * 128, 128, hT, HT_ALL, 512,
                            lambda pp, pr, c=c: act(qgT[:, c, :], pp[:, :], AF.Copy, [pr], ["qgT"], scale=float(128 ** -0.5)))
                fm_proj(walr, "walr", 0, 16, hT, HT_ALL, 512,
                        lambda pp, pr: act(alrT[0:16, :], pp[0:16, :], AF.Copy, [pr], ["alrT"]))
                wt, wr = wload(w_in, 0, 8, O_KG, 512)
                for t in range(4):
                    tm_proj(hT[:, :, t * 128:(t + 1) * 128], "hT%d" % t, wt, wr, 0, 512,
                            lambda pp, pr, t=t: act(kg[:, t, :], pp[:, :], AF.Copy, [pr], ["kg"]))
                for i in range(2):
                    wt, wr = wload(w_in, 0, 8, O_VG + i * 512, 512)
                    for t in range(4):
                        tm_proj(hT[:, :, t * 128:(t + 1) * 128], "hT%d" % t, wt, wr, 0, 512,
                                lambda pp, pr, t=t, i=i: act(vg[:, t, i * 512:(i + 1) * 512], pp[:, :], AF.Copy, [pr], ["vg", "qmT", "PmT", "omT"]))
                for i in range(2):
                    wt, wr = wload(w_in, 0, 8, O_GG + i * 512, 512)
                    for t in range(4):
                        tm_proj(hT[:, :, t * 128:(t + 1) * 128], "hT%d" % t, wt, wr, 0, 512,
                                lambda pp, pr, t=t, i=i: act(sg[:, t, i * 512:(i + 1) * 512], pp[:, :], AF.Silu, [pr], ["sg"]))
                for i in range(4):
                    wt, wr = wload(w_in, 0, 8, O_GT + i * 512, 512)
                    for c in range(4):
                        cc = i * 4 + c
                        fm_proj(wt, wr, c * 128, 128, hT, HT_ALL, 512,
                                lambda pp, pr, cc=cc: act(gT[:, cc, :], pp[:, :], AF.Sigmoid, [pr, "bgate"], ["gT"], bias=bgate[:, cc:cc + 1]))

                chk("proj")
                for t in range(4):
                    first = (grp == 0 and t == 0)
                    pO = [(psb[4 + i_], "ps%d" % (4 + i_)) for i_ in range(4)]
                    for g in range(2):
                        for r in range(2):
                            s_ = 2 * g + r
                            PTs, PTr = PT[s_ % 2], "PT%d" % (s_ % 2)
                            pA, pAr = psb[2 * r], "ps%d" % (2 * r)
                            pB, pBr = psb[2 * r + 1], "ps%d" % (2 * r + 1)
                            for j in range(4):
                                i = 4 * g + j
                                q_ap = qT[64 * r:64 * r + 64, i, t * 128:(t + 1) * 128]
                                mm(pA[:, j * 128:(j + 1) * 128], kT[64 * r:64 * r + 64, g, t * 128:(t + 1) * 128], q_ap, True, True, ["kT", "qT"], [pAr])
                                mm(pB[:, j * 128:(j + 1) * 128], kT[64 * r:64 * r + 64, g, (t + 1) * 128:(t + 2) * 128], q_ap, True, True, ["kT", "qT"], [pBr])
                            pAv = pA[:, :].rearrange("p (j q) -> p j q", j=4)
                            pBv = pB[:, :].rearrange("p (j q) -> p j q", j=4)
                            kw = dict(scale=0.125)
                            kwp = dict(scale=0.125, bias=flg[0:64, 0:1]) if first else kw
                            kwp2 = dict(scale=0.125, bias=flg[64:128, 0:1]) if first else kw
                            act(PTs[0:64, 0, :, 0:64], pAv[0:64, :, 0:64], AF.Exp, [pAr, "flg"], [PTr], **kwp)
                            act(PTs[64:128, 0, :, :], pAv[64:128, :, :], AF.Exp, [pAr, "flg"], [PTr], **kwp2)
                            act(PTs[0:64, 1, :, :], pBv[0:64, :, :], AF.Exp, [pBr], [PTr], **kw)
                            act(PTs[64:128, 1, :, 64:128], pBv[64:128, :, 64:128], AF.Exp, [pBr], [PTr], **kw)
                            po, por = pO[s_]
                            for j in range(4):
                                col = j * 65
                                mm(po[:, col:col + 65], PTs[:, 0, j, :], vaug[:, t, g, :], True, False, [PTr, "vaug"], [por])
                                mm(po[:, col:col + 65], PTs[:, 1, j, :], vaug[:, t + 1, g, :], False, True, [PTr, "vaug"], [por])
                    for g in range(2):
                        for r in range(2):
                            s_ = 2 * g + r
                            po, por = pO[s_]
                            pov = po[:, 0:260].rearrange("p (j e) -> p j e", j=4)
                            dv = den[:, s_ * 4:(s_ + 1) * 4]
                            ev = exps[:, g * 8:(g + 1) * 8].rearrange("p (j r) -> p r j", r=2)[:, r, :]
                            ov = oa[:, g * 512:(g + 1) * 512].rearrange("p (j r d) -> p r j d", j=4, r=2)[:, r, :, :]
                            S.dve(lambda e, pov=pov, dv=dv, ev=ev: e.tensor_tensor(out=dv, in0=pov[:, :, 64], in1=ev, op=ALU.add),
                                  reads=[por, "exps"], writes=["den"])
                            S.dve(lambda e, dv=dv: e.reciprocal(out=dv, in_=dv), reads=["den"], writes=["den"])
                            S.dve(lambda e, pov=pov, dv=dv, ov=ov: e.tensor_tensor(
                                out=ov, in0=pov[:, :, 0:64], in1=dv.unsqueeze(2).to_broadcast([128, 4, 64]), op=ALU.mult),
                                reads=[por, "den"], writes=["oa"])
                    dump("oa_%d_%d" % (grp, t), oa[:], 1024, ["oa"])
                    for half in range(2):
                        pp, pr = newps()
                        for k in range(4):
                            kk = half * 4 + k
                            tr(pp[:, k * 128:(k + 1) * 128], oa[:, kk * 128:(kk + 1) * 128], ["oa"], [pr])
                        act(oaT[:, half * 4:(half + 1) * 4, t * 128:(t + 1) * 128], pp[:, :].rearrange("p (a b) -> p a b", a=4), AF.Copy, [pr], ["oaT"])
                S.dve(lambda e: e.tensor_copy(out=kT[:, :, 0:128], in_=kT[:, :, 512:640]), reads=["kT"], writes=["kT"])
                S.dve(lambda e: e.tensor_copy(out=vaug[:, 0, :, :], in_=vaug[:, 4, :, :]), reads=["vaug"], writes=["vaug"])

                chk("swa")
                for t in range(4):
                    gla_gate(t * 128, uc64, "uc64", ind2, "ind2")
                    S.dve(lambda e, t=t: e.tensor_tensor(out=kdec[:], in0=kg[:, t, :], in1=Dm[:], op=ALU.mult), reads=["kg", "Dm"], writes=["kdec"])
                    for j in range(2):
                        state_update((64 * j, 64 * j + 64), j, vg[:, t, :], "vg", "r0" if j == 0 else "r1")
                        act(Sbf[j][:], Sst[:], AF.Copy, ["Sst"], ["Sbf%d" % j])
                        pA, pAr = newps()
                        pB, pBr = newps()
                        for h in range(4):
                            pp, pr = (pA, pAr) if h < 2 else (pB, pBr)
                            mm(pp[:, (h % 2) * 256:(h % 2 + 1) * 256], qgT[:, h, t * 128:(t + 1) * 128],
                               Sbf[j][:, h * 256:(h + 1) * 256], True, True, ["qgT", "Sbf%d" % j], [pr])
                        act(og[64 * j:64 * j + 64, 0:512], pA[64 * j:64 * j + 64, :], AF.Copy, [pAr], ["og"])
                        act(og[64 * j:64 * j + 64, 512:1024], pB[64 * j:64 * j + 64, :], AF.Copy, [pBr], ["og"])
                    for h in range(4):
                        act(junk[:, 0:256], og[:, h * 256:(h + 1) * 256], AF.Square, ["og"], ["junk", "ss"], accum_out=ss[:, 4 + h:5 + h])
                    rstd_from_ss(4, 1.0 / 256, ncol=4)
                    for h in range(4):
                        S.dve(lambda e, h=h, t=t: e.scalar_tensor_tensor(
                            out=og[:, h * 256:(h + 1) * 256], in0=og[:, h * 256:(h + 1) * 256], scalar=rs[:, 4 + h:5 + h],
                            in1=sg[:, t, h * 256:(h + 1) * 256], op0=ALU.mult, op1=ALU.mult),
                            reads=["og", "rs", "sg"], writes=["og"])
                    dump("og_%d_%d" % (grp, t), og[:], 1024, ["og"])
                    for half in range(2):
                        pp, pr = newps()
                        for k in range(4):
                            kk = half * 4 + k
                            tr(pp[:, k * 128:(k + 1) * 128], og[:, kk * 128:(kk + 1) * 128], ["og"], [pr])
                        S.dve(lambda e, pp=pp, half=half, t=t: e.tensor_tensor(
                            out=ogT[:, half * 4:(half + 1) * 4, t * 128:(t + 1) * 128], in0=pp[:, :].rearrange("p (a b) -> p a b", a=4),
                            in1=gnw8[:, half * 4:(half + 1) * 4].unsqueeze(2).to_broadcast([128, 4, 128]), op=ALU.mult),
                            reads=[pr, "gnw8"], writes=["ogT"])

                chk("gla")
                for i in range(2):
                    wt, wr = wload(w_attn_o, 0, 8, i * 512, 512)
                    for c in range(4):
                        cc = i * 4 + c
                        fm_proj(wt, wr, c * 128, 128, oaT, "oaT", 512,
                                lambda pp, pr, cc=cc: S.dve(lambda e: e.tensor_tensor(out=m1[:, cc, :], in0=pp[:, :], in1=gT[:, cc, :], op=ALU.mult),
                                                            reads=[pr, "gT"], writes=["sg"]))
                for i in range(2):
                    wt, wr = wload(w_gla_o, 0, 8, i * 512, 512)
                    for c in range(4):
                        cc = i * 4 + c

                        def ev(pp, pr, cc=cc):
                            S.dve(lambda e: e.tensor_tensor(out=mtmp[:], in0=pp[:, :], in1=gT[:, 8 + cc, :], op=ALU.mult),
                                  reads=[pr, "gT"], writes=["Gf"])
                            S.dve(lambda e: e.tensor_tensor(out=mT[:, cc, :], in0=mtmp[:], in1=m1[:, cc, :], op=ALU.add),
                                  reads=["Gf", "sg"], writes=["qT"])
                        fm_proj(wt, wr, c * 128, 128, ogT, "ogT", 512, ev)
                for n in range(2):
                    wt, wr = wload(w_mix_o, 0, 8, n * 512, 512)
                    for t in range(4):
                        tm_proj(mT[:, :, t * 128:(t + 1) * 128], "qT", wt, wr, 0, 512,
                                lambda pp, pr, t=t, n=n: S.dve(lambda e: e.tensor_tensor(
                                    out=xg[:, t, n * 512:(n + 1) * 512], in0=pp[:, :], in1=xg[:, t, n * 512:(n + 1) * 512], op=ALU.add),
                                    reads=[pr, "xg%d" % t], writes=["xg%d" % t]))

                for t in range(4):
                    dump("x1_%d_%d" % (grp, t), xg[:, t, :], 1024, ["xg%d" % t])
                chk("merge")
                for t in range(4):
                    norm_T(xg[:, t, :], "xg%d" % t, wn2, "wn", hT[:, :, t * 128:(t + 1) * 128], "hT%d" % t)
                for c in range(2):
                    fm_proj(wmq, "wmq", c * 128, 128, hT, HT_ALL, 512,
                            lambda pp, pr, c=c: act(qmT[:, c, :], pp[:, :], AF.Copy, [pr], ["qmT", "vg"]))
                for t in range(4):
                    for r in range(2):
                        pp, pr = newps("r0" if r == 0 else "r1")
                        for mt in range(2):
                            for c in range(2):
                                col = (mt * 2 + c) * 128
                                mm(pp[:, col:col + 128], kmT[64 * r:64 * r + 64, c, mt * 128:(mt + 1) * 128],
                                   qmT[64 * r:64 * r + 64, c, t * 128:(t + 1) * 128], True, True, ["kmT", "qmT"], [pr])
                        act(PmT[:, r, :, :, :], pp[:, :].rearrange("p (m c q) -> p m c q", m=2, c=2), AF.Exp, [pr], ["PmT", "vg"], scale=0.125)
                    po, por = newps()
                    for c in range(2):
                        for r in range(2):
                            h = 2 * c + r
                            for mt in range(2):
                                mm(po[:, h * 65:(h + 1) * 65], PmT[:, r, mt, c, :], vmaug[:, mt, h, :], mt == 0, mt == 1, ["PmT", "vmaug"], [por])
                    pov = po[:, 0:260].rearrange("p (h e) -> p h e", h=4)
                    S.dve(lambda e, pov=pov: e.reciprocal(out=den[:, 0:4], in_=pov[:, :, 64]), reads=[por], writes=["den"])
                    S.dve(lambda e, pov=pov: e.tensor_tensor(out=om[:, :].rearrange("p (h d) -> p h d", h=4), in0=pov[:, :, 0:64],
                                                             in1=den[:, 0:4].unsqueeze(2).to_broadcast([128, 4, 64]), op=ALU.mult),
                          reads=[por, "den"], writes=["om"])
                    pp, pr = newps()
                    for k in range(2):
                        tr(pp[:, k * 128:(k + 1) * 128], om[:, k * 128:(k + 1) * 128], ["om"], [pr])
                    act(omT[:, :, t * 128:(t + 1) * 128], pp[:, 0:256].rearrange("p (a b) -> p a b", a=2), AF.Copy, [pr], ["omT", "vg"])
                for t in range(4):
                    for n in range(2):
                        pp, pr = newps()
                        for c in range(2):
                            mm(pp[:, :], omT[:, c, t * 128:(t + 1) * 128], wmo[:, c, n * 512:(n + 1) * 512], c == 0, c == 1, ["omT", "wmo"], [pr])
                        S.dve(lambda e, pp=pp, t=t, n=n: e.tensor_tensor(
                            out=xg[:, t, n * 512:(n + 1) * 512], in0=pp[:, :], in1=xg[:, t, n * 512:(n + 1) * 512], op=ALU.add),
                            reads=[pr, "xg%d" % t], writes=["xg%d" % t])

                for t in range(4):
                    dump("x2_%d_%d" % (grp, t), xg[:, t, :], 1024, ["xg%d" % t])
                chk("xattn")
                for t in range(4):
                    norm_T(xg[:, t, :], "xg%d" % t, wn3, "wn", hT[:, :, t * 128:(t + 1) * 128], "hT%d" % t)
                for i in range(6):
                    ncol = 512 if i < 5 else 256
                    wg_, wgr = wload(w_ffn_gate, 0, 8, i * 512, ncol)
                    wu_, wur = wload(w_ffn_up, 0, 8, i * 512, ncol)
                    for c in range(ncol // 128):
                        fc = i * 4 + c
                        slt, slr = sl[fc % 2], "sl%d" % (fc % 2)
                        fm_proj(wg_, wgr, c * 128, 128, hT, HT_ALL, 512,
                                lambda pp, pr, slt=slt, slr=slr: act(slt[:], pp[:, :], AF.Silu, [pr], [slr]))
                        fm_proj(wu_, wur, c * 128, 128, hT, HT_ALL, 512,
                                lambda pp, pr, slt=slt, slr=slr, fc=fc: S.dve(lambda e: e.tensor_tensor(
                                    out=actT[:, fc, :], in0=pp[:, :], in1=slt[:], op=ALU.mult), reads=[pr, slr], writes=["big"]))
                for n in range(2):
                    pst = [newps() for _ in range(4)]
                    for piece, (k0, nk) in enumerate(((0, 8), (8, 8), (16, 6))):
                        wt, wr = wload(w_ffn_down, k0, nk, n * 512, 512)
                        for t in range(4):
                            pp, pr = pst[t]
                            for k in range(nk):
                                kk = k0 + k
                                mm(pp[:, :], actT[:, kk, t * 128:(t + 1) * 128], wt[:, k, :], kk == 0, kk == 21, ["big", wr], [pr])
                    for t in range(4):
                        pp, pr = pst[t]
                        S.dve(lambda e, pp=pp, t=t, n=n: e.tensor_tensor(
                            out=xg[:, t, n * 512:(n + 1) * 512], in0=pp[:, :], in1=xg[:, t, n * 512:(n + 1) * 512], op=ALU.add),
                            reads=[pr, "xg%d" % t], writes=["xg%d" % t])

                chk("ffn")
                for t in range(4):
                    act(junk[:], xg[:, t, :], AF.Square, ["xg%d" % t], ["junk", "ss"], accum_out=ss[:, 0:1])
                    rstd_from_ss(0, 1.0 / D)
                    S.dve(lambda e, t=t: e.scalar_tensor_tensor(out=yst[:], in0=xg[:, t, :], scalar=rs[:, 0:1], in1=wfin[:],
                                                                op0=ALU.mult, op1=ALU.mult),
                          reads=["xg%d" % t, "rs", "wfin"], writes=["xs"])
                    row0 = (grp * 4 + t) * 128
                    ld("sp", yout[row0:row0 + 128, :], yst[:], "yout", [], reads=["xs"])

        except _Stop:
            pass
        S.emit(final_wait_keys=[k for k in ["yout"] + dbg_keys if k in S.dma_count])
    return nc


_CONST_CACHE = {}


def _consts():
    if not _CONST_CACHE:
        ident = np.eye(128, dtype=np.float32)
        a = np.arange(128)
        u64 = ((a[:, None] // 64 == a[None, :] // 64) & (a[:, None] > a[None, :])).astype(np.float32) * (-1.0 / 16)
        u128 = (a[:, None] > a[None, :]).astype(np.float32) * (-1.0 / 16)
        ind2 = np.zeros((128, 2), np.float32)
        ind2[:64, 0] = -1.0 / 16
        ind2[64:, 1] = -1.0 / 16
        ind1 = np.full((128, 2), -1.0 / 16, np.float32)
        _CONST_CACHE.update(ident=ident, ucum64=u64, ucum128=u128, ind2=ind2, ind1=ind1)
    return _CONST_CACHE


def make_in_maps(inputs, ncore=NCORE, tok_core=TOK_CORE, npre_tiles=48):
    x = np.asarray(inputs["x"], dtype=np.float32)
    memv = np.asarray(inputs["mem"], dtype=np.float32)
    B, SEQ, _ = x.shape
    per_b = SEQ // tok_core
    c = _consts()
    maps = []
    for core in range(ncore):
        b, j = core // per_b, core % per_b
        npre = npre_tiles * 128
        xa = np.zeros((npre + tok_core, D), np.float32)
        end = (j + 1) * tok_core
        start = max(0, j * tok_core - npre)
        seg = x[b, start:end]
        xa[npre + tok_core - seg.shape[0]:] = seg
        m = dict(xall=xa, mem=np.ascontiguousarray(memv[b]),
                 flagb=np.full((128, 1), 0.0 if j > 0 else -30000.0, np.float32))
        m.update(c)
        for k, v in inputs.items():
            if k in ("x", "mem"):
                continue
            v = np.asarray(v, dtype=np.float32)
            m[k] = np.ascontiguousarray(v[0]) if k != "norm_final_w" else np.ascontiguousarray(v)
        maps.append(m)
    return maps


_NC_CACHE = {}


def kernel(**inputs):
    if "nc" not in _NC_CACHE:
        _NC_CACHE["nc"] = build()
    nc = _NC_CACHE["nc"]
    maps = make_in_maps(inputs)
    res = run_bass_kernel_spmd(nc, maps, core_ids=list(range(NCORE)))
    x = inputs["x"]
    B, SEQ, _ = x.shape
    per_b = SEQ // TOK_CORE
    out = np.empty((B, SEQ, D), np.float32)
    for core in range(NCORE):
        b, j = core // per_b, core % per_b
        out[b, j * TOK_CORE:(j + 1) * TOK_CORE] = res.results[core]["y"]
    return out
```

```python
import contextlib
import numpy as np
import concourse.bass as bass
import concourse.mybir as mybir
from concourse.bass_utils import run_bass_kernel_spmd

F32 = mybir.dt.float32
BF16 = mybir.dt.bfloat16
AF = mybir.ActivationFunctionType
ALU = mybir.AluOpType

D = 1024
NCORE = 8
TOK_CORE = 2048
DFF = 2816
INC = 6416
O_QA, O_KA, O_VA, O_QG, O_KG, O_VG, O_GG, O_AL, O_GT = 0, 1024, 1152, 1280, 1792, 2304, 3328, 4352, 4368


class _Stop(Exception):
    pass


class Sched:
    ENGS = ("pe", "act", "dve", "pool", "sp")

    def __init__(self, nc, same_engine_sync=True):
        self.nc = nc
        self.ops = []
        self.last_write = {}
        self.reads_since = {}
        self.dma_count = {}
        self.same_engine_sync = same_engine_sync

    def _add(self, eng, fn, reads, writes, dma_key=None):
        idx = len(self.ops)
        deps = set()
        for r in reads:
            lw = self.last_write.get(r)
            if lw is not None:
                deps.add(lw)
        for w in writes:
            lw = self.last_write.get(w)
            if lw is not None:
                deps.add(lw)
            for rd in self.reads_since.get(w, ()):
                deps.add(rd)
        for r in reads:
            self.reads_since.setdefault(r, []).append(idx)
        for w in writes:
            self.last_write[w] = idx
            self.reads_since[w] = []
        deps.discard(idx)
        op = dict(eng=eng, fn=fn, deps=deps, dma_key=dma_key, signal=False, idx=idx)
        if dma_key is not None:
            c = self.dma_count.get(dma_key, 0) + 16
            self.dma_count[dma_key] = c
            op["dma_val"] = c
        self.ops.append(op)
        return idx

    def pe(self, fn, reads=(), writes=()):
        return self._add("pe", fn, reads, writes)

    def act(self, fn, reads=(), writes=()):
        return self._add("act", fn, reads, writes)

    def dve(self, fn, reads=(), writes=()):
        return self._add("dve", fn, reads, writes)

    def pool(self, fn, reads=(), writes=()):
        return self._add("pool", fn, reads, writes)

    def dma(self, eng, fn, key, reads=(), writes=()):
        return self._add(eng, fn, reads, writes, dma_key=key)

    def emit(self, final_wait_keys=()):
        nc = self.nc
        ops = self.ops
        need = []
        for op in ops:
            nd = []
            for d in sorted(op["deps"]):
                y = ops[d]
                if y["dma_key"] is None and y["eng"] == op["eng"] and op["dma_key"] is None:
                    if op["eng"] == "pe" or not self.same_engine_sync:
                        continue
                nd.append(d)
                if y["dma_key"] is None:
                    y["signal"] = True
            need.append(nd)
        cnt = {e: 0 for e in self.ENGS}
        for op in ops:
            if op["dma_key"] is None and op["signal"]:
                cnt[op["eng"]] += 1
                op["sig_val"] = cnt[op["eng"]]
        with contextlib.ExitStack() as st:
            esem = {e: st.enter_context(nc.semaphore("s_" + e)) for e in self.ENGS}
            dsem = {k: st.enter_context(nc.semaphore("d_%d" % i)) for i, k in enumerate(self.dma_count)}
            block = st.enter_context(nc.Block())
            per = {e: [] for e in self.ENGS}
            for op, nd in zip(ops, need):
                per[op["eng"]].append((op, nd))

            def body(ename, handle):
                waited = {}
                for op, nd in per[ename]:
                    for d in nd:
                        y = ops[d]
                        if y["dma_key"] is not None:
                            sem, val, k = dsem[y["dma_key"]], y["dma_val"], ("d", y["dma_key"])
                        else:
                            sem, val, k = esem[y["eng"]], y["sig_val"], ("e", y["eng"])
                        if waited.get(k, 0) >= val:
                            continue
                        waited[k] = val
                        handle.wait_ge(sem, val)
                    ins = op["fn"](handle)
                    if op["dma_key"] is not None:
                        ins.then_inc(dsem[op["dma_key"]], 16)
                    elif op["signal"]:
                        ins.then_inc(esem[ename], 1)
                if ename == "sp":
                    for k in final_wait_keys:
                        handle.wait_ge(dsem[k], self.dma_count[k])

            @block.tensor
            def _(e):
                body("pe", e)

            @block.scalar
            def _(e):
                body("act", e)

            @block.vector
            def _(e):
                body("dve", e)

            @block.gpsimd
            def _(e):
                body("pool", e)

            @block.sync
            def _(e):
                body("sp", e)


def build(NPRE=48, NGRP=4, NSLOT=3, same_engine_sync=True, dbg=False, stop_after=None):
    nc = bass.Bass("TRN2", target_bir_lowering=False)
    NT_ALL = NPRE + NGRP * 4

    def din(name, shape):
        return nc.dram_tensor(name, list(shape), F32, kind="ExternalInput").ap()

    xall = din("xall", [NT_ALL * 128, D])
    mem = din("mem", [256, D])
    flagb = din("flagb", [128, 1])
    ident_d = din("ident", [128, 128])
    ucum64_d = din("ucum64", [128, 128])
    ucum128_d = din("ucum128", [128, 128])
    ind2_d = din("ind2", [128, 2])
    ind1_d = din("ind1", [128, 2])
    norm_mix_w = din("norm_mix_w", [D])
    w_in = din("w_in", [D, INC])
    b_gate = din("b_gate", [2 * D])
    attn_sinks = din("attn_sinks", [16])
    gla_gate_w2 = din("gla_gate_w2", [16, 512])
    gla_gate_b = din("gla_gate_b", [512])
    gla_norm_w = din("gla_norm_w", [256])
    w_attn_o = din("w_attn_o", [D, D])
    w_gla_o = din("w_gla_o", [D, D])
    w_mix_o = din("w_mix_o", [D, D])
    norm_mem_q_w = din("norm_mem_q_w", [D])
    norm_mem_kv_w = din("norm_mem_kv_w", [D])
    w_mem_q = din("w_mem_q", [D, 256])
    w_mem_kv = din("w_mem_kv", [D, 512])
    w_mem_o = din("w_mem_o", [256, D])
    norm_ffn_w = din("norm_ffn_w", [D])
    w_ffn_gate = din("w_ffn_gate", [D, DFF])
    w_ffn_up = din("w_ffn_up", [D, DFF])
    w_ffn_down = din("w_ffn_down", [DFF, D])
    norm_final_w = din("norm_final_w", [D])
    yout = nc.dram_tensor("y", [NGRP * 512, D], F32, kind="ExternalOutput").ap()

    S = Sched(nc, same_engine_sync=same_engine_sync)
    st = contextlib.ExitStack()
    with st:
        def sb(name, shape, dt=F32):
            return st.enter_context(nc.sbuf_tensor("sb_" + name, list(shape), dt))

        psb = [st.enter_context(nc.psum_tensor("ps%d" % i, [128, 512], F32)) for i in range(8)]
        pcnt = {"any": 0, "r0": 0, "r1": 0}
        PSETS = {"any": [0, 1, 2, 3, 4, 5, 6, 7], "r0": [0, 1, 4, 5], "r1": [2, 3, 6, 7]}

        def newps(kind="any"):
            lst = PSETS[kind]
            i = lst[pcnt[kind] % len(lst)]
            pcnt[kind] += 1
            return psb[i], "ps%d" % i

        idt = sb("idt", [128, 128])
        uc64 = sb("uc64", [128, 128])
        uc128 = sb("uc128", [128, 128])
        ind2 = sb("ind2", [128, 2])
        ind1 = sb("ind1", [128, 2])
        flg = sb("flg", [128, 1])
        vst = [sb("vst%d" % i, [16, 128]) for i in range(2)]
        wn1 = sb("wn1", [128, 8])
        wn2 = sb("wn2", [128, 8])
        wn3 = sb("wn3", [128, 8])
        wnkv = sb("wnkv", [128, 8])
        gnw8 = sb("gnw8", [128, 8])
        bgate = sb("bgate", [128, 16])
        exps = sb("exps", [128, 16])
        wfin = sb("wfin", [128, D])
        w2aug = sb("w2aug", [32, 512], BF16)
        wkdup = sb("wkdup", [128, 8, 2, 128], BF16)
        wva = sb("wva", [128, 8, 128], BF16)
        walr = sb("walr", [128, 8, 16], BF16)
        wmq = sb("wmq", [128, 8, 256], BF16)
        wmo = sb("wmo", [128, 2, D], BF16)
        big = sb("big", [128, 8 * 1536], BF16)
        wpre = big[:, :].rearrange("p (k c) -> p k c", k=8)
        actT = big[:, 0:22 * 512].rearrange("p (f t) -> p f t", f=22)
        kmT = sb("kmT", [128, 2, 256], BF16)
        vmaug = sb("vmaug", [128, 2, 4, 65], BF16)
        wsl = [sb("wsl%d" % i, [128, 8, 512], BF16) for i in range(NSLOT)]
        xg = sb("xg", [128, 4, D])
        xs = sb("xs", [128, D])
        junk = sb("junk", [128, D], BF16)
        ss = sb("ss", [128, 8])
        rs = sb("rs", [128, 8])
        hT = sb("hT", [128, 8, 512], BF16)
        qT = sb("qT", [128, 8, 512], BF16)
        kT = sb("kT", [128, 2, 640], BF16)
        vaug = sb("vaug", [128, 5, 2, 65], BF16)
        PT = [sb("PT%d" % i, [128, 2, 4, 128], BF16) for i in range(2)]
        oa = sb("oa", [128, D])
        oaT = sb("oaT", [128, 8, 512], BF16)
        den = sb("den", [128, 16])
        qgT = sb("qgT", [128, 4, 512], BF16)
        alrT = sb("alrT", [32, 512], BF16)
        kg = sb("kg", [128, 4, 512])
        vg = sb("vg", [128, 4, D], BF16)
        sg = sb("sg", [128, 4, D], BF16)
        Gf = sb("Gf", [128, 512])
        Dm = sb("Dm", [128, 512])
        av = sb("av", [128, 8])
        kdec = sb("kdec", [128, 512], BF16)
        Sst = sb("Sst", [128, D])
        Sbf = [sb("Sbf%d" % i, [128, D], BF16) for i in range(2)]
        og = sb("og", [128, D])
        ogT = sb("ogT", [128, 8, 512], BF16)
        gT = sb("gT", [128, 16, 512], BF16)
        m1 = sg[:, :, :].rearrange("p a (b c) -> p (a b) c", b=2)
        mtmp = Gf
        mT = qT
        qmT = vg[:, 0, :].rearrange("p (a b) -> p a b", a=2)
        PmT = vg[:, 1, :].rearrange("p (r m c q) -> p r m c q", r=2, m=2, c=2)
        omT = vg[:, 2, :].rearrange("p (a b) -> p a b", a=2)
        om = sb("om", [128, 256])
        sl = [sb("sl%d" % i, [128, 512], BF16) for i in range(2)]
        yst = xs
        xm = oa

        HT_ALL = ["hT0", "hT1", "hT2", "hT3"]

        def chk(name):
            if stop_after == name:
                raise _Stop()

        dbg_keys = []

        def dump(name, ap, ncols, reads):
            if not dbg:
                return
            d_ = nc.dram_tensor("dbg_" + name, [128, ncols], F32, kind="ExternalOutput").ap()
            k_ = "dbg_" + name
            dbg_keys.append(k_)
            S.dma("sp", lambda e: e.dma_start(out=d_, in_=ap), k_, reads=reads)

        ukey = [0]

        def ld(eng, out_ap, in_ap, key, writes, reads=(), slow=False):
            if key in ("c", "cw"):
                ukey[0] += 1
                if key == "cw":
                    reads = list(reads) + ["cwtok%d" % (ukey[0] % 2)]
                    writes = list(writes) + ["cwtok%d" % (ukey[0] % 2)]
                key = "%s%d" % (key, ukey[0])
            if slow:
                S.dma(eng, lambda e: e.dma_start(out=out_ap, in_=in_ap, allow_slow_non_contiguous=True), key, reads=reads, writes=writes)
            else:
                S.dma(eng, lambda e: e.dma_start(out=out_ap, in_=in_ap), key, reads=reads, writes=writes)

        slot_i = [0]

        def wload(W, k0, nk, c0, ncols):
            i = slot_i[0] % NSLOT
            slot_i[0] += 1
            src = W[k0 * 128:(k0 + nk) * 128, c0:c0 + ncols].rearrange("(k p) c -> p k c", p=128)
            ld("pool", wsl[i][:, 0:nk, 0:ncols], src, "wsl%d" % i, writes=["wsl%d" % i])
            return wsl[i], "wsl%d" % i

        def mm(out_ap, lhsT, rhs, start, stop, reads, writes):
            S.pe(lambda e: e.matmul(out=out_ap, lhsT=lhsT, rhs=rhs, start=start, stop=stop), reads=reads, writes=writes)

        def tr(out_ap, in_ap, reads, writes):
            S.pe(lambda e: e.transpose(out=out_ap, in_=in_ap, identity=idt[:]), reads=list(reads) + ["idt"], writes=writes)

        def act(out_ap, in_ap, func, reads, writes, **kw):
            S.act(lambda e: e.activation(out=out_ap, in_=in_ap, func=func, **kw), reads=reads, writes=writes)

        def rstd_from_ss(col, n_inv, ncol=1):
            act(rs[:, col:col + ncol], ss[:, col:col + ncol], AF.Sqrt, ["ss"], ["rs"], scale=n_inv, bias=1e-6)
            S.dve(lambda e: e.reciprocal(out=rs[:, col:col + ncol], in_=rs[:, col:col + ncol]), reads=["rs"], writes=["rs"])

        def norm_T(x_ap, xres, wn, wnres, dst, dstres):
            act(junk[:], x_ap, AF.Square, [xres], ["junk", "ss"], accum_out=ss[:, 0:1])
            rstd_from_ss(0, 1.0 / D)
            S.dve(lambda e: e.tensor_scalar(out=xs[:], in0=x_ap, scalar1=rs[:, 0:1], scalar2=None, op0=ALU.mult),
                  reads=[xres, "rs"], writes=["xs"])
            for half in range(2):
                pp, pr = newps()
                for k in range(4):
                    kk = half * 4 + k
                    tr(pp[:, k * 128:(k + 1) * 128], xs[:, kk * 128:(kk + 1) * 128], ["xs"], [pr])
                S.dve(lambda e, pp=pp, half=half: e.tensor_tensor(
                    out=dst[:, half * 4:(half + 1) * 4, :], in0=pp[:, :].rearrange("p (a b) -> p a b", a=4),
                    in1=wn[:, half * 4:(half + 1) * 4].unsqueeze(2).to_broadcast([128, 4, 128]), op=ALU.mult),
                    reads=[pr, wnres], writes=[dstres])

        def tm_proj(lhs_tile, lhs_res, wt, wres, c0, ncols, evac):
            pp, pr = newps()
            for k in range(8):
                mm(pp[:, 0:ncols], lhs_tile[:, k, :], wt[:, k, c0:c0 + ncols], k == 0, k == 7, [lhs_res, wres], [pr])
            evac(pp, pr)

        def fm_proj(wt, wres, c0, m, rhs, rhs_res, ntok, evac, nk=8):
            pp, pr = newps()
            rr = list(rhs_res) if isinstance(rhs_res, (list, tuple)) else [rhs_res]
            for k in range(nk):
                mm(pp[0:m, 0:ntok], wt[:, k, c0:c0 + m], rhs[:, k, 0:ntok], k == 0, k == nk - 1, rr + [wres], [pr])
            evac(pp, pr)

        def gla_gate(tokc0, umat, ures, indt, indres):
            pz, pzr = newps()
            mm(pz[:, :], alrT[0:17, tokc0:tokc0 + 128], w2aug[0:17, :], True, True, ["alrT", "w2aug"], [pzr])
            act(Gf[:], pz[:, :], AF.Exp, [pzr], ["Gf"], scale=-1.0)
            act(Gf[:], Gf[:], AF.Ln, ["Gf"], ["Gf"], bias=1.0)
            pR, pRr = newps()
            mm(pR[:, :], umat[:], Gf[:], True, True, [ures, "Gf"], [pRr])
            pa, par = newps()
            for h in range(4):
                mm(pa[:, 2 * h:2 * h + 2], Gf[:, h * 128:(h + 1) * 128], indt[:], True, True, ["Gf", indres], [par])
            act(Dm[:], pR[:, :], AF.Exp, [pRr], ["Dm"])
            act(av[:], pa[:, 0:8], AF.Exp, [par], ["av"])

        def state_update(krows, j, vsrc, vres, kind):
            r0, r1 = krows
            pA, pAr = newps(kind)
            pB, pBr = newps(kind)
            for h in range(4):
                pp, pr = (pA, pAr) if h < 2 else (pB, pBr)
                mm(pp[:, (h % 2) * 256:(h % 2 + 1) * 256], kdec[r0:r1, h * 128:(h + 1) * 128],
                   vsrc[r0:r1, h * 256:(h + 1) * 256], True, True, ["kdec", vres], [pr])
            for h in range(4):
                pp, pr = (pA, pAr) if h < 2 else (pB, pBr)
                S.dve(lambda e, h=h, pp=pp: e.scalar_tensor_tensor(
                    out=Sst[:, h * 256:(h + 1) * 256], in0=Sst[:, h * 256:(h + 1) * 256], scalar=av[:, 2 * h + j:2 * h + j + 1],
                    in1=pp[:, (h % 2) * 256:(h % 2 + 1) * 256], op0=ALU.mult, op1=ALU.add),
                    reads=["Sst", "av", pr], writes=["Sst"])

        try:
            ld("sp", idt[:], ident_d, "c", ["idt"])
            ld("sp", uc64[:], ucum64_d, "c", ["uc64"])
            ld("sp", uc128[:], ucum128_d, "c", ["uc128"])
            ld("sp", ind2[:], ind2_d, "c", ["ind2"])
            ld("sp", ind1[:], ind1_d, "c", ["ind1"])
            ld("sp", flg[:], flagb, "c", ["flg"])
            for vi, (t_, src_, kk_, res_) in enumerate(((wn1, norm_mix_w, 8, "wn"), (wn2, norm_mem_q_w, 8, "wn"), (wn3, norm_ffn_w, 8, "wn"),
                                                      (wnkv, norm_mem_kv_w, 8, "wn"), (bgate, b_gate, 16, "bgate"), (gnw8, gla_norm_w, 2, "gnw8"))):
                stg = vst[vi % 2]
                sres = "vst%d" % (vi % 2)
                ld("sp", stg[0:kk_, :], src_.rearrange("(k p) -> k p", p=128), "c", [sres])
                pp, pr = newps()
                S.pe(lambda e, pp=pp, stg=stg, kk_=kk_: e.transpose(out=pp[:, 0:kk_], in_=stg[0:kk_, :], identity=idt[0:kk_, 0:kk_]),
                     reads=[sres, "idt"], writes=[pr])
                if kk_ == 2:
                    for i in range(4):
                        act(t_[:, 2 * i:2 * i + 2], pp[:, 0:2], AF.Copy, [pr], [res_])
                else:
                    act(t_[:, 0:kk_], pp[:, 0:kk_], AF.Copy, [pr], [res_])
            ld("sp", exps[:], attn_sinks.partition_broadcast(128), "c", ["exps"])
            ld("sp", wfin[:], norm_final_w.partition_broadcast(128), "c", ["wfin"])
            CONSTS = ["idt", "uc64", "uc128", "ind2", "ind1", "flg", "wn", "gnw8", "bgate", "exps", "wfin"]
            act(exps[:], exps[:], AF.Exp, ["exps"], ["exps"])
            ld("pool", w2aug[0:16, :], gla_gate_w2, "cw", ["w2aug"])
            ld("pool", w2aug[16:17, :], gla_gate_b.rearrange("(o c) -> o c", o=1), "cw", ["w2aug"])
            for g in range(2):
                for r in range(2):
                    ld("pool", wkdup[:, :, g, 64 * r:64 * r + 64],
                       w_in[:, O_KA + 64 * g:O_KA + 64 * g + 64].rearrange("(k p) c -> p k c", p=128), "cw", ["wkdup"])
            ld("pool", wva[:], w_in[:, O_VA:O_VA + 128].rearrange("(k p) c -> p k c", p=128), "cw", ["wva"])
            ld("pool", walr[:], w_in[:, O_AL:O_AL + 16].rearrange("(k p) c -> p k c", p=128), "cw", ["walr"])
            ld("pool", wmq[:], w_mem_q.rearrange("(k p) c -> p k c", p=128), "cw", ["wmq"])
            ld("pool", wmo[:], w_mem_o.rearrange("(k p) c -> p k c", p=128), "cw", ["wmo"])
            for i in range(3):
                ld("pool", wpre[:, :, i * 512:(i + 1) * 512],
                   w_in[:, O_KG + i * 512:O_KG + (i + 1) * 512].rearrange("(k p) c -> p k c", p=128), "cw", ["big"])
            S.dve(lambda e: e.memset(Sst[:], 0.0), writes=["Sst"])
            S.dve(lambda e: e.memset(alrT[:], 1.0), writes=["alrT"])
            S.dve(lambda e: e.memset(vaug[:], 1.0), writes=["vaug"])
            S.dve(lambda e: e.memset(vmaug[:], 1.0), writes=["vmaug"])
            for i in range(2):
                S.dve(lambda e, i=i: e.memset(PT[i][:], 0.0), writes=["PT%d" % i])

            for mt in range(2):
                ld("sp", xm[:], mem[mt * 128:(mt + 1) * 128, :], "xm", ["oa"])
                norm_T(xm[:], "oa", wnkv, "wn", hT[:, :, mt * 128:(mt + 1) * 128], "hT%d" % mt)
            wt, wr = wload(w_mem_kv, 0, 8, 0, 512)
            for c in range(2):
                fm_proj(wt, wr, c * 128, 128, hT, ["hT0", "hT1"], 256,
                        lambda pp, pr, c=c: act(kmT[:, c, :], pp[:, 0:256], AF.Copy, [pr], ["kmT"]))
            for mt in range(2):
                tm_proj(hT[:, :, mt * 128:(mt + 1) * 128], "hT%d" % mt, wt, wr, 256, 256,
                        lambda pp, pr, mt=mt: act(vmaug[:, mt, :, 0:64], pp[:, 0:256].rearrange("p (h d) -> p h d", h=4), AF.Copy, [pr], ["vmaug"]))

            chk("setup")
            for t in range(NPRE):
                ld("sp", xm[:], xall[t * 128:(t + 1) * 128, :], "xm", ["oa"])
                hTt = hT[:, :, (t % 2) * 128:(t % 2 + 1) * 128]
                hres = "hT%d" % (t % 2)
                norm_T(xm[:], "oa", wn1, "wn", hTt, hres)
                tm_proj(hTt, hres, wpre, "big", 0, 512,
                        lambda pp, pr: act(kg[:, 0, :], pp[:, :], AF.Copy, [pr], ["kg"]))
                for i in range(2):
                    tm_proj(hTt, hres, wpre, "big", 512 + i * 512, 512,
                            lambda pp, pr, i=i: act(vg[:, 0, i * 512:(i + 1) * 512], pp[:, :], AF.Copy, [pr], ["vg", "qmT", "PmT", "omT"]))
                fm_proj(walr, "walr", 0, 16, hTt, hres, 128,
                        lambda pp, pr: act(alrT[0:16, 0:128], pp[0:16, 0:128], AF.Copy, [pr], ["alrT"]))
                gla_gate(0, uc128, "uc128", ind1, "ind1")
                S.dve(lambda e: e.tensor_tensor(out=kdec[:], in0=kg[:, 0, :], in1=Dm[:], op=ALU.mult), reads=["kg", "Dm"], writes=["kdec"])
                state_update((0, 128), 0, vg[:, 0, :], "vg", "any")
                if t == NPRE - 1:
                    for g in range(2):
                        fm_proj(wkdup[:, :, g, :], "wkdup", 0, 128, hTt, hres, 128,
                                lambda pp, pr, g=g: act(kT[:, g, 0:128], pp[:, 0:128], AF.Copy, [pr], ["kT"]))
                    tm_proj(hTt, hres, wva, "wva", 0, 128,
                            lambda pp, pr: act(vaug[:, 0, :, 0:64], pp[:, 0:128].rearrange("p (g d) -> p g d", g=2), AF.Copy, [pr], ["vaug"]))

            chk("prefix")
            for grp in range(NGRP):
                for t in range(4):
                    row0 = (NPRE + grp * 4 + t) * 128
                    ld("sp", xg[:, t, :], xall[row0:row0 + 128, :], "xg%d" % t, ["xg%d" % t])
                    norm_T(xg[:, t, :], "xg%d" % t, wn1, "wn", hT[:, :, t * 128:(t + 1) * 128], "hT%d" % t)
                chk("norm1")
                for i in range(2):
                    wt, wr = wload(w_in, 0, 8, O_QA + i * 512, 512)
                    for c in range(4):
                        fm_proj(wt, wr, c * 128, 128, hT, HT_ALL, 512,
                                lambda pp, pr, cc=i * 4 + c: act(qT[:, cc, :], pp[:, :], AF.Copy, [pr], ["qT"]))
                for g in range(2):
                    fm_proj(wkdup[:, :, g, :], "wkdup", 0, 128, hT, HT_ALL, 512,
                            lambda pp, pr, g=g: act(kT[:, g, 128:640], pp[:, :], AF.Copy, [pr], ["kT"]))
                for t in range(4):
                    tm_proj(hT[:, :, t * 128:(t + 1) * 128], "hT%d" % t, wva, "wva", 0, 128,
                            lambda pp, pr, t=t: act(vaug[:, 1 + t, :, 0:64], pp[:, 0:128].rearrange("p (g d) -> p g d", g=2), AF.Copy, [pr], ["vaug"]))
                wt, wr = wload(w_in, 0, 8, O_QG, 512)
                for c in range(4):
                    fm_proj(wt, wr, c * 128, 128, hT, HT_ALL, 512,
                            lambda pp, pr, c=c: act(qgT[:, c, :], pp[:, :], AF.Copy, [pr], ["qgT"], scale=float(128 ** -0.5)))
                fm_proj(walr, "walr", 0, 16, hT, HT_ALL, 512,
                        lambda pp, pr: act(alrT[0:16, :], pp[0:16, :], AF.Copy, [pr], ["alrT"]))
                wt, wr = wload(w_in, 0, 8, O_KG, 512)
                for t in range(4):
                    tm_proj(hT[:, :, t * 128:(t + 1) * 128], "hT%d" % t, wt, wr, 0, 512,
                            lambda pp, pr, t=t: act(kg[:, t, :], pp[:, :], AF.Copy, [pr], ["kg"]))
                for i in range(2):
                    wt, wr = wload(w_in, 0, 8, O_VG + i * 512, 512)
                    for t in range(4):
                        tm_proj(hT[:, :, t * 128:(t + 1) * 128], "hT%d" % t, wt, wr, 0, 512,
                                lambda pp, pr, t=t, i=i: act(vg[:, t, i * 512:(i + 1) * 512], pp[:, :], AF.Copy, [pr], ["vg", "qmT", "PmT", "omT"]))
                for i in range(2):
                    wt, wr = wload(w_in, 0, 8, O_GG + i * 512, 512)
                    for t in range(4):
                        tm_proj(hT[:, :, t * 128:(t + 1) * 128], "hT%d" % t, wt, wr, 0, 512,
                                lambda pp, pr, t=t, i=i: act(sg[:, t, i * 512:(i + 1) * 512], pp[:, :], AF.Silu, [pr], ["sg"]))
                for i in range(4):
                    wt, wr = wload(w_in, 0, 8, O_GT + i * 512, 512)
                    for c in range(4):
                        cc = i * 4 + c
                        fm_proj(wt, wr, c * 128, 128, hT, HT_ALL, 512,
                                lambda pp, pr, cc=cc: act(gT[:, cc, :], pp[:, :], AF.Sigmoid, [pr, "bgate"], ["gT"], bias=bgate[:, cc:cc + 1]))

                chk("proj")
                for t in range(4):
                    first = (grp == 0 and t == 0)
                    pO = [(psb[4 + i_], "ps%d" % (4 + i_)) for i_ in range(4)]
                    for g in range(2):
                        for r in range(2):
                            s_ = 2 * g + r
                            PTs, PTr = PT[s_ % 2], "PT%d" % (s_ % 2)
                            pA, pAr = psb[2 * r], "ps%d" % (2 * r)
                            pB, pBr = psb[2 * r + 1], "ps%d" % (2 * r + 1)
                            for j in range(4):
                                i = 4 * g + j
                                q_ap = qT[64 * r:64 * r + 64, i, t * 128:(t + 1) * 128]
                                mm(pA[:, j * 128:(j + 1) * 128], kT[64 * r:64 * r + 64, g, t * 128:(t + 1) * 128], q_ap, True, True, ["kT", "qT"], [pAr])
                                mm(pB[:, j * 128:(j + 1) * 128], kT[64 * r:64 * r + 64, g, (t + 1) * 128:(t + 2) * 128], q_ap, True, True, ["kT", "qT"], [pBr])
                            pAv = pA[:, :].rearrange("p (j q) -> p j q", j=4)
                            pBv = pB[:, :].rearrange("p (j q) -> p j q", j=4)
                            kw = dict(scale=0.125)
                            kwp = dict(scale=0.125, bias=flg[0:64, 0:1]) if first else kw
                            kwp2 = dict(scale=0.125, bias=flg[64:128, 0:1]) if first else kw
                            act(PTs[0:64, 0, :, 0:64], pAv[0:64, :, 0:64], AF.Exp, [pAr, "flg"], [PTr], **kwp)
                            act(PTs[64:128, 0, :, :], pAv[64:128, :, :], AF.Exp, [pAr, "flg"], [PTr], **kwp2)
                            act(PTs[0:64, 1, :, :], pBv[0:64, :, :], AF.Exp, [pBr], [PTr], **kw)
                            act(PTs[64:128, 1, :, 64:128], pBv[64:128, :, 64:128], AF.Exp, [pBr], [PTr], **kw)
                            po, por = pO[s_]
                            for j in range(4):
                                col = j * 65
                                mm(po[:, col:col + 65], PTs[:, 0, j, :], vaug[:, t, g, :], True, False, [PTr, "vaug"], [por])
                                mm(po[:, col:col + 65], PTs[:, 1, j, :], vaug[:, t + 1, g, :], False, True, [PTr, "vaug"], [por])
                    for g in range(2):
                        for r in range(2):
                            s_ = 2 * g + r
                            po, por = pO[s_]
                            pov = po[:, 0:260].rearrange("p (j e) -> p j e", j=4)
                            dv = den[:, s_ * 4:(s_ + 1) * 4]
                            ev = exps[:, g * 8:(g + 1) * 8].rearrange("p (j r) -> p r j", r=2)[:, r, :]
                            ov = oa[:, g * 512:(g + 1) * 512].rearrange("p (j r d) -> p r j d", j=4, r=2)[:, r, :, :]
                            S.dve(lambda e, pov=pov, dv=dv, ev=ev: e.tensor_tensor(out=dv, in0=pov[:, :, 64], in1=ev, op=ALU.add),
                                  reads=[por, "exps"], writes=["den"])
                            S.dve(lambda e, dv=dv: e.reciprocal(out=dv, in_=dv), reads=["den"], writes=["den"])
                            S.dve(lambda e, pov=pov, dv=dv, ov=ov: e.tensor_tensor(
                                out=ov, in0=pov[:, :, 0:64], in1=dv.unsqueeze(2).to_broadcast([128, 4, 64]), op=ALU.mult),
                                reads=[por, "den"], writes=["oa"])
                    dump("oa_%d_%d" % (grp, t), oa[:], 1024, ["oa"])
                    for half in range(2):
                        pp, pr = newps()
                        for k in range(4):
                            kk = half * 4 + k
                            tr(pp[:, k * 128:(k + 1) * 128], oa[:, kk * 128:(kk + 1) * 128], ["oa"], [pr])
                        act(oaT[:, half * 4:(half + 1) * 4, t * 128:(t + 1) * 128], pp[:, :].rearrange("p (a b) -> p a b", a=4), AF.Copy, [pr], ["oaT"])
                S.dve(lambda e: e.tensor_copy(out=kT[:, :, 0:128], in_=kT[:, :, 512:640]), reads=["kT"], writes=["kT"])
                S.dve(lambda e: e.tensor_copy(out=vaug[:, 0, :, :], in_=vaug[:, 4, :, :]), reads=["vaug"], writes=["vaug"])

                chk("swa")
                for t in range(4):
                    gla_gate(t * 128, uc64, "uc64", ind2, "ind2")
                    S.dve(lambda e, t=t: e.tensor_tensor(out=kdec[:], in0=kg[:, t, :], in1=Dm[:], op=ALU.mult), reads=["kg", "Dm"], writes=["kdec"])
                    for j in range(2):
                        state_update((64 * j, 64 * j + 64), j, vg[:, t, :], "vg", "r0" if j == 0 else "r1")
                        act(Sbf[j][:], Sst[:], AF.Copy, ["Sst"], ["Sbf%d" % j])
                        pA, pAr = newps()
                        pB, pBr = newps()
                        for h in range(4):
                            pp, pr = (pA, pAr) if h < 2 else (pB, pBr)
                            mm(pp[:, (h % 2) * 256:(h % 2 + 1) * 256], qgT[:, h, t * 128:(t + 1) * 128],
                               Sbf[j][:, h * 256:(h + 1) * 256], True, True, ["qgT", "Sbf%d" % j], [pr])
                        act(og[64 * j:64 * j + 64, 0:512], pA[64 * j:64 * j + 64, :], AF.Copy, [pAr], ["og"])
                        act(og[64 * j:64 * j + 64, 512:1024], pB[64 * j:64 * j + 64, :], AF.Copy, [pBr], ["og"])
                    for h in range(4):
                        act(junk[:, 0:256], og[:, h * 256:(h + 1) * 256], AF.Square, ["og"], ["junk", "ss"], accum_out=ss[:, 4 + h:5 + h])
                    rstd_from_ss(4, 1.0 / 256, ncol=4)
                    for h in range(4):
                        S.dve(lambda e, h=h, t=t: e.scalar_tensor_tensor(
                            out=og[:, h * 256:(h + 1) * 256], in0=og[:, h * 256:(h + 1) * 256], scalar=rs[:, 4 + h:5 + h],
                            in1=sg[:, t, h * 256:(h + 1) * 256], op0=ALU.mult, op1=ALU.mult),
                            reads=["og", "rs", "sg"], writes=["og"])
                    dump("og_%d_%d" % (grp, t), og[:], 1024, ["og"])
                    for half in range(2):
                        pp, pr = newps()
                        for k in range(4):
                            kk = half * 4 + k
                            tr(pp[:, k * 128:(k + 1) * 128], og[:, kk * 128:(kk + 1) * 128], ["og"], [pr])
                        S.dve(lambda e, pp=pp, half=half, t=t: e.tensor_tensor(
                            out=ogT[:, half * 4:(half + 1) * 4, t * 128:(t + 1) * 128], in0=pp[:, :].rearrange("p (a b) -> p a b", a=4),
                            in1=gnw8[:, half * 4:(half + 1) * 4].unsqueeze(2).to_broadcast([128, 4, 128]), op=ALU.mult),
                            reads=[pr, "gnw8"], writes=["ogT"])

                chk("gla")
                for i in range(2):
                    wt, wr = wload(w_attn_o, 0, 8, i * 512, 512)
                    for c in range(4):
                        cc = i * 4 + c
                        fm_proj(wt, wr, c * 128, 128, oaT, "oaT", 512,
                                lambda pp, pr, cc=cc: S.dve(lambda e: e.tensor_tensor(out=m1[:, cc, :], in0=pp[:, :], in1=gT[:, cc, :], op=ALU.mult),
                                                            reads=[pr, "gT"], writes=["sg"]))
                for i in range(2):
                    wt, wr = wload(w_gla_o, 0, 8, i * 512, 512)
                    for c in range(4):
                        cc = i * 4 + c

                        def ev(pp, pr, cc=cc):
                            S.dve(lambda e: e.tensor_tensor(out=mtmp[:], in0=pp[:, :], in1=gT[:, 8 + cc, :], op=ALU.mult),
                                  reads=[pr, "gT"], writes=["Gf"])
                            S.dve(lambda e: e.tensor_tensor(out=mT[:, cc, :], in0=mtmp[:], in1=m1[:, cc, :], op=ALU.add),
                                  reads=["Gf", "sg"], writes=["qT"])
                        fm_proj(wt, wr, c * 128, 128, ogT, "ogT", 512, ev)
                for n in range(2):
                    wt, wr = wload(w_mix_o, 0, 8, n * 512, 512)
                    for t in range(4):
                        tm_proj(mT[:, :, t * 128:(t + 1) * 128], "qT", wt, wr, 0, 512,
                                lambda pp, pr, t=t, n=n: S.dve(lambda e: e.tensor_tensor(
                                    out=xg[:, t, n * 512:(n + 1) * 512], in0=pp[:, :], in1=xg[:, t, n * 512:(n + 1) * 512], op=ALU.add),
                                    reads=[pr, "xg%d" % t], writes=["xg%d" % t]))

                for t in range(4):
                    dump("x1_%d_%d" % (grp, t), xg[:, t, :], 1024, ["xg%d" % t])
                chk("merge")
                for t in range(4):
                    norm_T(xg[:, t, :], "xg%d" % t, wn2, "wn", hT[:, :, t * 128:(t + 1) * 128], "hT%d" % t)
                for c in range(2):
                    fm_proj(wmq, "wmq", c * 128, 128, hT, HT_ALL, 512,
                            lambda pp, pr, c=c: act(qmT[:, c, :], pp[:, :], AF.Copy, [pr], ["qmT", "vg"]))
                for t in range(4):
                    for r in range(2):
                        pp, pr = newps("r0" if r == 0 else "r1")
                        for mt in range(2):
                            for c in range(2):
                                col = (mt * 2 + c) * 128
                                mm(pp[:, col:col + 128], kmT[64 * r:64 * r + 64, c, mt * 128:(mt + 1) * 128],
                                   qmT[64 * r:64 * r + 64, c, t * 128:(t + 1) * 128], True, True, ["kmT", "qmT"], [pr])
                        act(PmT[:, r, :, :, :], pp[:, :].rearrange("p (m c q) -> p m c q", m=2, c=2), AF.Exp, [pr], ["PmT", "vg"], scale=0.125)
                    po, por = newps()
                    for c in range(2):
                        for r in range(2):
                            h = 2 * c + r
                            for mt in range(2):
                                mm(po[:, h * 65:(h + 1) * 65], PmT[:, r, mt, c, :], vmaug[:, mt, h, :], mt == 0, mt == 1, ["PmT", "vmaug"], [por])
                    pov = po[:, 0:260].rearrange("p (h e) -> p h e", h=4)
                    S.dve(lambda e, pov=pov: e.reciprocal(out=den[:, 0:4], in_=pov[:, :, 64]), reads=[por], writes=["den"])
                    S.dve(lambda e, pov=pov: e.tensor_tensor(out=om[:, :].rearrange("p (h d) -> p h d", h=4), in0=pov[:, :, 0:64],
                                                             in1=den[:, 0:4].unsqueeze(2).to_broadcast([128, 4, 64]), op=ALU.mult),
                          reads=[por, "den"], writes=["om"])
                    pp, pr = newps()
                    for k in range(2):
                        tr(pp[:, k * 128:(k + 1) * 128], om[:, k * 128:(k + 1) * 128], ["om"], [pr])
                    act(omT[:, :, t * 128:(t + 1) * 128], pp[:, 0:256].rearrange("p (a b) -> p a b", a=2), AF.Copy, [pr], ["omT", "vg"])
                for t in range(4):
                    for n in range(2):
                        pp, pr = newps()
                        for c in range(2):
                            mm(pp[:, :], omT[:, c, t * 128:(t + 1) * 128], wmo[:, c, n * 512:(n + 1) * 512], c == 0, c == 1, ["omT", "wmo"], [pr])
                        S.dve(lambda e, pp=pp, t=t, n=n: e.tensor_tensor(
                            out=xg[:, t, n * 512:(n + 1) * 512], in0=pp[:, :], in1=xg[:, t, n * 512:(n + 1) * 512], op=ALU.add),
                            reads=[pr, "xg%d" % t], writes=["xg%d" % t])

                for t in range(4):
                    dump("x2_%d_%d" % (grp, t), xg[:, t, :], 1024, ["xg%d" % t])
                chk("xattn")
                for t in range(4):
                    norm_T(xg[:, t, :], "xg%d" % t, wn3, "wn", hT[:, :, t * 128:(t + 1) * 128], "hT%d" % t)
                for i in range(6):
                    ncol = 512 if i < 5 else 256
                    wg_, wgr = wload(w_ffn_gate, 0, 8, i * 512, ncol)
                    wu_, wur = wload(w_ffn_up, 0, 8, i * 512, ncol)
                    for c in range(ncol // 128):
                        fc = i * 4 + c
                        slt, slr = sl[fc % 2], "sl%d" % (fc % 2)
                        fm_proj(wg_, wgr, c * 128, 128, hT, HT_ALL, 512,
                                lambda pp, pr, slt=slt, slr=slr: act(slt[:], pp[:, :], AF.Silu, [pr], [slr]))
                        fm_proj(wu_, wur, c * 128, 128, hT, HT_ALL, 512,
                                lambda pp, pr, slt=slt, slr=slr, fc=fc: S.dve(lambda e: e.tensor_tensor(
                                    out=actT[:, fc, :], in0=pp[:, :], in1=slt[:], op=ALU.mult), reads=[pr, slr], writes=["big"]))
                for n in range(2):
                    pst = [newps() for _ in range(4)]
                    for piece, (k0, nk) in enumerate(((0, 8), (8, 8), (16, 6))):
                        wt, wr = wload(w_ffn_down, k0, nk, n * 512, 512)
                        for t in range(4):
                            pp, pr = pst[t]
                            for k in range(nk):
                                kk = k0 + k
                                mm(pp[:, :], actT[:, kk, t * 128:(t + 1) * 128], wt[:, k, :], kk == 0, kk == 21, ["big", wr], [pr])
                    for t in range(4):
                        pp, pr = pst[t]
                        S.dve(lambda e, pp=pp, t=t, n=n: e.tensor_tensor(
                            out=xg[:, t, n * 512:(n + 1) * 512], in0=pp[:, :], in1=xg[:, t, n * 512:(n + 1) * 512], op=ALU.add),
                            reads=[pr, "xg%d" % t], writes=["xg%d" % t])

                chk("ffn")
                for t in range(4):
                    act(junk[:], xg[:, t, :], AF.Square, ["xg%d" % t], ["junk", "ss"], accum_out=ss[:, 0:1])
                    rstd_from_ss(0, 1.0 / D)
                    S.dve(lambda e, t=t: e.scalar_tensor_tensor(out=yst[:], in0=xg[:, t, :], scalar=rs[:, 0:1], in1=wfin[:],
                                                                op0=ALU.mult, op1=ALU.mult),
                          reads=["xg%d" % t, "rs", "wfin"], writes=["xs"])
                    row0 = (grp * 4 + t) * 128
                    ld("sp", yout[row0:row0 + 128, :], yst[:], "yout", [], reads=["xs"])

        except _Stop:
            pass
        S.emit(final_wait_keys=[k for k in ["yout"] + dbg_keys if k in S.dma_count])
    return nc


_CONST_CACHE = {}


def _consts():
    if not _CONST_CACHE:
        ident = np.eye(128, dtype=np.float32)
        a = np.arange(128)
        u64 = ((a[:, None] // 64 == a[None, :] // 64) & (a[:, None] > a[None, :])).astype(np.float32) * (-1.0 / 16)
        u128 = (a[:, None] > a[None, :]).astype(np.float32) * (-1.0 / 16)
        ind2 = np.zeros((128, 2), np.float32)
        ind2[:64, 0] = -1.0 / 16
        ind2[64:, 1] = -1.0 / 16
        ind1 = np.full((128, 2), -1.0 / 16, np.float32)
        _CONST_CACHE.update(ident=ident, ucum64=u64, ucum128=u128, ind2=ind2, ind1=ind1)
    return _CONST_CACHE


def make_in_maps(inputs, ncore=NCORE, tok_core=TOK_CORE, npre_tiles=48):
    x = np.asarray(inputs["x"], dtype=np.float32)
    memv = np.asarray(inputs["mem"], dtype=np.float32)
    B, SEQ, _ = x.shape
    per_b = SEQ // tok_core
    c = _consts()
    maps = []
    for core in range(ncore):
        b, j = core // per_b, core % per_b
        npre = npre_tiles * 128
        xa = np.zeros((npre + tok_core, D), np.float32)
        end = (j + 1) * tok_core
        start = max(0, j * tok_core - npre)
        seg = x[b, start:end]
        xa[npre + tok_core - seg.shape[0]:] = seg
        m = dict(xall=xa, mem=np.ascontiguousarray(memv[b]),
                 flagb=np.full((128, 1), 0.0 if j > 0 else -30000.0, np.float32))
        m.update(c)
        for k, v in inputs.items():
            if k in ("x", "mem"):
                continue
            v = np.asarray(v, dtype=np.float32)
            m[k] = np.ascontiguousarray(v[0]) if k != "norm_final_w" else np.ascontiguousarray(v)
        maps.append(m)
    return maps


_NC_CACHE = {}


def kernel(**inputs):
    if "nc" not in _NC_CACHE:
        _NC_CACHE["nc"] = build()
    nc = _NC_CACHE["nc"]
    maps = make_in_maps(inputs)
    res = run_bass_kernel_spmd(nc, maps, core_ids=list(range(NCORE)))
    x = inputs["x"]
    B, SEQ, _ = x.shape
    per_b = SEQ // TOK_CORE
    out = np.empty((B, SEQ, D), np.float32)
    for core in range(NCORE):
        b, j = core // per_b, core % per_b
        out[b, j * TOK_CORE:(j + 1) * TOK_CORE] = res.results[core]["y"]
    return out
```

```python
import contextlib
import numpy as np
import concourse.bass as bass
import concourse.mybir as mybir
from concourse.bass_utils import run_bass_kernel_spmd

F32 = mybir.dt.float32
BF16 = mybir.dt.bfloat16
AF = mybir.ActivationFunctionType
ALU = mybir.AluOpType

D = 1024
NCORE = 8
TOK_CORE = 2048
DFF = 2816
INC = 6416
O_QA, O_KA, O_VA, O_QG, O_KG, O_VG, O_GG, O_AL, O_GT = 0, 1024, 1152, 1280, 1792, 2304, 3328, 4352, 4368


class _Stop(Exception):
    pass


class Sched:
    ENGS = ("pe", "act", "dve", "pool", "sp")

    def __init__(self, nc, same_engine_sync=True):
        self.nc = nc
        self.ops = []
        self.last_write = {}
        self.reads_since = {}
        self.dma_count = {}
        self.same_engine_sync = same_engine_sync

    def _add(self, eng, fn, reads, writes, dma_key=None):
        idx = len(self.ops)
        deps = set()
        for r in reads:
            lw = self.last_write.get(r)
            if lw is not None:
                deps.add(lw)
        for w in writes:
            lw = self.last_write.get(w)
            if lw is not None:
                deps.add(lw)
            for rd in self.reads_since.get(w, ()):
                deps.add(rd)
        for r in reads:
            self.reads_since.setdefault(r, []).append(idx)
        for w in writes:
            self.last_write[w] = idx
            self.reads_since[w] = []
        deps.discard(idx)
        op = dict(eng=eng, fn=fn, deps=deps, dma_key=dma_key, signal=False, idx=idx)
        if dma_key is not None:
            c = self.dma_count.get(dma_key, 0) + 16
            self.dma_count[dma_key] = c
            op["dma_val"] = c
        self.ops.append(op)
        return idx

    def pe(self, fn, reads=(), writes=()):
        return self._add("pe", fn, reads, writes)

    def act(self, fn, reads=(), writes=()):
        return self._add("act", fn, reads, writes)

    def dve(self, fn, reads=(), writes=()):
        return self._add("dve", fn, reads, writes)

    def pool(self, fn, reads=(), writes=()):
        return self._add("pool", fn, reads, writes)

    def dma(self, eng, fn, key, reads=(), writes=()):
        return self._add(eng, fn, reads, writes, dma_key=key)

    def emit(self, final_wait_keys=()):
        nc = self.nc
        ops = self.ops
        need = []
        for op in ops:
            nd = []
            for d in sorted(op["deps"]):
                y = ops[d]
                if y["dma_key"] is None and y["eng"] == op["eng"] and op["dma_key"] is None:
                    if op["eng"] == "pe" or not self.same_engine_sync:
                        continue
                nd.append(d)
                if y["dma_key"] is None:
                    y["signal"] = True
            need.append(nd)
        cnt = {e: 0 for e in self.ENGS}
        for op in ops:
            if op["dma_key"] is None and op["signal"]:
                cnt[op["eng"]] += 1
                op["sig_val"] = cnt[op["eng"]]
        with contextlib.ExitStack() as st:
            esem = {e: st.enter_context(nc.semaphore("s_" + e)) for e in self.ENGS}
            dsem = {k: st.enter_context(nc.semaphore("d_%d" % i)) for i, k in enumerate(self.dma_count)}
            block = st.enter_context(nc.Block())
            per = {e: [] for e in self.ENGS}
            for op, nd in zip(ops, need):
                per[op["eng"]].append((op, nd))

            def body(ename, handle):
                waited = {}
                for op, nd in per[ename]:
                    for d in nd:
                        y = ops[d]
                        if y["dma_key"] is not None:
                            sem, val, k = dsem[y["dma_key"]], y["dma_val"], ("d", y["dma_key"])
                        else:
                            sem, val, k = esem[y["eng"]], y["sig_val"], ("e", y["eng"])
                        if waited.get(k, 0) >= val:
                            continue
                        waited[k] = val
                        handle.wait_ge(sem, val)
                    ins = op["fn"](handle)
                    if op["dma_key"] is not None:
                        ins.then_inc(dsem[op["dma_key"]], 16)
                    elif op["signal"]:
                        ins.then_inc(esem[ename], 1)
                if ename == "sp":
                    for k in final_wait_keys:
                        handle.wait_ge(dsem[k], self.dma_count[k])

            @block.tensor
            def _(e):
                body("pe", e)

            @block.scalar
            def _(e):
                body("act", e)

            @block.vector
            def _(e):
                body("dve", e)

            @block.gpsimd
            def _(e):
                body("pool", e)

            @block.sync
            def _(e):
                body("sp", e)


def build(NPRE=48, NGRP=4, NSLOT=3, same_engine_sync=True, dbg=False, stop_after=None):
    nc = bass.Bass("TRN2", target_bir_lowering=False)
    NT_ALL = NPRE + NGRP * 4

    def din(name, shape):
        return nc.dram_tensor(name, list(shape), F32, kind="ExternalInput").ap()

    xall = din("xall", [NT_ALL * 128, D])
    mem = din("mem", [256, D])
    flagb = din("flagb", [128, 1])
    ident_d = din("ident", [128, 128])
    ucum64_d = din("ucum64", [128, 128])
    ucum128_d = din("ucum128", [128, 128])
    ind2_d = din("ind2", [128, 2])
    ind1_d = din("ind1", [128, 2])
    norm_mix_w = din("norm_mix_w", [D])
    w_in = din("w_in", [D, INC])
    b_gate = din("b_gate", [2 * D])
    attn_sinks = din("attn_sinks", [16])
    gla_gate_w2 = din("gla_gate_w2", [16, 512])
    gla_gate_b = din("gla_gate_b", [512])
    gla_norm_w = din("gla_norm_w", [256])
    w_attn_o = din("w_attn_o", [D, D])
    w_gla_o = din("w_gla_o", [D, D])
    w_mix_o = din("w_mix_o", [D, D])
    norm_mem_q_w = din("norm_mem_q_w", [D])
    norm_mem_kv_w = din("norm_mem_kv_w", [D])
    w_mem_q = din("w_mem_q", [D, 256])
    w_mem_kv = din("w_mem_kv", [D, 512])
    w_mem_o = din("w_mem_o", [256, D])
    norm_ffn_w = din("norm_ffn_w", [D])
    w_ffn_gate = din("w_ffn_gate", [D, DFF])
    w_ffn_up = din("w_ffn_up", [D, DFF])
    w_ffn_down = din("w_ffn_down", [DFF, D])
    norm_final_w = din("norm_final_w", [D])
    yout = nc.dram_tensor("y", [NGRP * 512, D], F32, kind="ExternalOutput").ap()
    WSRC = {"w_in": w_in, "w_attn_o": w_attn_o, "w_gla_o": w_gla_o, "w_mix_o": w_mix_o,
            "w_ffn_gate": w_ffn_gate, "w_ffn_up": w_ffn_up, "w_ffn_down": w_ffn_down, "w_mem_kv": w_mem_kv}
    WB = {}
    for nm_ in ("w_in", "w_attn_o", "w_gla_o", "w_mix_o", "w_ffn_gate", "w_ffn_up", "w_ffn_down"):
        WB[nm_] = nc.dram_tensor("wb_" + nm_, list(WSRC[nm_].shape), BF16, kind="Internal").ap()

    S = Sched(nc, same_engine_sync=same_engine_sync)
    st = contextlib.ExitStack()
    with st:
        def sb(name, shape, dt=F32):
            return st.enter_context(nc.sbuf_tensor("sb_" + name, list(shape), dt))

        psb = [st.enter_context(nc.psum_tensor("ps%d" % i, [128, 512], F32)) for i in range(8)]
        pcnt = {"any": 0, "r0": 0, "r1": 0}
        PSETS = {"any": [0, 1, 2, 3, 4, 5, 6, 7], "r0": [0, 1, 4, 5], "r1": [2, 3, 6, 7]}

        def newps(kind="any"):
            lst = PSETS[kind]
            i = lst[pcnt[kind] % len(lst)]
            pcnt[kind] += 1
            return psb[i], "ps%d" % i

        idt = sb("idt", [128, 128])
        uc64 = sb("uc64", [128, 128])
        uc128 = sb("uc128", [128, 128])
        ind2 = sb("ind2", [128, 2])
        ind1 = sb("ind1", [128, 2])
        flg = sb("flg", [128, 1])
        vst = [sb("vst%d" % i, [16, 128]) for i in range(2)]
        wn1 = sb("wn1", [128, 8])
        wn2 = sb("wn2", [128, 8])
        wn3 = sb("wn3", [128, 8])
        wnkv = sb("wnkv", [128, 8])
        gnw8 = sb("gnw8", [128, 8])
        bgate = sb("bgate", [128, 16])
        exps = sb("exps", [128, 16])
        wfin = sb("wfin", [128, D])
        w2aug = sb("w2aug", [32, 512], BF16)
        wkdup = sb("wkdup", [128, 8, 2, 128], BF16)
        wva = sb("wva", [128, 8, 128], BF16)
        walr = sb("walr", [128, 8, 16], BF16)
        wmq = sb("wmq", [128, 8, 256], BF16)
        wmo = sb("wmo", [128, 2, D], BF16)
        big = sb("big", [128, 8 * 1536], BF16)
        wpre = big[:, :].rearrange("p (k c) -> p k c", k=8)
        actT = big[:, 0:22 * 512].rearrange("p (f t) -> p f t", f=22)
        kmT = sb("kmT", [128, 2, 256], BF16)
        vmaug = sb("vmaug", [128, 2, 4, 65], BF16)
        wsl = [sb("wsl%d" % i, [128, 8, 512], BF16) for i in range(NSLOT)]
        xg = sb("xg", [128, 4, D])
        xs = sb("xs", [128, D])
        junk = sb("junk", [128, D], BF16)
        ss = sb("ss", [128, 8])
        rs = sb("rs", [128, 8])
        hT = sb("hT", [128, 8, 512], BF16)
        qT = sb("qT", [128, 8, 512], BF16)
        kT = sb("kT", [128, 2, 640], BF16)
        vaug = sb("vaug", [128, 5, 2, 65], BF16)
        PT = [sb("PT%d" % i, [128, 2, 4, 128], BF16) for i in range(2)]
        oa = sb("oa", [128, D])
        oaT = sb("oaT", [128, 8, 512], BF16)
        den = sb("den", [128, 16])
        qgT = sb("qgT", [128, 4, 512], BF16)
        alrT = sb("alrT", [32, 512], BF16)
        kg = sb("kg", [128, 4, 512])
        vg = sb("vg", [128, 4, D], BF16)
        sg = sb("sg", [128, 4, D], BF16)
        Gf = sb("Gf", [128, 512])
        Dm = sb("Dm", [128, 512])
        av = sb("av", [128, 8])
        kdec = sb("kdec", [128, 512], BF16)
        Sst = sb("Sst", [128, D])
        Sbf = [sb("Sbf%d" % i, [128, D], BF16) for i in range(2)]
        og = sb("og", [128, D])
        ogT = sb("ogT", [128, 8, 512], BF16)
        gT = sb("gT", [128, 16, 512], BF16)
        m1 = sg[:, :, :].rearrange("p a (b c) -> p (a b) c", b=2)
        mtmp = Gf
        mT = qT
        qmT = vg[:, 0, :].rearrange("p (a b) -> p a b", a=2)
        PmT = vg[:, 1, :].rearrange("p (r m c q) -> p r m c q", r=2, m=2, c=2)
        omT = vg[:, 2, :].rearrange("p (a b) -> p a b", a=2)
        om = sb("om", [128, 256])
        sl = [sb("sl%d" % i, [128, 512], BF16) for i in range(2)]
        yst = xs
        xm = oa

        HT_ALL = ["hT0", "hT1", "hT2", "hT3"]

        def chk(name):
            if stop_after == name:
                raise _Stop()

        dbg_keys = []

        def dump(name, ap, ncols, reads):
            if not dbg:
                return
            d_ = nc.dram_tensor("dbg_" + name, [128, ncols], F32, kind="ExternalOutput").ap()
            k_ = "dbg_" + name
            dbg_keys.append(k_)
            S.dma("sp", lambda e: e.dma_start(out=d_, in_=ap), k_, reads=reads)

        ukey = [0]

        def ld(eng, out_ap, in_ap, key, writes, reads=(), slow=False):
            if key in ("c", "cw"):
                ukey[0] += 1
                if key == "cw":
                    reads = list(reads) + ["cwtok%d" % (ukey[0] % 2)]
                    writes = list(writes) + ["cwtok%d" % (ukey[0] % 2)]
                key = "%s%d" % (key, ukey[0])
            if slow:
                S.dma(eng, lambda e: e.dma_start(out=out_ap, in_=in_ap, allow_slow_non_contiguous=True), key, reads=reads, writes=writes)
            else:
                S.dma(eng, lambda e: e.dma_start(out=out_ap, in_=in_ap), key, reads=reads, writes=writes)

        slot_i = [0]

        def wload(wname, k0, nk, c0, ncols):
            i = slot_i[0] % NSLOT
            slot_i[0] += 1
            W = WB.get(wname, WSRC[wname])
            src = W[k0 * 128:(k0 + nk) * 128, c0:c0 + ncols].rearrange("(k p) c -> p k c", p=128)
            ld("pool", wsl[i][:, 0:nk, 0:ncols], src, "wsl%d" % i, writes=["wsl%d" % i], reads=["wb_" + wname])
            return wsl[i], "wsl%d" % i

        def mm(out_ap, lhsT, rhs, start, stop, reads, writes):
            S.pe(lambda e: e.matmul(out=out_ap, lhsT=lhsT, rhs=rhs, start=start, stop=stop), reads=reads, writes=writes)

        def tr(out_ap, in_ap, reads, writes):
            S.pe(lambda e: e.transpose(out=out_ap, in_=in_ap, identity=idt[:]), reads=list(reads) + ["idt"], writes=writes)

        def act(out_ap, in_ap, func, reads, writes, **kw):
            S.act(lambda e: e.activation(out=out_ap, in_=in_ap, func=func, **kw), reads=reads, writes=writes)

        def rstd_from_ss(col, n_inv, ncol=1):
            act(rs[:, col:col + ncol], ss[:, col:col + ncol], AF.Sqrt, ["ss"], ["rs"], scale=n_inv, bias=1e-6)
            S.dve(lambda e: e.reciprocal(out=rs[:, col:col + ncol], in_=rs[:, col:col + ncol]), reads=["rs"], writes=["rs"])

        def norm_T(x_ap, xres, wn, wnres, dst, dstres):
            act(junk[:], x_ap, AF.Square, [xres], ["junk", "ss"], accum_out=ss[:, 0:1])
            rstd_from_ss(0, 1.0 / D)
            S.dve(lambda e: e.tensor_scalar(out=xs[:], in0=x_ap, scalar1=rs[:, 0:1], scalar2=None, op0=ALU.mult),
                  reads=[xres, "rs"], writes=["xs"])
            for half in range(2):
                pp, pr = newps()
                for k in range(4):
                    kk = half * 4 + k
                    tr(pp[:, k * 128:(k + 1) * 128], xs[:, kk * 128:(kk + 1) * 128], ["xs"], [pr])
                S.dve(lambda e, pp=pp, half=half: e.tensor_tensor(
                    out=dst[:, half * 4:(half + 1) * 4, :], in0=pp[:, :].rearrange("p (a b) -> p a b", a=4),
                    in1=wn[:, half * 4:(half + 1) * 4].unsqueeze(2).to_broadcast([128, 4, 128]), op=ALU.mult),
                    reads=[pr, wnres], writes=[dstres])

        def tm_proj(lhs_tile, lhs_res, wt, wres, c0, ncols, evac):
            pp, pr = newps()
            for k in range(8):
                mm(pp[:, 0:ncols], lhs_tile[:, k, :], wt[:, k, c0:c0 + ncols], k == 0, k == 7, [lhs_res, wres], [pr])
            evac(pp, pr)

        def fm_proj(wt, wres, c0, m, rhs, rhs_res, ntok, evac, nk=8):
            pp, pr = newps()
            rr = list(rhs_res) if isinstance(rhs_res, (list, tuple)) else [rhs_res]
            for k in range(nk):
                mm(pp[0:m, 0:ntok], wt[:, k, c0:c0 + m], rhs[:, k, 0:ntok], k == 0, k == nk - 1, rr + [wres], [pr])
            evac(pp, pr)

        def gla_gate(tokc0, umat, ures, indt, indres):
            pz, pzr = newps()
            mm(pz[:, :], alrT[0:17, tokc0:tokc0 + 128], w2aug[0:17, :], True, True, ["alrT", "w2aug"], [pzr])
            act(Gf[:], pz[:, :], AF.Exp, [pzr], ["Gf"], scale=-1.0)
            act(Gf[:], Gf[:], AF.Ln, ["Gf"], ["Gf"], bias=1.0)
            pR, pRr = newps()
            mm(pR[:, :], umat[:], Gf[:], True, True, [ures, "Gf"], [pRr])
            pa, par = newps()
            for h in range(4):
                mm(pa[:, 2 * h:2 * h + 2], Gf[:, h * 128:(h + 1) * 128], indt[:], True, True, ["Gf", indres], [par])
            act(Dm[:], pR[:, :], AF.Exp, [pRr], ["Dm"])
            act(av[:], pa[:, 0:8], AF.Exp, [par], ["av"])

        def state_update(krows, j, vsrc, vres, kind):
            r0, r1 = krows
            pA, pAr = newps(kind)
            pB, pBr = newps(kind)
            for h in range(4):
                pp, pr = (pA, pAr) if h < 2 else (pB, pBr)
                mm(pp[:, (h % 2) * 256:(h % 2 + 1) * 256], kdec[r0:r1, h * 128:(h + 1) * 128],
                   vsrc[r0:r1, h * 256:(h + 1) * 256], True, True, ["kdec", vres], [pr])
            for h in range(4):
                pp, pr = (pA, pAr) if h < 2 else (pB, pBr)
                S.dve(lambda e, h=h, pp=pp: e.scalar_tensor_tensor(
                    out=Sst[:, h * 256:(h + 1) * 256], in0=Sst[:, h * 256:(h + 1) * 256], scalar=av[:, 2 * h + j:2 * h + j + 1],
                    in1=pp[:, (h % 2) * 256:(h % 2 + 1) * 256], op0=ALU.mult, op1=ALU.add),
                    reads=["Sst", "av", pr], writes=["Sst"])

        try:
            ld("sp", idt[:], ident_d, "c", ["idt"])
            ld("sp", uc64[:], ucum64_d, "c", ["uc64"])
            ld("sp", uc128[:], ucum128_d, "c", ["uc128"])
            ld("sp", ind2[:], ind2_d, "c", ["ind2"])
            ld("sp", ind1[:], ind1_d, "c", ["ind1"])
            ld("sp", flg[:], flagb, "c", ["flg"])
            for vi, (t_, src_, kk_, res_) in enumerate(((wn1, norm_mix_w, 8, "wn"), (wn2, norm_mem_q_w, 8, "wn"), (wn3, norm_ffn_w, 8, "wn"),
                                                      (wnkv, norm_mem_kv_w, 8, "wn"), (bgate, b_gate, 16, "bgate"), (gnw8, gla_norm_w, 2, "gnw8"))):
                stg = vst[vi % 2]
                sres = "vst%d" % (vi % 2)
                ld("sp", stg[0:kk_, :], src_.rearrange("(k p) -> k p", p=128), "c", [sres])
                pp, pr = newps()
                S.pe(lambda e, pp=pp, stg=stg, kk_=kk_: e.transpose(out=pp[:, 0:kk_], in_=stg[0:kk_, :], identity=idt[0:kk_, 0:kk_]),
                     reads=[sres, "idt"], writes=[pr])
                if kk_ == 2:
                    for i in range(4):
                        act(t_[:, 2 * i:2 * i + 2], pp[:, 0:2], AF.Copy, [pr], [res_])
                else:
                    act(t_[:, 0:kk_], pp[:, 0:kk_], AF.Copy, [pr], [res_])
            ld("sp", exps[:], attn_sinks.partition_broadcast(128), "c", ["exps"])
            ld("sp", wfin[:], norm_final_w.partition_broadcast(128), "c", ["wfin"])
            CONSTS = ["idt", "uc64", "uc128", "ind2", "ind1", "flg", "wn", "gnw8", "bgate", "exps", "wfin"]
            act(exps[:], exps[:], AF.Exp, ["exps"], ["exps"])
            ld("pool", w2aug[0:16, :], gla_gate_w2, "cw", ["w2aug"])
            ld("pool", w2aug[16:17, :], gla_gate_b.rearrange("(o c) -> o c", o=1), "cw", ["w2aug"])
            for g in range(2):
                for r in range(2):
                    ld("pool", wkdup[:, :, g, 64 * r:64 * r + 64],
                       w_in[:, O_KA + 64 * g:O_KA + 64 * g + 64].rearrange("(k p) c -> p k c", p=128), "cw", ["wkdup"])
            ld("pool", wva[:], w_in[:, O_VA:O_VA + 128].rearrange("(k p) c -> p k c", p=128), "cw", ["wva"])
            ld("pool", walr[:], w_in[:, O_AL:O_AL + 16].rearrange("(k p) c -> p k c", p=128), "cw", ["walr"])
            ld("pool", wmq[:], w_mem_q.rearrange("(k p) c -> p k c", p=128), "cw", ["wmq"])
            ld("pool", wmo[:], w_mem_o.rearrange("(k p) c -> p k c", p=128), "cw", ["wmo"])
            for i in range(3):
                ld("pool", wpre[:, :, i * 512:(i + 1) * 512],
                   w_in[:, O_KG + i * 512:O_KG + (i + 1) * 512].rearrange("(k p) c -> p k c", p=128), "cw", ["big"])
            S.dve(lambda e: e.memset(Sst[:], 0.0), writes=["Sst"])
            S.dve(lambda e: e.memset(alrT[:], 1.0), writes=["alrT"])
            S.dve(lambda e: e.memset(vaug[:], 1.0), writes=["vaug"])
            S.dve(lambda e: e.memset(vmaug[:], 1.0), writes=["vmaug"])
            for i in range(2):
                S.dve(lambda e, i=i: e.memset(PT[i][:], 0.0), writes=["PT%d" % i])

            for mt in range(2):
                ld("sp", xm[:], mem[mt * 128:(mt + 1) * 128, :], "xm", ["oa"])
                norm_T(xm[:], "oa", wnkv, "wn", hT[:, :, mt * 128:(mt + 1) * 128], "hT%d" % mt)
            wt, wr = wload("w_mem_kv", 0, 8, 0, 512)
            for c in range(2):
                fm_proj(wt, wr, c * 128, 128, hT, ["hT0", "hT1"], 256,
                        lambda pp, pr, c=c: act(kmT[:, c, :], pp[:, 0:256], AF.Copy, [pr], ["kmT"]))
            for mt in range(2):
                tm_proj(hT[:, :, mt * 128:(mt + 1) * 128], "hT%d" % mt, wt, wr, 256, 256,
                        lambda pp, pr, mt=mt: act(vmaug[:, mt, :, 0:64], pp[:, 0:256].rearrange("p (h d) -> p h d", h=4), AF.Copy, [pr], ["vmaug"]))

            for nm_ in ("w_in", "w_attn_o", "w_gla_o", "w_mix_o", "w_ffn_gate", "w_ffn_up", "w_ffn_down"):
                src_, dst_ = WSRC[nm_], WB[nm_]
                for rb in range(src_.shape[0] // 128):
                    S.dma("pool", lambda e, src_=src_, dst_=dst_, rb=rb: e.dma_start(
                        out=dst_[rb * 128:(rb + 1) * 128, :], in_=src_[rb * 128:(rb + 1) * 128, :], max_dma_last_dim=4096),
                        "cast", reads=["casttok"], writes=["casttok", "wb_" + nm_])
            chk("setup")
            for t in range(NPRE):
                ld("sp", xm[:], xall[t * 128:(t + 1) * 128, :], "xm", ["oa"])
                hTt = hT[:, :, (t % 2) * 128:(t % 2 + 1) * 128]
                hres = "hT%d" % (t % 2)
                norm_T(xm[:], "oa", wn1, "wn", hTt, hres)
                tm_proj(hTt, hres, wpre, "big", 0, 512,
                        lambda pp, pr: act(kg[:, 0, :], pp[:, :], AF.Copy, [pr], ["kg"]))
                for i in range(2):
                    tm_proj(hTt, hres, wpre, "big", 512 + i * 512, 512,
                            lambda pp, pr, i=i: act(vg[:, 0, i * 512:(i + 1) * 512], pp[:, :], AF.Copy, [pr], ["vg", "qmT", "PmT", "omT"]))
                fm_proj(walr, "walr", 0, 16, hTt, hres, 128,
                        lambda pp, pr: act(alrT[0:16, 0:128], pp[0:16, 0:128], AF.Copy, [pr], ["alrT"]))
                gla_gate(0, uc128, "uc128", ind1, "ind1")
                S.dve(lambda e: e.tensor_tensor(out=kdec[:], in0=kg[:, 0, :], in1=Dm[:], op=ALU.mult), reads=["kg", "Dm"], writes=["kdec"])
                state_update((0, 128), 0, vg[:, 0, :], "vg", "any")
                if t == NPRE - 1:
                    for g in range(2):
                        fm_proj(wkdup[:, :, g, :], "wkdup", 0, 128, hTt, hres, 128,
                                lambda pp, pr, g=g: act(kT[:, g, 0:128], pp[:, 0:128], AF.Copy, [pr], ["kT"]))
                    tm_proj(hTt, hres, wva, "wva", 0, 128,
                            lambda pp, pr: act(vaug[:, 0, :, 0:64], pp[:, 0:128].rearrange("p (g d) -> p g d", g=2), AF.Copy, [pr], ["vaug"]))

            chk("prefix")
            for grp in range(NGRP):
                for t in range(4):
                    row0 = (NPRE + grp * 4 + t) * 128
                    ld("sp", xg[:, t, :], xall[row0:row0 + 128, :], "xg%d" % t, ["xg%d" % t])
                    norm_T(xg[:, t, :], "xg%d" % t, wn1, "wn", hT[:, :, t * 128:(t + 1) * 128], "hT%d" % t)
                chk("norm1")
                for i in range(2):
                    wt, wr = wload("w_in", 0, 8, O_QA + i * 512, 512)
                    for c in range(4):
                        fm_proj(wt, wr, c * 128, 128, hT, HT_ALL, 512,
                                lambda pp, pr, cc=i * 4 + c: act(qT[:, cc, :], pp[:, :], AF.Copy, [pr], ["qT"]))
                for g in range(2):
                    fm_proj(wkdup[:, :, g, :], "wkdup", 0, 128, hT, HT_ALL, 512,
                            lambda pp, pr, g=g: act(kT[:, g, 128:640], pp[:, :], AF.Copy, [pr], ["kT"]))
                for t in range(4):
                    tm_proj(hT[:, :, t * 128:(t + 1) * 128], "hT%d" % t, wva, "wva", 0, 128,
                            lambda pp, pr, t=t: act(vaug[:, 1 + t, :, 0:64], pp[:, 0:128].rearrange("p (g d) -> p g d", g=2), AF.Copy, [pr], ["vaug"]))
                wt, wr = wload("w_in", 0, 8, O_QG, 512)
                for c in range(4):
                    fm_proj(wt, wr, c * 128, 128, hT, HT_ALL, 512,
                            lambda pp, pr, c=c: act(qgT[:, c, :], pp[:, :], AF.Copy, [pr], ["qgT"], scale=float(128 ** -0.5)))
                fm_proj(walr, "walr", 0, 16, hT, HT_ALL, 512,
                        lambda pp, pr: act(alrT[0:16, :], pp[0:16, :], AF.Copy, [pr], ["alrT"]))
                wt, wr = wload("w_in", 0, 8, O_KG, 512)
                for t in range(4):
                    tm_proj(hT[:, :, t * 128:(t + 1) * 128], "hT%d" % t, wt, wr, 0, 512,
                            lambda pp, pr, t=t: act(kg[:, t, :], pp[:, :], AF.Copy, [pr], ["kg"]))
                for i in range(2):
                    wt, wr = wload("w_in", 0, 8, O_VG + i * 512, 512)
                    for t in range(4):
                        tm_proj(hT[:, :, t * 128:(t + 1) * 128], "hT%d" % t, wt, wr, 0, 512,
                                lambda pp, pr, t=t, i=i: act(vg[:, t, i * 512:(i + 1) * 512], pp[:, :], AF.Copy, [pr], ["vg", "qmT", "PmT", "omT"]))
                for i in range(2):
                    wt, wr = wload("w_in", 0, 8, O_GG + i * 512, 512)
                    for t in range(4):
                        tm_proj(hT[:, :, t * 128:(t + 1) * 128], "hT%d" % t, wt, wr, 0, 512,
                                lambda pp, pr, t=t, i=i: act(sg[:, t, i * 512:(i + 1) * 512], pp[:, :], AF.Silu, [pr], ["sg"]))
                for i in range(4):
                    wt, wr = wload("w_in", 0, 8, O_GT + i * 512, 512)
                    for c in range(4):
                        cc = i * 4 + c
                        fm_proj(wt, wr, c * 128, 128, hT, HT_ALL, 512,
                                lambda pp, pr, cc=cc: act(gT[:, cc, :], pp[:, :], AF.Sigmoid, [pr, "bgate"], ["gT"], bias=bgate[:, cc:cc + 1]))

                chk("proj")
                for t in range(4):
                    first = (grp == 0 and t == 0)
                    pO = [(psb[4 + i_], "ps%d" % (4 + i_)) for i_ in range(4)]
                    for g in range(2):
                        for r in range(2):
                            s_ = 2 * g + r
                            PTs, PTr = PT[s_ % 2], "PT%d" % (s_ % 2)
                            pA, pAr = psb[2 * r], "ps%d" % (2 * r)
                            pB, pBr = psb[2 * r + 1], "ps%d" % (2 * r + 1)
                            for j in range(4):
                                i = 4 * g + j
                                q_ap = qT[64 * r:64 * r + 64, i, t * 128:(t + 1) * 128]
                                mm(pA[:, j * 128:(j + 1) * 128], kT[64 * r:64 * r + 64, g, t * 128:(t + 1) * 128], q_ap, True, True, ["kT", "qT"], [pAr])
                                mm(pB[:, j * 128:(j + 1) * 128], kT[64 * r:64 * r + 64, g, (t + 1) * 128:(t + 2) * 128], q_ap, True, True, ["kT", "qT"], [pBr])
                            pAv = pA[:, :].rearrange("p (j q) -> p j q", j=4)
                            pBv = pB[:, :].rearrange("p (j q) -> p j q", j=4)
                            kw = dict(scale=0.125)
                            kwp = dict(scale=0.125, bias=flg[0:64, 0:1]) if first else kw
                            kwp2 = dict(scale=0.125, bias=flg[64:128, 0:1]) if first else kw
                            act(PTs[0:64, 0, :, 0:64], pAv[0:64, :, 0:64], AF.Exp, [pAr, "flg"], [PTr], **kwp)
                            act(PTs[64:128, 0, :, :], pAv[64:128, :, :], AF.Exp, [pAr, "flg"], [PTr], **kwp2)
                            act(PTs[0:64, 1, :, :], pBv[0:64, :, :], AF.Exp, [pBr], [PTr], **kw)
                            act(PTs[64:128, 1, :, 64:128], pBv[64:128, :, 64:128], AF.Exp, [pBr], [PTr], **kw)
                            po, por = pO[s_]
                            for j in range(4):
                                col = j * 65
                                mm(po[:, col:col + 65], PTs[:, 0, j, :], vaug[:, t, g, :], True, False, [PTr, "vaug"], [por])
                                mm(po[:, col:col + 65], PTs[:, 1, j, :], vaug[:, t + 1, g, :], False, True, [PTr, "vaug"], [por])
                    for g in range(2):
                        for r in range(2):
                            s_ = 2 * g + r
                            po, por = pO[s_]
                            pov = po[:, 0:260].rearrange("p (j e) -> p j e", j=4)
                            dv = den[:, s_ * 4:(s_ + 1) * 4]
                            ev = exps[:, g * 8:(g + 1) * 8].rearrange("p (j r) -> p r j", r=2)[:, r, :]
                            ov = oa[:, g * 512:(g + 1) * 512].rearrange("p (j r d) -> p r j d", j=4, r=2)[:, r, :, :]
                            S.dve(lambda e, pov=pov, dv=dv, ev=ev: e.tensor_tensor(out=dv, in0=pov[:, :, 64], in1=ev, op=ALU.add),
                                  reads=[por, "exps"], writes=["den"])
                            S.dve(lambda e, dv=dv: e.reciprocal(out=dv, in_=dv), reads=["den"], writes=["den"])
                            S.dve(lambda e, pov=pov, dv=dv, ov=ov: e.tensor_tensor(
                                out=ov, in0=pov[:, :, 0:64], in1=dv.unsqueeze(2).to_broadcast([128, 4, 64]), op=ALU.mult),
                                reads=[por, "den"], writes=["oa"])
                    dump("oa_%d_%d" % (grp, t), oa[:], 1024, ["oa"])
                    for half in range(2):
                        pp, pr = newps()
                        for k in range(4):
                            kk = half * 4 + k
                            tr(pp[:, k * 128:(k + 1) * 128], oa[:, kk * 128:(kk + 1) * 128], ["oa"], [pr])
                        act(oaT[:, half * 4:(half + 1) * 4, t * 128:(t + 1) * 128], pp[:, :].rearrange("p (a b) -> p a b", a=4), AF.Copy, [pr], ["oaT"])
                S.dve(lambda e: e.tensor_copy(out=kT[:, :, 0:128], in_=kT[:, :, 512:640]), reads=["kT"], writes=["kT"])
                S.dve(lambda e: e.tensor_copy(out=vaug[:, 0, :, :], in_=vaug[:, 4, :, :]), reads=["vaug"], writes=["vaug"])

                chk("swa")
                for t in range(4):
                    gla_gate(t * 128, uc64, "uc64", ind2, "ind2")
                    S.dve(lambda e, t=t: e.tensor_tensor(out=kdec[:], in0=kg[:, t, :], in1=Dm[:], op=ALU.mult), reads=["kg", "Dm"], writes=["kdec"])
                    for j in range(2):
                        state_update((64 * j, 64 * j + 64), j, vg[:, t, :], "vg", "r0" if j == 0 else "r1")
                        act(Sbf[j][:], Sst[:], AF.Copy, ["Sst"], ["Sbf%d" % j])
                        pA, pAr = newps()
                        pB, pBr = newps()
                        for h in range(4):
                            pp, pr = (pA, pAr) if h < 2 else (pB, pBr)
                            mm(pp[:, (h % 2) * 256:(h % 2 + 1) * 256], qgT[:, h, t * 128:(t + 1) * 128],
                               Sbf[j][:, h * 256:(h + 1) * 256], True, True, ["qgT", "Sbf%d" % j], [pr])
                        act(og[64 * j:64 * j + 64, 0:512], pA[64 * j:64 * j + 64, :], AF.Copy, [pAr], ["og"])
                        act(og[64 * j:64 * j + 64, 512:1024], pB[64 * j:64 * j + 64, :], AF.Copy, [pBr], ["og"])
                    for h in range(4):
                        act(junk[:, 0:256], og[:, h * 256:(h + 1) * 256], AF.Square, ["og"], ["junk", "ss"], accum_out=ss[:, 4 + h:5 + h])
                    rstd_from_ss(4, 1.0 / 256, ncol=4)
                    for h in range(4):
                        S.dve(lambda e, h=h, t=t: e.scalar_tensor_tensor(
                            out=og[:, h * 256:(h + 1) * 256], in0=og[:, h * 256:(h + 1) * 256], scalar=rs[:, 4 + h:5 + h],
                            in1=sg[:, t, h * 256:(h + 1) * 256], op0=ALU.mult, op1=ALU.mult),
                            reads=["og", "rs", "sg"], writes=["og"])
                    dump("og_%d_%d" % (grp, t), og[:], 1024, ["og"])
                    for half in range(2):
                        pp, pr = newps()
                        for k in range(4):
                            kk = half * 4 + k
                            tr(pp[:, k * 128:(k + 1) * 128], og[:, kk * 128:(kk + 1) * 128], ["og"], [pr])
                        S.dve(lambda e, pp=pp, half=half, t=t: e.tensor_tensor(
                            out=ogT[:, half * 4:(half + 1) * 4, t * 128:(t + 1) * 128], in0=pp[:, :].rearrange("p (a b) -> p a b", a=4),
                            in1=gnw8[:, half * 4:(half + 1) * 4].unsqueeze(2).to_broadcast([128, 4, 128]), op=ALU.mult),
                            reads=[pr, "gnw8"], writes=["ogT"])

                chk("gla")
                for i in range(2):
                    wt, wr = wload("w_attn_o", 0, 8, i * 512, 512)
                    for c in range(4):
                        cc = i * 4 + c
                        fm_proj(wt, wr, c * 128, 128, oaT, "oaT", 512,
                                lambda pp, pr, cc=cc: S.dve(lambda e: e.tensor_tensor(out=m1[:, cc, :], in0=pp[:, :], in1=gT[:, cc, :], op=ALU.mult),
                                                            reads=[pr, "gT"], writes=["sg"]))
                for i in range(2):
                    wt, wr = wload("w_gla_o", 0, 8, i * 512, 512)
                    for c in range(4):
                        cc = i * 4 + c

                        def ev(pp, pr, cc=cc):
                            S.dve(lambda e: e.tensor_tensor(out=mtmp[:], in0=pp[:, :], in1=gT[:, 8 + cc, :], op=ALU.mult),
                                  reads=[pr, "gT"], writes=["Gf"])
                            S.dve(lambda e: e.tensor_tensor(out=mT[:, cc, :], in0=mtmp[:], in1=m1[:, cc, :], op=ALU.add),
                                  reads=["Gf", "sg"], writes=["qT"])
                        fm_proj(wt, wr, c * 128, 128, ogT, "ogT", 512, ev)
                for n in range(2):
                    wt, wr = wload("w_mix_o", 0, 8, n * 512, 512)
                    for t in range(4):
                        tm_proj(mT[:, :, t * 128:(t + 1) * 128], "qT", wt, wr, 0, 512,
                                lambda pp, pr, t=t, n=n: S.dve(lambda e: e.tensor_tensor(
                                    out=xg[:, t, n * 512:(n + 1) * 512], in0=pp[:, :], in1=xg[:, t, n * 512:(n + 1) * 512], op=ALU.add),
                                    reads=[pr, "xg%d" % t], writes=["xg%d" % t]))

                for t in range(4):
                    dump("x1_%d_%d" % (grp, t), xg[:, t, :], 1024, ["xg%d" % t])
                chk("merge")
                for t in range(4):
                    norm_T(xg[:, t, :], "xg%d" % t, wn2, "wn", hT[:, :, t * 128:(t + 1) * 128], "hT%d" % t)
                for c in range(2):
                    fm_proj(wmq, "wmq", c * 128, 128, hT, HT_ALL, 512,
                            lambda pp, pr, c=c: act(qmT[:, c, :], pp[:, :], AF.Copy, [pr], ["qmT", "vg"]))
                for t in range(4):
                    for r in range(2):
                        pp, pr = newps("r0" if r == 0 else "r1")
                        for mt in range(2):
                            for c in range(2):
                                col = (mt * 2 + c) * 128
                                mm(pp[:, col:col + 128], kmT[64 * r:64 * r + 64, c, mt * 128:(mt + 1) * 128],
                                   qmT[64 * r:64 * r + 64, c, t * 128:(t + 1) * 128], True, True, ["kmT", "qmT"], [pr])
                        act(PmT[:, r, :, :, :], pp[:, :].rearrange("p (m c q) -> p m c q", m=2, c=2), AF.Exp, [pr], ["PmT", "vg"], scale=0.125)
                    po, por = newps()
                    for c in range(2):
                        for r in range(2):
                            h = 2 * c + r
                            for mt in range(2):
                                mm(po[:, h * 65:(h + 1) * 65], PmT[:, r, mt, c, :], vmaug[:, mt, h, :], mt == 0, mt == 1, ["PmT", "vmaug"], [por])
                    pov = po[:, 0:260].rearrange("p (h e) -> p h e", h=4)
                    S.dve(lambda e, pov=pov: e.reciprocal(out=den[:, 0:4], in_=pov[:, :, 64]), reads=[por], writes=["den"])
                    S.dve(lambda e, pov=pov: e.tensor_tensor(out=om[:, :].rearrange("p (h d) -> p h d", h=4), in0=pov[:, :, 0:64],
                                                             in1=den[:, 0:4].unsqueeze(2).to_broadcast([128, 4, 64]), op=ALU.mult),
                          reads=[por, "den"], writes=["om"])
                    pp, pr = newps()
                    for k in range(2):
                        tr(pp[:, k * 128:(k + 1) * 128], om[:, k * 128:(k + 1) * 128], ["om"], [pr])
                    act(omT[:, :, t * 128:(t + 1) * 128], pp[:, 0:256].rearrange("p (a b) -> p a b", a=2), AF.Copy, [pr], ["omT", "vg"])
                for t in range(4):
                    for n in range(2):
                        pp, pr = newps()
                        for c in range(2):
                            mm(pp[:, :], omT[:, c, t * 128:(t + 1) * 128], wmo[:, c, n * 512:(n + 1) * 512], c == 0, c == 1, ["omT", "wmo"], [pr])
                        S.dve(lambda e, pp=pp, t=t, n=n: e.tensor_tensor(
                            out=xg[:, t, n * 512:(n + 1) * 512], in0=pp[:, :], in1=xg[:, t, n * 512:(n + 1) * 512], op=ALU.add),
                            reads=[pr, "xg%d" % t], writes=["xg%d" % t])

                for t in range(4):
                    dump("x2_%d_%d" % (grp, t), xg[:, t, :], 1024, ["xg%d" % t])
                chk("xattn")
                for t in range(4):
                    norm_T(xg[:, t, :], "xg%d" % t, wn3, "wn", hT[:, :, t * 128:(t + 1) * 128], "hT%d" % t)
                for i in range(6):
                    ncol = 512 if i < 5 else 256
                    wg_, wgr = wload("w_ffn_gate", 0, 8, i * 512, ncol)
                    wu_, wur = wload("w_ffn_up", 0, 8, i * 512, ncol)
                    for c in range(ncol // 128):
                        fc = i * 4 + c
                        slt, slr = sl[fc % 2], "sl%d" % (fc % 2)
                        fm_proj(wg_, wgr, c * 128, 128, hT, HT_ALL, 512,
                                lambda pp, pr, slt=slt, slr=slr: act(slt[:], pp[:, :], AF.Silu, [pr], [slr]))
                        fm_proj(wu_, wur, c * 128, 128, hT, HT_ALL, 512,
                                lambda pp, pr, slt=slt, slr=slr, fc=fc: S.dve(lambda e: e.tensor_tensor(
                                    out=actT[:, fc, :], in0=pp[:, :], in1=slt[:], op=ALU.mult), reads=[pr, slr], writes=["big"]))
                for n in range(2):
                    pst = [newps() for _ in range(4)]
                    for piece, (k0, nk) in enumerate(((0, 8), (8, 8), (16, 6))):
                        wt, wr = wload("w_ffn_down", k0, nk, n * 512, 512)
                        for t in range(4):
                            pp, pr = pst[t]
                            for k in range(nk):
                                kk = k0 + k
                                mm(pp[:, :], actT[:, kk, t * 128:(t + 1) * 128], wt[:, k, :], kk == 0, kk == 21, ["big", wr], [pr])
                    for t in range(4):
                        pp, pr = pst[t]
                        S.dve(lambda e, pp=pp, t=t, n=n: e.tensor_tensor(
                            out=xg[:, t, n * 512:(n + 1) * 512], in0=pp[:, :], in1=xg[:, t, n * 512:(n + 1) * 512], op=ALU.add),
                            reads=[pr, "xg%d" % t], writes=["xg%d" % t])

                chk("ffn")
                for t in range(4):
                    act(junk[:], xg[:, t, :], AF.Square, ["xg%d" % t], ["junk", "ss"], accum_out=ss[:, 0:1])
                    rstd_from_ss(0, 1.0 / D)
                    S.dve(lambda e, t=t: e.scalar_tensor_tensor(out=yst[:], in0=xg[:, t, :], scalar=rs[:, 0:1], in1=wfin[:],
                                                                op0=ALU.mult, op1=ALU.mult),
                          reads=["xg%d" % t, "rs", "wfin"], writes=["xs"])
                    row0 = (grp * 4 + t) * 128
                    ld("sp", yout[row0:row0 + 128, :], yst[:], "yout", [], reads=["xs"])

        except _Stop:
            pass
        S.emit(final_wait_keys=[k for k in ["yout"] + dbg_keys if k in S.dma_count])
    return nc


_CONST_CACHE = {}


def _consts():
    if not _CONST_CACHE:
        ident = np.eye(128, dtype=np.float32)
        a = np.arange(128)
        u64 = ((a[:, None] // 64 == a[None, :] // 64) & (a[:, None] > a[None, :])).astype(np.float32) * (-1.0 / 16)
        u128 = (a[:, None] > a[None, :]).astype(np.float32) * (-1.0 / 16)
        ind2 = np.zeros((128, 2), np.float32)
        ind2[:64, 0] = -1.0 / 16
        ind2[64:, 1] = -1.0 / 16
        ind1 = np.full((128, 2), -1.0 / 16, np.float32)
        _CONST_CACHE.update(ident=ident, ucum64=u64, ucum128=u128, ind2=ind2, ind1=ind1)
    return _CONST_CACHE


def make_in_maps(inputs, ncore=NCORE, tok_core=TOK_CORE, npre_tiles=48):
    x = np.asarray(inputs["x"], dtype=np.float32)
    memv = np.asarray(inputs["mem"], dtype=np.float32)
    B, SEQ, _ = x.shape
    per_b = SEQ // tok_core
    c = _consts()
    maps = []
    for core in range(ncore):
        b, j = core // per_b, core % per_b
        npre = npre_tiles * 128
        xa = np.zeros((npre + tok_core, D), np.float32)
        end = (j + 1) * tok_core
        start = max(0, j * tok_core - npre)
        seg = x[b, start:end]
        xa[npre + tok_core - seg.shape[0]:] = seg
        m = dict(xall=xa, mem=np.ascontiguousarray(memv[b]),
                 flagb=np.full((128, 1), 0.0 if j > 0 else -30000.0, np.float32))
        m.update(c)
        for k, v in inputs.items():
            if k in ("x", "mem"):
                continue
            v = np.asarray(v, dtype=np.float32)
            m[k] = np.ascontiguousarray(v[0]) if k != "norm_final_w" else np.ascontiguousarray(v)
        maps.append(m)
    return maps


_NC_CACHE = {}


def kernel(**inputs):
    if "nc" not in _NC_CACHE:
        _NC_CACHE["nc"] = build()
    nc = _NC_CACHE["nc"]
    maps = make_in_maps(inputs)
    res = run_bass_kernel_spmd(nc, maps, core_ids=list(range(NCORE)))
    x = inputs["x"]
    B, SEQ, _ = x.shape
    per_b = SEQ // TOK_CORE
    out = np.empty((B, SEQ, D), np.float32)
    for core in range(NCORE):
        b, j = core // per_b, core % per_b
        out[b, j * TOK_CORE:(j + 1) * TOK_CORE] = res.results[core]["y"]
    return out
```

```python
import contextlib
import numpy as np
import concourse.bass as bass
import concourse.mybir as mybir
from concourse.bass_utils import run_bass_kernel_spmd

F32 = mybir.dt.float32
BF16 = mybir.dt.bfloat16
AF = mybir.ActivationFunctionType
ALU = mybir.AluOpType

D = 1024
NCORE = 8
TOK_CORE = 2048
DFF = 2816
INC = 6416
O_QA, O_KA, O_VA, O_QG, O_KG, O_VG, O_GG, O_AL, O_GT = 0, 1024, 1152, 1280, 1792, 2304, 3328, 4352, 4368


class _Stop(Exception):
    pass


class Sched:
    ENGS = ("pe", "act", "dve", "pool", "sp")

    def __init__(self, nc, same_engine_sync=True):
        self.nc = nc
        self.ops = []
        self.last_write = {}
        self.reads_since = {}
        self.dma_count = {}
        self.same_engine_sync = same_engine_sync

    def _add(self, eng, fn, reads, writes, dma_key=None):
        idx = len(self.ops)
        deps = set()
        for r in reads:
            lw = self.last_write.get(r)
            if lw is not None:
                deps.add(lw)
        for w in writes:
            lw = self.last_write.get(w)
            if lw is not None:
                deps.add(lw)
            for rd in self.reads_since.get(w, ()):
                deps.add(rd)
        for r in reads:
            self.reads_since.setdefault(r, []).append(idx)
        for w in writes:
            self.last_write[w] = idx
            self.reads_since[w] = []
        deps.discard(idx)
        op = dict(eng=eng, fn=fn, deps=deps, dma_key=dma_key, signal=False, idx=idx)
        if dma_key is not None:
            c = self.dma_count.get(dma_key, 0) + 16
            self.dma_count[dma_key] = c
            op["dma_val"] = c
        self.ops.append(op)
        return idx

    def pe(self, fn, reads=(), writes=()):
        return self._add("pe", fn, reads, writes)

    def act(self, fn, reads=(), writes=()):
        return self._add("act", fn, reads, writes)

    def dve(self, fn, reads=(), writes=()):
        return self._add("dve", fn, reads, writes)

    def pool(self, fn, reads=(), writes=()):
        return self._add("pool", fn, reads, writes)

    def dma(self, eng, fn, key, reads=(), writes=()):
        return self._add(eng, fn, reads, writes, dma_key=key)

    def emit(self, final_wait_keys=()):
        nc = self.nc
        ops = self.ops
        need = []
        for op in ops:
            nd = []
            for d in sorted(op["deps"]):
                y = ops[d]
                if y["dma_key"] is None and y["eng"] == op["eng"] and op["dma_key"] is None:
                    if op["eng"] == "pe" or not self.same_engine_sync:
                        continue
                nd.append(d)
                if y["dma_key"] is None:
                    y["signal"] = True
            need.append(nd)
        cnt = {e: 0 for e in self.ENGS}
        for op in ops:
            if op["dma_key"] is None and op["signal"]:
                cnt[op["eng"]] += 1
                op["sig_val"] = cnt[op["eng"]]
        with contextlib.ExitStack() as st:
            esem = {e: st.enter_context(nc.semaphore("s_" + e)) for e in self.ENGS}
            dsem = {k: st.enter_context(nc.semaphore("d_%d" % i)) for i, k in enumerate(self.dma_count)}
            block = st.enter_context(nc.Block())
            per = {e: [] for e in self.ENGS}
            for op, nd in zip(ops, need):
                per[op["eng"]].append((op, nd))

            def body(ename, handle):
                waited = {}
                for op, nd in per[ename]:
                    for d in nd:
                        y = ops[d]
                        if y["dma_key"] is not None:
                            sem, val, k = dsem[y["dma_key"]], y["dma_val"], ("d", y["dma_key"])
                        else:
                            sem, val, k = esem[y["eng"]], y["sig_val"], ("e", y["eng"])
                        if waited.get(k, 0) >= val:
                            continue
                        waited[k] = val
                        handle.wait_ge(sem, val)
                    ins = op["fn"](handle)
                    if op["dma_key"] is not None:
                        ins.then_inc(dsem[op["dma_key"]], 16)
                    elif op["signal"]:
                        ins.then_inc(esem[ename], 1)
                if ename == "sp":
                    for k in final_wait_keys:
                        handle.wait_ge(dsem[k], self.dma_count[k])

            @block.tensor
            def _(e):
                body("pe", e)

            @block.scalar
            def _(e):
                body("act", e)

            @block.vector
            def _(e):
                body("dve", e)

            @block.gpsimd
            def _(e):
                body("pool", e)

            @block.sync
            def _(e):
                body("sp", e)


def build(NPRE=48, NGRP=4, NSLOT=3, same_engine_sync=True, dbg=False, stop_after=None):
    nc = bass.Bass("TRN2", target_bir_lowering=False)
    NT_ALL = NPRE + NGRP * 4

    def din(name, shape):
        return nc.dram_tensor(name, list(shape), F32, kind="ExternalInput").ap()

    xall = din("xall", [NT_ALL * 128, D])
    mem = din("mem", [256, D])
    flagb = din("flagb", [128, 1])
    ident_d = din("ident", [128, 128])
    ucum64_d = din("ucum64", [128, 128])
    ucum128_d = din("ucum128", [128, 128])
    ind2_d = din("ind2", [128, 2])
    ind1_d = din("ind1", [128, 2])
    norm_mix_w = din("norm_mix_w", [D])
    w_in = din("w_in", [D, INC])
    b_gate = din("b_gate", [2 * D])
    attn_sinks = din("attn_sinks", [16])
    gla_gate_w2 = din("gla_gate_w2", [16, 512])
    gla_gate_b = din("gla_gate_b", [512])
    gla_norm_w = din("gla_norm_w", [256])
    w_attn_o = din("w_attn_o", [D, D])
    w_gla_o = din("w_gla_o", [D, D])
    w_mix_o = din("w_mix_o", [D, D])
    norm_mem_q_w = din("norm_mem_q_w", [D])
    norm_mem_kv_w = din("norm_mem_kv_w", [D])
    w_mem_q = din("w_mem_q", [D, 256])
    w_mem_kv = din("w_mem_kv", [D, 512])
    w_mem_o = din("w_mem_o", [256, D])
    norm_ffn_w = din("norm_ffn_w", [D])
    w_ffn_gate = din("w_ffn_gate", [D, DFF])
    w_ffn_up = din("w_ffn_up", [D, DFF])
    w_ffn_down = din("w_ffn_down", [DFF, D])
    norm_final_w = din("norm_final_w", [D])
    yout = nc.dram_tensor("y", [NGRP * 512, D], F32, kind="ExternalOutput").ap()
    WSRC = {"w_in": w_in, "w_attn_o": w_attn_o, "w_gla_o": w_gla_o, "w_mix_o": w_mix_o,
            "w_ffn_gate": w_ffn_gate, "w_ffn_up": w_ffn_up, "w_ffn_down": w_ffn_down, "w_mem_kv": w_mem_kv}
    WB = {}
    for nm_ in ("w_in", "w_attn_o", "w_gla_o", "w_mix_o", "w_ffn_gate", "w_ffn_up", "w_ffn_down"):
        WB[nm_] = nc.dram_tensor("wb_" + nm_, list(WSRC[nm_].shape), BF16, kind="Internal").ap()

    S = Sched(nc, same_engine_sync=same_engine_sync)
    st = contextlib.ExitStack()
    with st:
        def sb(name, shape, dt=F32):
            return st.enter_context(nc.sbuf_tensor("sb_" + name, list(shape), dt))

        psb = [st.enter_context(nc.psum_tensor("ps%d" % i, [128, 512], F32)) for i in range(8)]
        pcnt = {"any": 0, "r0": 0, "r1": 0}
        PSETS = {"any": [0, 1, 2, 3, 4, 5, 6, 7], "r0": [0, 1, 4, 5], "r1": [2, 3, 6, 7]}

        def newps(kind="any"):
            lst = PSETS[kind]
            i = lst[pcnt[kind] % len(lst)]
            pcnt[kind] += 1
            return psb[i], "ps%d" % i

        idt = sb("idt", [128, 128])
        uc64 = sb("uc64", [128, 128])
        uc128 = sb("uc128", [128, 128])
        ind2 = sb("ind2", [128, 2])
        ind1 = sb("ind1", [128, 2])
        flg = sb("flg", [128, 1])
        vst = [sb("vst%d" % i, [16, 128]) for i in range(2)]
        wn1 = sb("wn1", [128, 8])
        wn2 = sb("wn2", [128, 8])
        wn3 = sb("wn3", [128, 8])
        wnkv = sb("wnkv", [128, 8])
        gnw8 = sb("gnw8", [128, 8])
        bgate = sb("bgate", [128, 16])
        exps = sb("exps", [128, 16])
        wfin = sb("wfin", [128, D])
        w2aug = sb("w2aug", [32, 512], BF16)
        wkdup = sb("wkdup", [128, 8, 2, 128], BF16)
        wva = sb("wva", [128, 8, 128], BF16)
        walr = sb("walr", [128, 8, 16], BF16)
        wmq = sb("wmq", [128, 8, 256], BF16)
        wmo = sb("wmo", [128, 2, D], BF16)
        big = sb("big", [128, 8 * 1536], BF16)
        wpre = big[:, :].rearrange("p (k c) -> p k c", k=8)
        actT = big[:, 0:22 * 512].rearrange("p (f t) -> p f t", f=22)
        kmT = sb("kmT", [128, 2, 256], BF16)
        vmaug = sb("vmaug", [128, 2, 4, 65], BF16)
        wsl = [sb("wsl%d" % i, [128, 8, 512], BF16) for i in range(NSLOT)]
        xg = sb("xg", [128, 4, D])
        xs = sb("xs", [128, D])
        junk = sb("junk", [128, D], BF16)
        ss = sb("ss", [128, 8])
        rs = sb("rs", [128, 8])
        hT = sb("hT", [128, 8, 512], BF16)
        qT = sb("qT", [128, 8, 512], BF16)
        kT = sb("kT", [128, 2, 640], BF16)
        vaug = sb("vaug", [128, 5, 2, 65], BF16)
        PT = [sb("PT%d" % i, [128, 2, 4, 128], BF16) for i in range(2)]
        oa = sb("oa", [128, D])
        oaT = sb("oaT", [128, 8, 512], BF16)
        den = sb("den", [128, 16])
        qgT = sb("qgT", [128, 4, 512], BF16)
        alrT = sb("alrT", [32, 512], BF16)
        kg = sb("kg", [128, 4, 512])
        vg = sb("vg", [128, 4, D], BF16)
        sg = sb("sg", [128, 4, D], BF16)
        Gf = sb("Gf", [128, 512])
        Dm = sb("Dm", [128, 512])
        av = sb("av", [128, 8])
        kdec = sb("kdec", [128, 512], BF16)
        Sst = sb("Sst", [128, D])
        Sbf = [sb("Sbf%d" % i, [128, D], BF16) for i in range(2)]
        og = sb("og", [128, D])
        ogT = sb("ogT", [128, 8, 512], BF16)
        gT = sb("gT", [128, 16, 512], BF16)
        m1 = sg[:, :, :].rearrange("p a (b c) -> p (a b) c", b=2)
        mtmp = Gf
        mT = qT
        qmT = vg[:, 0, :].rearrange("p (a b) -> p a b", a=2)
        PmT = vg[:, 1, :].rearrange("p (r m c q) -> p r m c q", r=2, m=2, c=2)
        omT = vg[:, 2, :].rearrange("p (a b) -> p a b", a=2)
        om = sb("om", [128, 256])
        sl = [sb("sl%d" % i, [128, 512], BF16) for i in range(2)]
        yst = xs
        xm = oa
        xs2 = sb("xs2", [128, D])
        av2 = sb("av2", [128, 8])
        XS = [(xs, "xs0"), (xs2, "xs1")]
        xm1 = gT[:, 0:4, :].rearrange("p a b -> p (a b)").bitcast(F32)
        Gf1 = gT[:, 4:6, :].rearrange("p a b -> p (a b)").bitcast(F32)
        Dm1 = gT[:, 6:8, :].rearrange("p a b -> p (a b)").bitcast(F32)
        kdec1 = gT[:, 8, :]
        PRE_RES = ["pxm1", "pGf1", "pDm1", "pkdec1"]
        GB = [dict(Gf=Gf[:], Dm=Dm[:], av=av, kdec=kdec[:], rGf="Gf", rDm="Dm", rav="av", rkdec="kdec"),
              dict(Gf=Gf1, Dm=Dm1, av=av2, kdec=kdec1, rGf="pGf1", rDm="pDm1", rav="av2", rkdec="pkdec1")]

        HT_ALL = ["hT0", "hT1", "hT2", "hT3"]

        def chk(name):
            if stop_after == name:
                raise _Stop()

        dbg_keys = []

        def dump(name, ap, ncols, reads):
            if not dbg:
                return
            d_ = nc.dram_tensor("dbg_" + name, [128, ncols], F32, kind="ExternalOutput").ap()
            k_ = "dbg_" + name
            dbg_keys.append(k_)
            S.dma("sp", lambda e: e.dma_start(out=d_, in_=ap), k_, reads=reads)

        ukey = [0]

        def ld(eng, out_ap, in_ap, key, writes, reads=(), slow=False):
            if key in ("c", "cw"):
                ukey[0] += 1
                if key == "cw":
                    reads = list(reads) + ["cwtok%d" % (ukey[0] % 2)]
                    writes = list(writes) + ["cwtok%d" % (ukey[0] % 2)]
                key = "%s%d" % (key, ukey[0])
            if slow:
                S.dma(eng, lambda e: e.dma_start(out=out_ap, in_=in_ap, allow_slow_non_contiguous=True), key, reads=reads, writes=writes)
            else:
                S.dma(eng, lambda e: e.dma_start(out=out_ap, in_=in_ap), key, reads=reads, writes=writes)

        slot_i = [0]

        def wload(wname, k0, nk, c0, ncols):
            i = slot_i[0] % NSLOT
            slot_i[0] += 1
            W = WB.get(wname, WSRC[wname])
            src = W[k0 * 128:(k0 + nk) * 128, c0:c0 + ncols].rearrange("(k p) c -> p k c", p=128)
            ld("pool", wsl[i][:, 0:nk, 0:ncols], src, "wsl%d" % i, writes=["wsl%d" % i], reads=["wb_" + wname])
            return wsl[i], "wsl%d" % i

        def mm(out_ap, lhsT, rhs, start, stop, reads, writes):
            S.pe(lambda e: e.matmul(out=out_ap, lhsT=lhsT, rhs=rhs, start=start, stop=stop), reads=reads, writes=writes)

        def tr(out_ap, in_ap, reads, writes):
            S.pe(lambda e: e.transpose(out=out_ap, in_=in_ap, identity=idt[:]), reads=list(reads) + ["idt"], writes=writes)

        def act(out_ap, in_ap, func, reads, writes, **kw):
            S.act(lambda e: e.activation(out=out_ap, in_=in_ap, func=func, **kw), reads=reads, writes=writes)

        def rstd_from_ss(col, n_inv, ncol=1, sres="ss", rres="rs"):
            act(rs[:, col:col + ncol], ss[:, col:col + ncol], AF.Ln, [sres], [rres], scale=n_inv, bias=1e-6)
            act(rs[:, col:col + ncol], rs[:, col:col + ncol], AF.Exp, [rres], [rres], scale=-0.5)

        nbuf = [0]

        def norm_T(x_ap, xres, wn, wnres, dst, dstres):
            b_ = nbuf[0] % 2
            nbuf[0] += 1
            xs_, xsr = XS[b_]
            sres, rres = "ssn%d" % b_, "rsn%d" % b_
            act(junk[:], x_ap, AF.Square, [xres], ["junk", sres], accum_out=ss[:, b_:b_ + 1])
            rstd_from_ss(b_, 1.0 / D, sres=sres, rres=rres)
            S.dve(lambda e: e.tensor_scalar(out=xs_[:], in0=x_ap, scalar1=rs[:, b_:b_ + 1], scalar2=None, op0=ALU.mult),
                  reads=[xres, rres], writes=[xsr])
            for half in range(2):
                pp, pr = newps()
                for k in range(4):
                    kk = half * 4 + k
                    tr(pp[:, k * 128:(k + 1) * 128], xs_[:, kk * 128:(kk + 1) * 128], [xsr], [pr])
                S.dve(lambda e, pp=pp, half=half: e.tensor_tensor(
                    out=dst[:, half * 4:(half + 1) * 4, :], in0=pp[:, :].rearrange("p (a b) -> p a b", a=4),
                    in1=wn[:, half * 4:(half + 1) * 4].unsqueeze(2).to_broadcast([128, 4, 128]), op=ALU.mult),
                    reads=[pr, wnres], writes=[dstres])

        def tm_proj(lhs_tile, lhs_res, wt, wres, c0, ncols, evac):
            pp, pr = newps()
            for k in range(8):
                mm(pp[:, 0:ncols], lhs_tile[:, k, :], wt[:, k, c0:c0 + ncols], k == 0, k == 7, [lhs_res, wres], [pr])
            evac(pp, pr)

        def fm_proj(wt, wres, c0, m, rhs, rhs_res, ntok, evac, nk=8):
            pp, pr = newps()
            rr = list(rhs_res) if isinstance(rhs_res, (list, tuple)) else [rhs_res]
            for k in range(nk):
                mm(pp[0:m, 0:ntok], wt[:, k, c0:c0 + m], rhs[:, k, 0:ntok], k == 0, k == nk - 1, rr + [wres], [pr])
            evac(pp, pr)

        def gla_gate(tokc0, umat, ures, indt, indres, B=None, alr_ap=None, alr_res="alrT"):
            B = B or GB[0]
            Gf, Dm, av = B["Gf"], B["Dm"], B["av"]
            rGf, rDm, rav = B["rGf"], B["rDm"], B["rav"]
            alr_ap = alr_ap if alr_ap is not None else alrT[0:17, tokc0:tokc0 + 128]
            pz, pzr = newps()
            mm(pz[:, :], alr_ap, w2aug[0:17, :], True, True, [alr_res, "w2aug"], [pzr])
            act(Gf, pz[:, :], AF.Exp, [pzr], [rGf], scale=-1.0)
            act(Gf, Gf, AF.Ln, [rGf], [rGf], bias=1.0)
            pR, pRr = newps()
            mm(pR[:, :], umat[:], Gf, True, True, [ures, rGf], [pRr])
            pa, par = newps()
            for h in range(4):
                mm(pa[:, 2 * h:2 * h + 2], Gf[:, h * 128:(h + 1) * 128], indt[:], True, True, [rGf, indres], [par])
            act(Dm, pR[:, :], AF.Exp, [pRr], [rDm])
            act(av[:], pa[:, 0:8], AF.Exp, [par], [rav])

        def state_update(krows, j, vsrc, vres, kind, B=None):
            B = B or GB[0]
            kdec, av = B["kdec"], B["av"]
            rkdec, rav = B["rkdec"], B["rav"]
            r0, r1 = krows
            pA, pAr = newps(kind)
            pB, pBr = newps(kind)
            for h in range(4):
                pp, pr = (pA, pAr) if h < 2 else (pB, pBr)
                mm(pp[:, (h % 2) * 256:(h % 2 + 1) * 256], kdec[r0:r1, h * 128:(h + 1) * 128],
                   vsrc[r0:r1, h * 256:(h + 1) * 256], True, True, [rkdec, vres], [pr])
            for h in range(4):
                pp, pr = (pA, pAr) if h < 2 else (pB, pBr)
                S.dve(lambda e, h=h, pp=pp: e.scalar_tensor_tensor(
                    out=Sst[:, h * 256:(h + 1) * 256], in0=Sst[:, h * 256:(h + 1) * 256], scalar=av[:, 2 * h + j:2 * h + j + 1],
                    in1=pp[:, (h % 2) * 256:(h % 2 + 1) * 256], op0=ALU.mult, op1=ALU.add),
                    reads=["Sst", rav, pr], writes=["Sst"])

        try:
            ld("sp", idt[:], ident_d, "c", ["idt"])
            ld("sp", uc64[:], ucum64_d, "c", ["uc64"])
            ld("sp", uc128[:], ucum128_d, "c", ["uc128"])
            ld("sp", ind2[:], ind2_d, "c", ["ind2"])
            ld("sp", ind1[:], ind1_d, "c", ["ind1"])
            ld("sp", flg[:], flagb, "c", ["flg"])
            for vi, (t_, src_, kk_, res_) in enumerate(((wn1, norm_mix_w, 8, "wn"), (wn2, norm_mem_q_w, 8, "wn"), (wn3, norm_ffn_w, 8, "wn"),
                                                      (wnkv, norm_mem_kv_w, 8, "wn"), (bgate, b_gate, 16, "bgate"), (gnw8, gla_norm_w, 2, "gnw8"))):
                stg = vst[vi % 2]
                sres = "vst%d" % (vi % 2)
                ld("sp", stg[0:kk_, :], src_.rearrange("(k p) -> k p", p=128), "c", [sres])
                pp, pr = newps()
                S.pe(lambda e, pp=pp, stg=stg, kk_=kk_: e.transpose(out=pp[:, 0:kk_], in_=stg[0:kk_, :], identity=idt[0:kk_, 0:kk_]),
                     reads=[sres, "idt"], writes=[pr])
                if kk_ == 2:
                    for i in range(4):
                        act(t_[:, 2 * i:2 * i + 2], pp[:, 0:2], AF.Copy, [pr], [res_])
                else:
                    act(t_[:, 0:kk_], pp[:, 0:kk_], AF.Copy, [pr], [res_])
            ld("sp", exps[:], attn_sinks.partition_broadcast(128), "c", ["exps"])
            ld("sp", wfin[:], norm_final_w.partition_broadcast(128), "c", ["wfin"])
            CONSTS = ["idt", "uc64", "uc128", "ind2", "ind1", "flg", "wn", "gnw8", "bgate", "exps", "wfin"]
            act(exps[:], exps[:], AF.Exp, ["exps"], ["exps"])
            ld("pool", w2aug[0:16, :], gla_gate_w2, "cw", ["w2aug"])
            ld("pool", w2aug[16:17, :], gla_gate_b.rearrange("(o c) -> o c", o=1), "cw", ["w2aug"])
            for g in range(2):
                for r in range(2):
                    ld("pool", wkdup[:, :, g, 64 * r:64 * r + 64],
                       w_in[:, O_KA + 64 * g:O_KA + 64 * g + 64].rearrange("(k p) c -> p k c", p=128), "cw", ["wkdup"])
            ld("pool", wva[:], w_in[:, O_VA:O_VA + 128].rearrange("(k p) c -> p k c", p=128), "cw", ["wva"])
            ld("pool", walr[:], w_in[:, O_AL:O_AL + 16].rearrange("(k p) c -> p k c", p=128), "cw", ["walr"])
            ld("pool", wmq[:], w_mem_q.rearrange("(k p) c -> p k c", p=128), "cw", ["wmq"])
            ld("pool", wmo[:], w_mem_o.rearrange("(k p) c -> p k c", p=128), "cw", ["wmo"])
            for i in range(3):
                ld("pool", wpre[:, :, i * 512:(i + 1) * 512],
                   w_in[:, O_KG + i * 512:O_KG + (i + 1) * 512].rearrange("(k p) c -> p k c", p=128), "cw", ["big"])
            S.dve(lambda e: e.memset(Sst[:], 0.0), writes=["Sst"])
            S.dve(lambda e: e.memset(alrT[:], 1.0), writes=["alrT"])
            S.dve(lambda e: e.memset(vaug[:], 1.0), writes=["vaug"])
            S.dve(lambda e: e.memset(vmaug[:], 1.0), writes=["vmaug"])
            for i in range(2):
                S.dve(lambda e, i=i: e.memset(PT[i][:], 0.0), writes=["PT%d" % i])

            for mt in range(2):
                ld("sp", xm[:], mem[mt * 128:(mt + 1) * 128, :], "xm", ["oa"])
                norm_T(xm[:], "oa", wnkv, "wn", hT[:, :, mt * 128:(mt + 1) * 128], "hT%d" % mt)
            wt, wr = wload("w_mem_kv", 0, 8, 0, 512)
            for c in range(2):
                fm_proj(wt, wr, c * 128, 128, hT, ["hT0", "hT1"], 256,
                        lambda pp, pr, c=c: act(kmT[:, c, :], pp[:, 0:256], AF.Copy, [pr], ["kmT"]))
            for mt in range(2):
                tm_proj(hT[:, :, mt * 128:(mt + 1) * 128], "hT%d" % mt, wt, wr, 256, 256,
                        lambda pp, pr, mt=mt: act(vmaug[:, mt, :, 0:64], pp[:, 0:256].rearrange("p (h d) -> p h d", h=4), AF.Copy, [pr], ["vmaug"]))

            for nm_ in ("w_in", "w_attn_o", "w_gla_o", "w_mix_o", "w_ffn_gate", "w_ffn_up", "w_ffn_down"):
                src_, dst_ = WSRC[nm_], WB[nm_]
                for rb in range(src_.shape[0] // 128):
                    S.dma("pool", lambda e, src_=src_, dst_=dst_, rb=rb: e.dma_start(
                        out=dst_[rb * 128:(rb + 1) * 128, :], in_=src_[rb * 128:(rb + 1) * 128, :], max_dma_last_dim=4096),
                        "cast", reads=["casttok"], writes=["casttok", "wb_" + nm_])
            chk("setup")
            def pre_A(t):
                b = t % 2
                xmb, xmr = ((xm[:], "oa"), (xm1, "pxm1"))[b]
                ld("sp", xmb, xall[t * 128:(t + 1) * 128, :], "xm%d" % b, [xmr])
                norm_T(xmb, xmr, wn1, "wn", hT[:, :, (t % 4) * 128:(t % 4 + 1) * 128], "hT%d" % (t % 4))

            def pre_B(t):
                b = t % 2
                kgr, vgr, alr = (("kg", "vg", "alrT"), ("kgB", "vgB", "alrTB"))[b]
                B = GB[b]
                hTt = hT[:, :, (t % 4) * 128:(t % 4 + 1) * 128]
                hres = "hT%d" % (t % 4)
                tm_proj(hTt, hres, wpre, "big", 0, 512,
                        lambda pp, pr, b=b, kgr=kgr: act(kg[:, b, :], pp[:, :], AF.Copy, [pr], [kgr]))
                for i in range(2):
                    tm_proj(hTt, hres, wpre, "big", 512 + i * 512, 512,
                            lambda pp, pr, i=i, b=b, vgr=vgr: act(vg[:, b, i * 512:(i + 1) * 512], pp[:, :], AF.Copy, [pr], [vgr]))
                fm_proj(walr, "walr", 0, 16, hTt, hres, 128,
                        lambda pp, pr, b=b, alr=alr: act(alrT[0:16, b * 128:(b + 1) * 128], pp[0:16, 0:128], AF.Copy, [pr], [alr]))
                gla_gate(0, uc128, "uc128", ind1, "ind1", B=B, alr_ap=alrT[0:17, b * 128:(b + 1) * 128], alr_res=alr)
                S.dve(lambda e, b=b, B=B: e.tensor_tensor(out=B["kdec"], in0=kg[:, b, :], in1=B["Dm"], op=ALU.mult),
                      reads=[kgr, B["rDm"]], writes=[B["rkdec"]])
                state_update((0, 128), 0, vg[:, b, :], vgr, "any", B=B)
                if t == NPRE - 1:
                    for g in range(2):
                        fm_proj(wkdup[:, :, g, :], "wkdup", 0, 128, hTt, hres, 128,
                                lambda pp, pr, g=g: act(kT[:, g, 0:128], pp[:, 0:128], AF.Copy, [pr], ["kT"]))
                    tm_proj(hTt, hres, wva, "wva", 0, 128,
                            lambda pp, pr: act(vaug[:, 0, :, 0:64], pp[:, 0:128].rearrange("p (g d) -> p g d", g=2), AF.Copy, [pr], ["vaug"]))

            for t in range(min(2, NPRE)):
                pre_A(t)
            for t in range(NPRE):
                if t + 2 < NPRE:
                    pre_A(t + 2)
                pre_B(t)

            chk("prefix")
            for grp in range(NGRP):
                X0 = (lambda *names: list(names)) if grp == 0 else (lambda *names: [])
                for t in range(4):
                    row0 = (NPRE + grp * 4 + t) * 128
                    ld("sp", xg[:, t, :], xall[row0:row0 + 128, :], "xg%d" % t, ["xg%d" % t])
                    norm_T(xg[:, t, :], "xg%d" % t, wn1, "wn", hT[:, :, t * 128:(t + 1) * 128], "hT%d" % t)
                chk("norm1")
                for i in range(2):
                    wt, wr = wload("w_in", 0, 8, O_QA + i * 512, 512)
                    for c in range(4):
                        fm_proj(wt, wr, c * 128, 128, hT, HT_ALL, 512,
                                lambda pp, pr, cc=i * 4 + c: act(qT[:, cc, :], pp[:, :], AF.Copy, [pr], ["qT"]))
                for g in range(2):
                    fm_proj(wkdup[:, :, g, :], "wkdup", 0, 128, hT, HT_ALL, 512,
                            lambda pp, pr, g=g: act(kT[:, g, 128:640], pp[:, :], AF.Copy, [pr], ["kT"]))
                for t in range(4):
                    tm_proj(hT[:, :, t * 128:(t + 1) * 128], "hT%d" % t, wva, "wva", 0, 128,
                            lambda pp, pr, t=t: act(vaug[:, 1 + t, :, 0:64], pp[:, 0:128].rearrange("p (g d) -> p g d", g=2), AF.Copy, [pr], ["vaug"]))
                wt, wr = wload("w_in", 0, 8, O_QG, 512)
                for c in range(4):
                    fm_proj(wt, wr, c * 128, 128, hT, HT_ALL, 512,
                            lambda pp, pr, c=c: act(qgT[:, c, :], pp[:, :], AF.Copy, [pr], ["qgT"], scale=float(128 ** -0.5)))
                fm_proj(walr, "walr", 0, 16, hT, HT_ALL, 512,
                        lambda pp, pr: act(alrT[0:16, :], pp[0:16, :], AF.Copy, [pr], ["alrT"] + X0("alrTB")))
                wt, wr = wload("w_in", 0, 8, O_KG, 512)
                for t in range(4):
                    tm_proj(hT[:, :, t * 128:(t + 1) * 128], "hT%d" % t, wt, wr, 0, 512,
                            lambda pp, pr, t=t: act(kg[:, t, :], pp[:, :], AF.Copy, [pr], ["kg"] + X0("kgB")))
                for i in range(2):
                    wt, wr = wload("w_in", 0, 8, O_VG + i * 512, 512)
                    for t in range(4):
                        tm_proj(hT[:, :, t * 128:(t + 1) * 128], "hT%d" % t, wt, wr, 0, 512,
                                lambda pp, pr, t=t, i=i: act(vg[:, t, i * 512:(i + 1) * 512], pp[:, :], AF.Copy, [pr], ["vg", "qmT", "PmT", "omT"] + X0("vgB")))
                for i in range(2):
                    wt, wr = wload("w_in", 0, 8, O_GG + i * 512, 512)
                    for t in range(4):
                        tm_proj(hT[:, :, t * 128:(t + 1) * 128], "hT%d" % t, wt, wr, 0, 512,
                                lambda pp, pr, t=t, i=i: act(sg[:, t, i * 512:(i + 1) * 512], pp[:, :], AF.Silu, [pr], ["sg"]))
                for i in range(4):
                    wt, wr = wload("w_in", 0, 8, O_GT + i * 512, 512)
                    for c in range(4):
                        cc = i * 4 + c
                        fm_proj(wt, wr, c * 128, 128, hT, HT_ALL, 512,
                                lambda pp, pr, cc=cc: act(gT[:, cc, :], pp[:, :], AF.Sigmoid, [pr, "bgate"], ["gT"] + X0(*PRE_RES), bias=bgate[:, cc:cc + 1]))

                chk("proj")
                for t in range(4):
                    first = (grp == 0 and t == 0)
                    pO = [(psb[4 + i_], "ps%d" % (4 + i_)) for i_ in range(4)]
                    for g in range(2):
                        for r in range(2):
                            s_ = 2 * g + r
                            PTs, PTr = PT[s_ % 2], "PT%d" % (s_ % 2)
                            pA, pAr = psb[2 * r], "ps%d" % (2 * r)
                            pB, pBr = psb[2 * r + 1], "ps%d" % (2 * r + 1)
                            for j in range(4):
                                i = 4 * g + j
                                q_ap = qT[64 * r:64 * r + 64, i, t * 128:(t + 1) * 128]
                                mm(pA[:, j * 128:(j + 1) * 128], kT[64 * r:64 * r + 64, g, t * 128:(t + 1) * 128], q_ap, True, True, ["kT", "qT"], [pAr])
                                mm(pB[:, j * 128:(j + 1) * 128], kT[64 * r:64 * r + 64, g, (t + 1) * 128:(t + 2) * 128], q_ap, True, True, ["kT", "qT"], [pBr])
                            pAv = pA[:, :].rearrange("p (j q) -> p j q", j=4)
                            pBv = pB[:, :].rearrange("p (j q) -> p j q", j=4)
                            kw = dict(scale=0.125)
                            kwp = dict(scale=0.125, bias=flg[0:64, 0:1]) if first else kw
                            kwp2 = dict(scale=0.125, bias=flg[64:128, 0:1]) if first else kw
                            act(PTs[0:64, 0, :, 0:64], pAv[0:64, :, 0:64], AF.Exp, [pAr, "flg"], [PTr], **kwp)
                            act(PTs[64:128, 0, :, :], pAv[64:128, :, :], AF.Exp, [pAr, "flg"], [PTr], **kwp2)
                            act(PTs[0:64, 1, :, :], pBv[0:64, :, :], AF.Exp, [pBr], [PTr], **kw)
                            act(PTs[64:128, 1, :, 64:128], pBv[64:128, :, 64:128], AF.Exp, [pBr], [PTr], **kw)
                            po, por = pO[s_]
                            for j in range(4):
                                col = j * 65
                                mm(po[:, col:col + 65], PTs[:, 0, j, :], vaug[:, t, g, :], True, False, [PTr, "vaug"], [por])
                                mm(po[:, col:col + 65], PTs[:, 1, j, :], vaug[:, t + 1, g, :], False, True, [PTr, "vaug"], [por])
                    for g in range(2):
                        for r in range(2):
                            s_ = 2 * g + r
                            po, por = pO[s_]
                            pov = po[:, 0:260].rearrange("p (j e) -> p j e", j=4)
                            dv = den[:, s_ * 4:(s_ + 1) * 4]
                            ev = exps[:, g * 8:(g + 1) * 8].rearrange("p (j r) -> p r j", r=2)[:, r, :]
                            ov = oa[:, g * 512:(g + 1) * 512].rearrange("p (j r d) -> p r j d", j=4, r=2)[:, r, :, :]
                            S.dve(lambda e, pov=pov, dv=dv, ev=ev: e.tensor_tensor(out=dv, in0=pov[:, :, 64], in1=ev, op=ALU.add),
                                  reads=[por, "exps"], writes=["den"])
                            S.dve(lambda e, dv=dv: e.reciprocal(out=dv, in_=dv), reads=["den"], writes=["den"])
                            S.dve(lambda e, pov=pov, dv=dv, ov=ov: e.tensor_tensor(
                                out=ov, in0=pov[:, :, 0:64], in1=dv.unsqueeze(2).to_broadcast([128, 4, 64]), op=ALU.mult),
                                reads=[por, "den"], writes=["oa"])
                    dump("oa_%d_%d" % (grp, t), oa[:], 1024, ["oa"])
                    for half in range(2):
                        pp, pr = newps()
                        for k in range(4):
                            kk = half * 4 + k
                            tr(pp[:, k * 128:(k + 1) * 128], oa[:, kk * 128:(kk + 1) * 128], ["oa"], [pr])
                        act(oaT[:, half * 4:(half + 1) * 4, t * 128:(t + 1) * 128], pp[:, :].rearrange("p (a b) -> p a b", a=4), AF.Copy, [pr], ["oaT"])
                S.dve(lambda e: e.tensor_copy(out=kT[:, :, 0:128], in_=kT[:, :, 512:640]), reads=["kT"], writes=["kT"])
                S.dve(lambda e: e.tensor_copy(out=vaug[:, 0, :, :], in_=vaug[:, 4, :, :]), reads=["vaug"], writes=["vaug"])

                chk("swa")
                for t in range(4):
                    gla_gate(t * 128, uc64, "uc64", ind2, "ind2")
                    S.dve(lambda e, t=t: e.tensor_tensor(out=kdec[:], in0=kg[:, t, :], in1=Dm[:], op=ALU.mult), reads=["kg", "Dm"], writes=["kdec"])
                    for j in range(2):
                        state_update((64 * j, 64 * j + 64), j, vg[:, t, :], "vg", "r0" if j == 0 else "r1")
                        act(Sbf[j][:], Sst[:], AF.Copy, ["Sst"], ["Sbf%d" % j])
                        pA, pAr = newps()
                        pB, pBr = newps()
                        for h in range(4):
                            pp, pr = (pA, pAr) if h < 2 else (pB, pBr)
                            mm(pp[:, (h % 2) * 256:(h % 2 + 1) * 256], qgT[:, h, t * 128:(t + 1) * 128],
                               Sbf[j][:, h * 256:(h + 1) * 256], True, True, ["qgT", "Sbf%d" % j], [pr])
                        act(og[64 * j:64 * j + 64, 0:512], pA[64 * j:64 * j + 64, :], AF.Copy, [pAr], ["og"])
                        act(og[64 * j:64 * j + 64, 512:1024], pB[64 * j:64 * j + 64, :], AF.Copy, [pBr], ["og"])
                    for h in range(4):
                        act(junk[:, 0:256], og[:, h * 256:(h + 1) * 256], AF.Square, ["og"], ["junk", "ss"], accum_out=ss[:, 4 + h:5 + h])
                    rstd_from_ss(4, 1.0 / 256, ncol=4)
                    for h in range(4):
                        S.dve(lambda e, h=h, t=t: e.scalar_tensor_tensor(
                            out=og[:, h * 256:(h + 1) * 256], in0=og[:, h * 256:(h + 1) * 256], scalar=rs[:, 4 + h:5 + h],
                            in1=sg[:, t, h * 256:(h + 1) * 256], op0=ALU.mult, op1=ALU.mult),
                            reads=["og", "rs", "sg"], writes=["og"])
                    dump("og_%d_%d" % (grp, t), og[:], 1024, ["og"])
                    for half in range(2):
                        pp, pr = newps()
                        for k in range(4):
                            kk = half * 4 + k
                            tr(pp[:, k * 128:(k + 1) * 128], og[:, kk * 128:(kk + 1) * 128], ["og"], [pr])
                        S.dve(lambda e, pp=pp, half=half, t=t: e.tensor_tensor(
                            out=ogT[:, half * 4:(half + 1) * 4, t * 128:(t + 1) * 128], in0=pp[:, :].rearrange("p (a b) -> p a b", a=4),
                            in1=gnw8[:, half * 4:(half + 1) * 4].unsqueeze(2).to_broadcast([128, 4, 128]), op=ALU.mult),
                            reads=[pr, "gnw8"], writes=["ogT"])

                chk("gla")
                for i in range(2):
                    wt, wr = wload("w_attn_o", 0, 8, i * 512, 512)
                    for c in range(4):
                        cc = i * 4 + c
                        fm_proj(wt, wr, c * 128, 128, oaT, "oaT", 512,
                                lambda pp, pr, cc=cc: S.dve(lambda e: e.tensor_tensor(out=m1[:, cc, :], in0=pp[:, :], in1=gT[:, cc, :], op=ALU.mult),
                                                            reads=[pr, "gT"], writes=["sg"]))
                for i in range(2):
                    wt, wr = wload("w_gla_o", 0, 8, i * 512, 512)
                    for c in range(4):
                        cc = i * 4 + c

                        def ev(pp, pr, cc=cc):
                            S.dve(lambda e: e.tensor_tensor(out=mtmp[:], in0=pp[:, :], in1=gT[:, 8 + cc, :], op=ALU.mult),
                                  reads=[pr, "gT"], writes=["Gf"])
                            S.dve(lambda e: e.tensor_tensor(out=mT[:, cc, :], in0=mtmp[:], in1=m1[:, cc, :], op=ALU.add),
                                  reads=["Gf", "sg"], writes=["qT"])
                        fm_proj(wt, wr, c * 128, 128, ogT, "ogT", 512, ev)
                for n in range(2):
                    wt, wr = wload("w_mix_o", 0, 8, n * 512, 512)
                    for t in range(4):
                        tm_proj(mT[:, :, t * 128:(t + 1) * 128], "qT", wt, wr, 0, 512,
                                lambda pp, pr, t=t, n=n: S.dve(lambda e: e.tensor_tensor(
                                    out=xg[:, t, n * 512:(n + 1) * 512], in0=pp[:, :], in1=xg[:, t, n * 512:(n + 1) * 512], op=ALU.add),
                                    reads=[pr, "xg%d" % t], writes=["xg%d" % t]))

                for t in range(4):
                    dump("x1_%d_%d" % (grp, t), xg[:, t, :], 1024, ["xg%d" % t])
                chk("merge")
                for t in range(4):
                    norm_T(xg[:, t, :], "xg%d" % t, wn2, "wn", hT[:, :, t * 128:(t + 1) * 128], "hT%d" % t)
                for c in range(2):
                    fm_proj(wmq, "wmq", c * 128, 128, hT, HT_ALL, 512,
                            lambda pp, pr, c=c: act(qmT[:, c, :], pp[:, :], AF.Copy, [pr], ["qmT", "vg"]))
                for t in range(4):
                    for r in range(2):
                        pp, pr = newps("r0" if r == 0 else "r1")
                        for mt in range(2):
                            for c in range(2):
                                col = (mt * 2 + c) * 128
                                mm(pp[:, col:col + 128], kmT[64 * r:64 * r + 64, c, mt * 128:(mt + 1) * 128],
                                   qmT[64 * r:64 * r + 64, c, t * 128:(t + 1) * 128], True, True, ["kmT", "qmT"], [pr])
                        act(PmT[:, r, :, :, :], pp[:, :].rearrange("p (m c q) -> p m c q", m=2, c=2), AF.Exp, [pr], ["PmT", "vg"], scale=0.125)
                    po, por = newps()
                    for c in range(2):
                        for r in range(2):
                            h = 2 * c + r
                            for mt in range(2):
                                mm(po[:, h * 65:(h + 1) * 65], PmT[:, r, mt, c, :], vmaug[:, mt, h, :], mt == 0, mt == 1, ["PmT", "vmaug"], [por])
                    pov = po[:, 0:260].rearrange("p (h e) -> p h e", h=4)
                    S.dve(lambda e, pov=pov: e.reciprocal(out=den[:, 0:4], in_=pov[:, :, 64]), reads=[por], writes=["den"])
                    S.dve(lambda e, pov=pov: e.tensor_tensor(out=om[:, :].rearrange("p (h d) -> p h d", h=4), in0=pov[:, :, 0:64],
                                                             in1=den[:, 0:4].unsqueeze(2).to_broadcast([128, 4, 64]), op=ALU.mult),
                          reads=[por, "den"], writes=["om"])
                    pp, pr = newps()
                    for k in range(2):
                        tr(pp[:, k * 128:(k + 1) * 128], om[:, k * 128:(k + 1) * 128], ["om"], [pr])
                    act(omT[:, :, t * 128:(t + 1) * 128], pp[:, 0:256].rearrange("p (a b) -> p a b", a=2), AF.Copy, [pr], ["omT", "vg"])
                for t in range(4):
                    for n in range(2):
                        pp, pr = newps()
                        for c in range(2):
                            mm(pp[:, :], omT[:, c, t * 128:(t + 1) * 128], wmo[:, c, n * 512:(n + 1) * 512], c == 0, c == 1, ["omT", "wmo"], [pr])
                        S.dve(lambda e, pp=pp, t=t, n=n: e.tensor_tensor(
                            out=xg[:, t, n * 512:(n + 1) * 512], in0=pp[:, :], in1=xg[:, t, n * 512:(n + 1) * 512], op=ALU.add),
                            reads=[pr, "xg%d" % t], writes=["xg%d" % t])

                for t in range(4):
                    dump("x2_%d_%d" % (grp, t), xg[:, t, :], 1024, ["xg%d" % t])
                chk("xattn")
                for t in range(4):
                    norm_T(xg[:, t, :], "xg%d" % t, wn3, "wn", hT[:, :, t * 128:(t + 1) * 128], "hT%d" % t)
                for i in range(6):
                    ncol = 512 if i < 5 else 256
                    wg_, wgr = wload("w_ffn_gate", 0, 8, i * 512, ncol)
                    wu_, wur = wload("w_ffn_up", 0, 8, i * 512, ncol)
                    for c in range(ncol // 128):
                        fc = i * 4 + c
                        slt, slr = sl[fc % 2], "sl%d" % (fc % 2)
                        fm_proj(wg_, wgr, c * 128, 128, hT, HT_ALL, 512,
                                lambda pp, pr, slt=slt, slr=slr: act(slt[:], pp[:, :], AF.Silu, [pr], [slr]))
                        fm_proj(wu_, wur, c * 128, 128, hT, HT_ALL, 512,
                                lambda pp, pr, slt=slt, slr=slr, fc=fc: S.dve(lambda e: e.tensor_tensor(
                                    out=actT[:, fc, :], in0=pp[:, :], in1=slt[:], op=ALU.mult), reads=[pr, slr], writes=["big"]))
                for n in range(2):
                    pst = [newps() for _ in range(4)]
                    for piece, (k0, nk) in enumerate(((0, 8), (8, 8), (16, 6))):
                        wt, wr = wload("w_ffn_down", k0, nk, n * 512, 512)
                        for t in range(4):
                            pp, pr = pst[t]
                            for k in range(nk):
                                kk = k0 + k
                                mm(pp[:, :], actT[:, kk, t * 128:(t + 1) * 128], wt[:, k, :], kk == 0, kk == 21, ["big", wr], [pr])
                    for t in range(4):
                        pp, pr = pst[t]
                        S.dve(lambda e, pp=pp, t=t, n=n: e.tensor_tensor(
                            out=xg[:, t, n * 512:(n + 1) * 512], in0=pp[:, :], in1=xg[:, t, n * 512:(n + 1) * 512], op=ALU.add),
                            reads=[pr, "xg%d" % t], writes=["xg%d" % t])

                chk("ffn")
                for t in range(4):
                    ys_, ysr = XS[t % 2]
                    act(junk[:], xg[:, t, :], AF.Square, ["xg%d" % t], ["junk", "ss2"], accum_out=ss[:, 2:3])
                    rstd_from_ss(2, 1.0 / D, sres="ss2", rres="rs2")
                    S.dve(lambda e, t=t, ys_=ys_: e.scalar_tensor_tensor(out=ys_[:], in0=xg[:, t, :], scalar=rs[:, 2:3], in1=wfin[:],
                                                                         op0=ALU.mult, op1=ALU.mult),
                          reads=["xg%d" % t, "rs2", "wfin"], writes=[ysr])
                    row0 = (grp * 4 + t) * 128
                    ld("sp", yout[row0:row0 + 128, :], ys_[:], "yout%d" % (t % 2), [], reads=[ysr])

        except _Stop:
            pass
        S.emit(final_wait_keys=[k for k in ["yout0", "yout1"] + dbg_keys if k in S.dma_count])
    return nc


_CONST_CACHE = {}


def _consts():
    if not _CONST_CACHE:
        ident = np.eye(128, dtype=np.float32)
        a = np.arange(128)
        u64 = ((a[:, None] // 64 == a[None, :] // 64) & (a[:, None] > a[None, :])).astype(np.float32) * (-1.0 / 16)
        u128 = (a[:, None] > a[None, :]).astype(np.float32) * (-1.0 / 16)
        ind2 = np.zeros((128, 2), np.float32)
        ind2[:64, 0] = -1.0 / 16
        ind2[64:, 1] = -1.0 / 16
        ind1 = np.full((128, 2), -1.0 / 16, np.float32)
        _CONST_CACHE.update(ident=ident, ucum64=u64, ucum128=u128, ind2=ind2, ind1=ind1)
    return _CONST_CACHE


def make_in_maps(inputs, ncore=NCORE, tok_core=TOK_CORE, npre_tiles=48):
    x = np.asarray(inputs["x"], dtype=np.float32)
    memv = np.asarray(inputs["mem"], dtype=np.float32)
    B, SEQ, _ = x.shape
    per_b = SEQ // tok_core
    c = _consts()
    maps = []
    for core in range(ncore):
        b, j = core // per_b, core % per_b
        npre = npre_tiles * 128
        xa = np.zeros((npre + tok_core, D), np.float32)
        end = (j + 1) * tok_core
        start = max(0, j * tok_core - npre)
        seg = x[b, start:end]
        xa[npre + tok_core - seg.shape[0]:] = seg
        m = dict(xall=xa, mem=np.ascontiguousarray(memv[b]),
                 flagb=np.full((128, 1), 0.0 if j > 0 else -30000.0, np.float32))
        m.update(c)
        for k, v in inputs.items():
            if k in ("x", "mem"):
                continue
            v = np.asarray(v, dtype=np.float32)
            m[k] = np.ascontiguousarray(v[0]) if k != "norm_final_w" else np.ascontiguousarray(v)
        maps.append(m)
    return maps


_NC_CACHE = {}


def kernel(**inputs):
    if "nc" not in _NC_CACHE:
        _NC_CACHE["nc"] = build()
    nc = _NC_CACHE["nc"]
    maps = make_in_maps(inputs)
    res = run_bass_kernel_spmd(nc, maps, core_ids=list(range(NCORE)))
    x = inputs["x"]
    B, SEQ, _ = x.shape
    per_b = SEQ // TOK_CORE
    out = np.empty((B, SEQ, D), np.float32)
    for core in range(NCORE):
        b, j = core // per_b, core % per_b
        out[b, j * TOK_CORE:(j + 1) * TOK_CORE] = res.results[core]["y"]
    return out
```

```python
import contextlib
import numpy as np
import concourse.bass as bass
import concourse.mybir as mybir
from concourse.bass_utils import run_bass_kernel_spmd

F32 = mybir.dt.float32
BF16 = mybir.dt.bfloat16
AF = mybir.ActivationFunctionType
ALU = mybir.AluOpType

D = 1024
NCORE = 8
TOK_CORE = 2048
DFF = 2816
INC = 6416
O_QA, O_KA, O_VA, O_QG, O_KG, O_VG, O_GG, O_AL, O_GT = 0, 1024, 1152, 1280, 1792, 2304, 3328, 4352, 4368


class _Stop(Exception):
    pass


class Sched:
    ENGS = ("pe", "act", "dve", "pool", "sp")

    def __init__(self, nc, same_engine_sync=True):
        self.nc = nc
        self.ops = []
        self.last_write = {}
        self.reads_since = {}
        self.dma_count = {}
        self.same_engine_sync = same_engine_sync

    def _add(self, eng, fn, reads, writes, dma_key=None):
        idx = len(self.ops)
        deps = set()
        for r in reads:
            lw = self.last_write.get(r)
            if lw is not None:
                deps.add(lw)
        for w in writes:
            lw = self.last_write.get(w)
            if lw is not None:
                deps.add(lw)
            for rd in self.reads_since.get(w, ()):
                deps.add(rd)
        for r in reads:
            self.reads_since.setdefault(r, []).append(idx)
        for w in writes:
            self.last_write[w] = idx
            self.reads_since[w] = []
        deps.discard(idx)
        op = dict(eng=eng, fn=fn, deps=deps, dma_key=dma_key, signal=False, idx=idx)
        if dma_key is not None:
            c = self.dma_count.get(dma_key, 0) + 16
            self.dma_count[dma_key] = c
            op["dma_val"] = c
        self.ops.append(op)
        return idx

    def pe(self, fn, reads=(), writes=()):
        return self._add("pe", fn, reads, writes)

    def act(self, fn, reads=(), writes=()):
        return self._add("act", fn, reads, writes)

    def dve(self, fn, reads=(), writes=()):
        return self._add("dve", fn, reads, writes)

    def pool(self, fn, reads=(), writes=()):
        return self._add("pool", fn, reads, writes)

    def dma(self, eng, fn, key, reads=(), writes=()):
        return self._add(eng, fn, reads, writes, dma_key=key)

    def emit(self, final_wait_keys=()):
        nc = self.nc
        ops = self.ops
        need = []
        for op in ops:
            nd = []
            for d in sorted(op["deps"]):
                y = ops[d]
                if y["dma_key"] is None and y["eng"] == op["eng"] and op["dma_key"] is None:
                    if op["eng"] == "pe" or not self.same_engine_sync:
                        continue
                nd.append(d)
                if y["dma_key"] is None:
                    y["signal"] = True
            need.append(nd)
        cnt = {e: 0 for e in self.ENGS}
        for op in ops:
            if op["dma_key"] is None and op["signal"]:
                cnt[op["eng"]] += 1
                op["sig_val"] = cnt[op["eng"]]
        with contextlib.ExitStack() as st:
            esem = {e: st.enter_context(nc.semaphore("s_" + e)) for e in self.ENGS}
            dsem = {k: st.enter_context(nc.semaphore("d_%d" % i)) for i, k in enumerate(self.dma_count)}
            block = st.enter_context(nc.Block())
            per = {e: [] for e in self.ENGS}
            for op, nd in zip(ops, need):
                per[op["eng"]].append((op, nd))

            def body(ename, handle):
                waited = {}
                for op, nd in per[ename]:
                    for d in nd:
                        y = ops[d]
                        if y["dma_key"] is not None:
                            sem, val, k = dsem[y["dma_key"]], y["dma_val"], ("d", y["dma_key"])
                        else:
                            sem, val, k = esem[y["eng"]], y["sig_val"], ("e", y["eng"])
                        if waited.get(k, 0) >= val:
                            continue
                        waited[k] = val
                        handle.wait_ge(sem, val)
                    ins = op["fn"](handle)
                    if op["dma_key"] is not None:
                        ins.then_inc(dsem[op["dma_key"]], 16)
                    elif op["signal"]:
                        ins.then_inc(esem[ename], 1)
                if ename == "sp":
                    for k in final_wait_keys:
                        handle.wait_ge(dsem[k], self.dma_count[k])

            @block.tensor
            def _(e):
                body("pe", e)

            @block.scalar
            def _(e):
                body("act", e)

            @block.vector
            def _(e):
                body("dve", e)

            @block.gpsimd
            def _(e):
                body("pool", e)

            @block.sync
            def _(e):
                body("sp", e)


def build(NPRE=48, NGRP=4, NSLOT=3, same_engine_sync=True, dbg=False, stop_after=None):
    nc = bass.Bass("TRN2", target_bir_lowering=False)
    NT_ALL = NPRE + NGRP * 4

    def din(name, shape):
        return nc.dram_tensor(name, list(shape), F32, kind="ExternalInput").ap()

    xall = din("xall", [NT_ALL * 128, D])
    mem = din("mem", [256, D])
    flagb = din("flagb", [128, 1])
    ident_d = din("ident", [128, 128])
    ucum64_d = din("ucum64", [128, 128])
    ucum128_d = din("ucum128", [128, 128])
    ind2_d = din("ind2", [128, 2])
    ind1_d = din("ind1", [128, 2])
    norm_mix_w = din("norm_mix_w", [D])
    w_in = din("w_in", [D, INC])
    b_gate = din("b_gate", [2 * D])
    attn_sinks = din("attn_sinks", [16])
    gla_gate_w2 = din("gla_gate_w2", [16, 512])
    gla_gate_b = din("gla_gate_b", [512])
    gla_norm_w = din("gla_norm_w", [256])
    w_attn_o = din("w_attn_o", [D, D])
    w_gla_o = din("w_gla_o", [D, D])
    w_mix_o = din("w_mix_o", [D, D])
    norm_mem_q_w = din("norm_mem_q_w", [D])
    norm_mem_kv_w = din("norm_mem_kv_w", [D])
    w_mem_q = din("w_mem_q", [D, 256])
    w_mem_kv = din("w_mem_kv", [D, 512])
    w_mem_o = din("w_mem_o", [256, D])
    norm_ffn_w = din("norm_ffn_w", [D])
    w_ffn_gate = din("w_ffn_gate", [D, DFF])
    w_ffn_up = din("w_ffn_up", [D, DFF])
    w_ffn_down = din("w_ffn_down", [DFF, D])
    norm_final_w = din("norm_final_w", [D])
    yout = nc.dram_tensor("y", [NGRP * 512, D], F32, kind="ExternalOutput").ap()
    WSRC = {"w_in": w_in, "w_attn_o": w_attn_o, "w_gla_o": w_gla_o, "w_mix_o": w_mix_o,
            "w_ffn_gate": w_ffn_gate, "w_ffn_up": w_ffn_up, "w_ffn_down": w_ffn_down, "w_mem_kv": w_mem_kv}
    WB = {}
    for nm_ in ("w_in", "w_attn_o", "w_gla_o", "w_mix_o", "w_ffn_gate", "w_ffn_up", "w_ffn_down"):
        WB[nm_] = nc.dram_tensor("wb_" + nm_, list(WSRC[nm_].shape), BF16, kind="Internal").ap()

    S = Sched(nc, same_engine_sync=same_engine_sync)
    st = contextlib.ExitStack()
    with st:
        def sb(name, shape, dt=F32):
            return st.enter_context(nc.sbuf_tensor("sb_" + name, list(shape), dt))

        psb = [st.enter_context(nc.psum_tensor("ps%d" % i, [128, 512], F32)) for i in range(8)]
        pcnt = {"any": 0, "r0": 0, "r1": 0}
        PSETS = {"any": [0, 1, 2, 3, 4, 5, 6, 7], "r0": [0, 1, 4, 5], "r1": [2, 3, 6, 7]}

        def newps(kind="any"):
            lst = PSETS[kind]
            i = lst[pcnt[kind] % len(lst)]
            pcnt[kind] += 1
            return psb[i], "ps%d" % i

        idt = sb("idt", [128, 128])
        uc64 = sb("uc64", [128, 128])
        uc128 = sb("uc128", [128, 128])
        ind2 = sb("ind2", [128, 2])
        ind1 = sb("ind1", [128, 2])
        flg = sb("flg", [128, 1])
        vst = [sb("vst%d" % i, [16, 128]) for i in range(2)]
        wn1 = sb("wn1", [128, 8])
        wn2 = sb("wn2", [128, 8])
        wn3 = sb("wn3", [128, 8])
        wnkv = sb("wnkv", [128, 8])
        gnw8 = sb("gnw8", [128, 8])
        bgate = sb("bgate", [128, 16])
        exps = sb("exps", [128, 16])
        wfin = sb("wfin", [128, D])
        w2aug = sb("w2aug", [32, 512], BF16)
        wkdup = sb("wkdup", [128, 8, 2, 128], BF16)
        wva = sb("wva", [128, 8, 128], BF16)
        walr = sb("walr", [128, 8, 16], BF16)
        wmq = sb("wmq", [128, 8, 256], BF16)
        wmo = sb("wmo", [128, 2, D], BF16)
        big = sb("big", [128, 8 * 1536], BF16)
        wpre = big[:, :].rearrange("p (k c) -> p k c", k=8)
        actT = big[:, 0:22 * 512].rearrange("p (f t) -> p f t", f=22)
        kmT = sb("kmT", [128, 2, 256], BF16)
        vmaug = sb("vmaug", [128, 2, 4, 65], BF16)
        wsl = [sb("wsl%d" % i, [128, 8, 512], BF16) for i in range(NSLOT)]
        xg = sb("xg", [128, 4, D])
        xs = sb("xs", [128, D])
        junk = sb("junk", [128, D], BF16)
        ss = sb("ss", [128, 8])
        rs = sb("rs", [128, 8])
        hT = sb("hT", [128, 8, 512], BF16)
        qT = sb("qT", [128, 8, 512], BF16)
        kT = sb("kT", [128, 2, 640], BF16)
        vaug = sb("vaug", [128, 5, 2, 65], BF16)
        PT = [sb("PT%d" % i, [128, 2, 4, 128], BF16) for i in range(2)]
        oa = sb("oa", [128, D])
        oaT = sb("oaT", [128, 8, 512], BF16)
        den = sb("den", [128, 16])
        qgT = sb("qgT", [128, 4, 512], BF16)
        alrT = sb("alrT", [32, 512], BF16)
        kg = sb("kg", [128, 4, 512])
        vg = sb("vg", [128, 4, D], BF16)
        sg = sb("sg", [128, 4, D], BF16)
        Gf = sb("Gf", [128, 512])
        Dm = sb("Dm", [128, 512])
        av = sb("av", [128, 8])
        kdec = sb("kdec", [128, 512], BF16)
        Sst = sb("Sst", [128, D])
        Sbf = [sb("Sbf%d" % i, [128, D], BF16) for i in range(2)]
        og = sb("og", [128, D])
        ogT = sb("ogT", [128, 8, 512], BF16)
        gT = sb("gT", [128, 16, 512], BF16)
        m1 = sg[:, :, :].rearrange("p a (b c) -> p (a b) c", b=2)
        mtmp = Gf
        mT = qT
        qmT = vg[:, 0, :].rearrange("p (a b) -> p a b", a=2)
        PmT = vg[:, 1, :].rearrange("p (r m c q) -> p r m c q", r=2, m=2, c=2)
        omT = vg[:, 2, :].rearrange("p (a b) -> p a b", a=2)
        om = sb("om", [128, 256])
        sl = [sb("sl%d" % i, [128, 512], BF16) for i in range(2)]
        yst = xs
        xm = oa
        xs2 = sb("xs2", [128, D])
        av2 = sb("av2", [128, 8])
        XS = [(xs, "xs0"), (xs2, "xs1")]
        xm1 = gT[:, 0:4, :].rearrange("p a b -> p (a b)").bitcast(F32)
        Gf1 = gT[:, 4:6, :].rearrange("p a b -> p (a b)").bitcast(F32)
        Dm1 = gT[:, 6:8, :].rearrange("p a b -> p (a b)").bitcast(F32)
        kdec1 = gT[:, 8, :]
        PRE_RES = ["pxm1", "pGf1", "pDm1", "pkdec1"]
        GB = [dict(Gf=Gf[:], Dm=Dm[:], av=av, kdec=kdec[:], rGf="Gf", rDm="Dm", rav="av", rkdec="kdec"),
              dict(Gf=Gf1, Dm=Dm1, av=av2, kdec=kdec1, rGf="pGf1", rDm="pDm1", rav="av2", rkdec="pkdec1")]

        HT_ALL = ["hT0", "hT1", "hT2", "hT3"]

        def chk(name):
            if stop_after == name:
                raise _Stop()

        dbg_keys = []

        def dump(name, ap, ncols, reads):
            if not dbg:
                return
            d_ = nc.dram_tensor("dbg_" + name, [128, ncols], F32, kind="ExternalOutput").ap()
            k_ = "dbg_" + name
            dbg_keys.append(k_)
            S.dma("sp", lambda e: e.dma_start(out=d_, in_=ap), k_, reads=reads)

        ukey = [0]

        def ld(eng, out_ap, in_ap, key, writes, reads=(), slow=False):
            if key in ("c", "cw"):
                ukey[0] += 1
                if key == "cw":
                    reads = list(reads) + ["cwtok%d" % (ukey[0] % 2)]
                    writes = list(writes) + ["cwtok%d" % (ukey[0] % 2)]
                key = "%s%d" % (key, ukey[0])
            if slow:
                S.dma(eng, lambda e: e.dma_start(out=out_ap, in_=in_ap, allow_slow_non_contiguous=True), key, reads=reads, writes=writes)
            else:
                S.dma(eng, lambda e: e.dma_start(out=out_ap, in_=in_ap), key, reads=reads, writes=writes)

        slot_i = [0]

        def wload(wname, k0, nk, c0, ncols):
            i = slot_i[0] % NSLOT
            slot_i[0] += 1
            W = WB.get(wname, WSRC[wname])
            src = W[k0 * 128:(k0 + nk) * 128, c0:c0 + ncols].rearrange("(k p) c -> p k c", p=128)
            ld("pool", wsl[i][:, 0:nk, 0:ncols], src, "wsl%d" % i, writes=["wsl%d" % i], reads=["wb_" + wname])
            return wsl[i], "wsl%d" % i

        def mm(out_ap, lhsT, rhs, start, stop, reads, writes):
            S.pe(lambda e: e.matmul(out=out_ap, lhsT=lhsT, rhs=rhs, start=start, stop=stop), reads=reads, writes=writes)

        def tr(out_ap, in_ap, reads, writes):
            S.pe(lambda e: e.transpose(out=out_ap, in_=in_ap, identity=idt[:]), reads=list(reads) + ["idt"], writes=writes)

        def act(out_ap, in_ap, func, reads, writes, **kw):
            S.act(lambda e: e.activation(out=out_ap, in_=in_ap, func=func, **kw), reads=reads, writes=writes)

        def rstd_from_ss(col, n_inv, ncol=1, sres="ss", rres="rs"):
            act(rs[:, col:col + ncol], ss[:, col:col + ncol], AF.Ln, [sres], [rres], scale=n_inv, bias=1e-6)
            act(rs[:, col:col + ncol], rs[:, col:col + ncol], AF.Exp, [rres], [rres], scale=-0.5)

        nbuf = [0]

        def norm_T(x_ap, xres, wn, wnres, dst, dstres):
            b_ = nbuf[0] % 2
            nbuf[0] += 1
            xs_, xsr = XS[b_]
            sres, rres = "ssn%d" % b_, "rsn%d" % b_
            act(junk[:], x_ap, AF.Square, [xres], ["junk", sres], accum_out=ss[:, b_:b_ + 1])
            rstd_from_ss(b_, 1.0 / D, sres=sres, rres=rres)
            S.dve(lambda e: e.tensor_scalar(out=xs_[:], in0=x_ap, scalar1=rs[:, b_:b_ + 1], scalar2=None, op0=ALU.mult),
                  reads=[xres, rres], writes=[xsr])
            for half in range(2):
                pp, pr = newps()
                for k in range(4):
                    kk = half * 4 + k
                    tr(pp[:, k * 128:(k + 1) * 128], xs_[:, kk * 128:(kk + 1) * 128], [xsr], [pr])
                S.dve(lambda e, pp=pp, half=half: e.tensor_tensor(
                    out=dst[:, half * 4:(half + 1) * 4, :], in0=pp[:, :].rearrange("p (a b) -> p a b", a=4),
                    in1=wn[:, half * 4:(half + 1) * 4].unsqueeze(2).to_broadcast([128, 4, 128]), op=ALU.mult),
                    reads=[pr, wnres], writes=[dstres])

        def tm_proj(lhs_tile, lhs_res, wt, wres, c0, ncols, evac):
            pp, pr = newps()
            for k in range(8):
                mm(pp[:, 0:ncols], lhs_tile[:, k, :], wt[:, k, c0:c0 + ncols], k == 0, k == 7, [lhs_res, wres], [pr])
            evac(pp, pr)

        def fm_proj(wt, wres, c0, m, rhs, rhs_res, ntok, evac, nk=8):
            pp, pr = newps()
            rr = list(rhs_res) if isinstance(rhs_res, (list, tuple)) else [rhs_res]
            for k in range(nk):
                mm(pp[0:m, 0:ntok], wt[:, k, c0:c0 + m], rhs[:, k, 0:ntok], k == 0, k == nk - 1, rr + [wres], [pr])
            evac(pp, pr)

        def gla_gate(tokc0, umat, ures, indt, indres, B=None, alr_ap=None, alr_res="alrT"):
            B = B or GB[0]
            Gf, Dm, av = B["Gf"], B["Dm"], B["av"]
            rGf, rDm, rav = B["rGf"], B["rDm"], B["rav"]
            alr_ap = alr_ap if alr_ap is not None else alrT[0:17, tokc0:tokc0 + 128]
            pz, pzr = newps()
            mm(pz[:, :], alr_ap, w2aug[0:17, :], True, True, [alr_res, "w2aug"], [pzr])
            act(Gf, pz[:, :], AF.Exp, [pzr], [rGf], scale=-1.0)
            act(Gf, Gf, AF.Ln, [rGf], [rGf], bias=1.0)
            pR, pRr = newps()
            mm(pR[:, :], umat[:], Gf, True, True, [ures, rGf], [pRr])
            pa, par = newps()
            for h in range(4):
                mm(pa[:, 2 * h:2 * h + 2], Gf[:, h * 128:(h + 1) * 128], indt[:], True, True, [rGf, indres], [par])
            act(Dm, pR[:, :], AF.Exp, [pRr], [rDm])
            act(av[:], pa[:, 0:8], AF.Exp, [par], [rav])

        def state_update(krows, j, vsrc, vres, kind, B=None):
            B = B or GB[0]
            kdec, av = B["kdec"], B["av"]
            rkdec, rav = B["rkdec"], B["rav"]
            r0, r1 = krows
            pA, pAr = newps(kind)
            pB, pBr = newps(kind)
            for h in range(4):
                pp, pr = (pA, pAr) if h < 2 else (pB, pBr)
                mm(pp[:, (h % 2) * 256:(h % 2 + 1) * 256], kdec[r0:r1, h * 128:(h + 1) * 128],
                   vsrc[r0:r1, h * 256:(h + 1) * 256], True, True, [rkdec, vres], [pr])
            for h in range(4):
                pp, pr = (pA, pAr) if h < 2 else (pB, pBr)
                S.dve(lambda e, h=h, pp=pp: e.scalar_tensor_tensor(
                    out=Sst[:, h * 256:(h + 1) * 256], in0=Sst[:, h * 256:(h + 1) * 256], scalar=av[:, 2 * h + j:2 * h + j + 1],
                    in1=pp[:, (h % 2) * 256:(h % 2 + 1) * 256], op0=ALU.mult, op1=ALU.add),
                    reads=["Sst", rav, pr], writes=["Sst"])

        try:
            ld("sp", idt[:], ident_d, "c", ["idt"])
            ld("sp", uc64[:], ucum64_d, "c", ["uc64"])
            ld("sp", uc128[:], ucum128_d, "c", ["uc128"])
            ld("sp", ind2[:], ind2_d, "c", ["ind2"])
            ld("sp", ind1[:], ind1_d, "c", ["ind1"])
            ld("sp", flg[:], flagb, "c", ["flg"])
            for vi, (t_, src_, kk_, res_) in enumerate(((wn1, norm_mix_w, 8, "wn"), (wn2, norm_mem_q_w, 8, "wn"), (wn3, norm_ffn_w, 8, "wn"),
                                                      (wnkv, norm_mem_kv_w, 8, "wn"), (bgate, b_gate, 16, "bgate"), (gnw8, gla_norm_w, 2, "gnw8"))):
                stg = vst[vi % 2]
                sres = "vst%d" % (vi % 2)
                ld("sp", stg[0:kk_, :], src_.rearrange("(k p) -> k p", p=128), "c", [sres])
                pp, pr = newps()
                S.pe(lambda e, pp=pp, stg=stg, kk_=kk_: e.transpose(out=pp[:, 0:kk_], in_=stg[0:kk_, :], identity=idt[0:kk_, 0:kk_]),
                     reads=[sres, "idt"], writes=[pr])
                if kk_ == 2:
                    for i in range(4):
                        act(t_[:, 2 * i:2 * i + 2], pp[:, 0:2], AF.Copy, [pr], [res_])
                else:
                    act(t_[:, 0:kk_], pp[:, 0:kk_], AF.Copy, [pr], [res_])
            ld("sp", exps[:], attn_sinks.partition_broadcast(128), "c", ["exps"])
            ld("sp", wfin[:], norm_final_w.partition_broadcast(128), "c", ["wfin"])
            CONSTS = ["idt", "uc64", "uc128", "ind2", "ind1", "flg", "wn", "gnw8", "bgate", "exps", "wfin"]
            act(exps[:], exps[:], AF.Exp, ["exps"], ["exps"])
            ld("pool", w2aug[0:16, :], gla_gate_w2, "cw", ["w2aug"])
            ld("pool", w2aug[16:17, :], gla_gate_b.rearrange("(o c) -> o c", o=1), "cw", ["w2aug"])
            for g in range(2):
                for r in range(2):
                    ld("pool", wkdup[:, :, g, 64 * r:64 * r + 64],
                       w_in[:, O_KA + 64 * g:O_KA + 64 * g + 64].rearrange("(k p) c -> p k c", p=128), "cw", ["wkdup"])
            ld("pool", wva[:], w_in[:, O_VA:O_VA + 128].rearrange("(k p) c -> p k c", p=128), "cw", ["wva"])
            ld("pool", walr[:], w_in[:, O_AL:O_AL + 16].rearrange("(k p) c -> p k c", p=128), "cw", ["walr"])
            ld("pool", wmq[:], w_mem_q.rearrange("(k p) c -> p k c", p=128), "cw", ["wmq"])
            ld("pool", wmo[:], w_mem_o.rearrange("(k p) c -> p k c", p=128), "cw", ["wmo"])
            for i in range(3):
                ld("pool", wpre[:, :, i * 512:(i + 1) * 512],
                   w_in[:, O_KG + i * 512:O_KG + (i + 1) * 512].rearrange("(k p) c -> p k c", p=128), "cw", ["big"])
            S.dve(lambda e: e.memset(Sst[:], 0.0), writes=["Sst"])
            S.dve(lambda e: e.memset(alrT[:], 1.0), writes=["alrT"])
            S.dve(lambda e: e.memset(vaug[:], 1.0), writes=["vaug"])
            S.dve(lambda e: e.memset(vmaug[:], 1.0), writes=["vmaug"])
            for i in range(2):
                S.dve(lambda e, i=i: e.memset(PT[i][:], 0.0), writes=["PT%d" % i])

            for mt in range(2):
                ld("sp", xm[:], mem[mt * 128:(mt + 1) * 128, :], "xm", ["oa"])
                norm_T(xm[:], "oa", wnkv, "wn", hT[:, :, mt * 128:(mt + 1) * 128], "hT%d" % mt)
            wt, wr = wload("w_mem_kv", 0, 8, 0, 512)
            for c in range(2):
                fm_proj(wt, wr, c * 128, 128, hT, ["hT0", "hT1"], 256,
                        lambda pp, pr, c=c: act(kmT[:, c, :], pp[:, 0:256], AF.Copy, [pr], ["kmT"]))
            for mt in range(2):
                tm_proj(hT[:, :, mt * 128:(mt + 1) * 128], "hT%d" % mt, wt, wr, 256, 256,
                        lambda pp, pr, mt=mt: act(vmaug[:, mt, :, 0:64], pp[:, 0:256].rearrange("p (h d) -> p h d", h=4), AF.Copy, [pr], ["vmaug"]))

            for nm_ in ("w_in", "w_attn_o", "w_gla_o", "w_mix_o", "w_ffn_gate", "w_ffn_up", "w_ffn_down"):
                src_, dst_ = WSRC[nm_], WB[nm_]
                for rb in range(src_.shape[0] // 128):
                    S.dma("pool", lambda e, src_=src_, dst_=dst_, rb=rb: e.dma_start(
                        out=dst_[rb * 128:(rb + 1) * 128, :], in_=src_[rb * 128:(rb + 1) * 128, :], max_dma_last_dim=4096),
                        "cast", reads=["casttok"], writes=["casttok", "wb_" + nm_])
            chk("setup")
            def hslot(t):
                return hT[:, :, (t % 4) * 128:(t % 4 + 1) * 128], "hT%d" % (t % 4)

            def bufs(t):
                b = t % 2
                kgr, vgr, alr = (("kg", "vg", "alrT"), ("kgB", "vgB", "alrTB"))[b]
                return b, kgr, vgr, alr, GB[b]

            def pre_A1(t):
                b = t % 2
                xmb, xmr = ((xm[:], "oa"), (xm1, "pxm1"))[b]
                xs_, xsr = XS[b]
                sres, rres = "ssn%d" % b, "rsn%d" % b
                ld("sp", xmb, xall[t * 128:(t + 1) * 128, :], "xm%d" % b, [xmr])
                act(junk[:], xmb, AF.Square, [xmr], ["junk", sres], accum_out=ss[:, b:b + 1])
                rstd_from_ss(b, 1.0 / D, sres=sres, rres=rres)
                S.dve(lambda e: e.tensor_scalar(out=xs_[:], in0=xmb, scalar1=rs[:, b:b + 1], scalar2=None, op0=ALU.mult),
                      reads=[xmr, rres], writes=[xsr])

            def pre_A2(t):
                xs_, xsr = XS[t % 2]
                dst, dstres = hslot(t)
                for half in range(2):
                    pp, pr = newps()
                    for k in range(4):
                        kk = half * 4 + k
                        tr(pp[:, k * 128:(k + 1) * 128], xs_[:, kk * 128:(kk + 1) * 128], [xsr], [pr])
                    S.dve(lambda e, pp=pp, half=half: e.tensor_tensor(
                        out=dst[:, half * 4:(half + 1) * 4, :], in0=pp[:, :].rearrange("p (a b) -> p a b", a=4),
                        in1=wn1[:, half * 4:(half + 1) * 4].unsqueeze(2).to_broadcast([128, 4, 128]), op=ALU.mult),
                        reads=[pr, "wn"], writes=[dstres])

            def pre_B1a(t):
                b, kgr, vgr, alr, B = bufs(t)
                hTt, hres = hslot(t)
                tm_proj(hTt, hres, wpre, "big", 0, 512,
                        lambda pp, pr: act(kg[:, b, :], pp[:, :], AF.Copy, [pr], [kgr]))
                fm_proj(walr, "walr", 0, 16, hTt, hres, 128,
                        lambda pp, pr: act(alrT[0:16, b * 128:(b + 1) * 128], pp[0:16, 0:128], AF.Copy, [pr], [alr]))

            def pre_B1b(t):
                b, kgr, vgr, alr, B = bufs(t)
                hTt, hres = hslot(t)
                for i in range(2):
                    tm_proj(hTt, hres, wpre, "big", 512 + i * 512, 512,
                            lambda pp, pr, i=i: act(vg[:, b, i * 512:(i + 1) * 512], pp[:, :], AF.Copy, [pr], [vgr]))
                if t == NPRE - 1:
                    for g in range(2):
                        fm_proj(wkdup[:, :, g, :], "wkdup", 0, 128, hTt, hres, 128,
                                lambda pp, pr, g=g: act(kT[:, g, 0:128], pp[:, 0:128], AF.Copy, [pr], ["kT"]))
                    tm_proj(hTt, hres, wva, "wva", 0, 128,
                            lambda pp, pr: act(vaug[:, 0, :, 0:64], pp[:, 0:128].rearrange("p (g d) -> p g d", g=2), AF.Copy, [pr], ["vaug"]))

            def pre_B2a(t):
                b, kgr, vgr, alr, B = bufs(t)
                pz, pzr = newps()
                mm(pz[:, :], alrT[0:17, b * 128:(b + 1) * 128], w2aug[0:17, :], True, True, [alr, "w2aug"], [pzr])
                act(B["Gf"], pz[:, :], AF.Exp, [pzr], [B["rGf"]], scale=-1.0)
                act(B["Gf"], B["Gf"], AF.Ln, [B["rGf"]], [B["rGf"]], bias=1.0)

            def pre_B2b(t):
                b, kgr, vgr, alr, B = bufs(t)
                Gf_ = B["Gf"]
                pR, pRr = newps()
                mm(pR[:, :], uc128[:], Gf_, True, True, ["uc128", B["rGf"]], [pRr])
                pa, par = newps()
                for h in range(4):
                    mm(pa[:, 2 * h:2 * h + 2], Gf_[:, h * 128:(h + 1) * 128], ind1[:], True, True, [B["rGf"], "ind1"], [par])
                act(B["Dm"], pR[:, :], AF.Exp, [pRr], [B["rDm"]])
                act(B["av"][:], pa[:, 0:8], AF.Exp, [par], [B["rav"]])
                S.dve(lambda e: e.tensor_tensor(out=B["kdec"], in0=kg[:, b, :], in1=B["Dm"], op=ALU.mult),
                      reads=[kgr, B["rDm"]], writes=[B["rkdec"]])

            def pre_B2c(t):
                b, kgr, vgr, alr, B = bufs(t)
                state_update((0, 128), 0, vg[:, b, :], vgr, "any", B=B)

            for t in range(min(3, NPRE)):
                pre_A1(t)
            for t in range(min(2, NPRE)):
                pre_A2(t)
            pre_B1a(0)
            pre_B1b(0)
            for t in range(NPRE):
                if t + 3 < NPRE:
                    pre_A1(t + 3)
                pre_B2a(t)
                if t + 2 < NPRE:
                    pre_A2(t + 2)
                if t + 1 < NPRE:
                    pre_B1a(t + 1)
                pre_B2b(t)
                if t + 1 < NPRE:
                    pre_B1b(t + 1)
                pre_B2c(t)

            chk("prefix")
            for grp in range(NGRP):
                X0 = (lambda *names: list(names)) if grp == 0 else (lambda *names: [])
                for t in range(4):
                    row0 = (NPRE + grp * 4 + t) * 128
                    ld("sp", xg[:, t, :], xall[row0:row0 + 128, :], "xg%d" % t, ["xg%d" % t])
                    norm_T(xg[:, t, :], "xg%d" % t, wn1, "wn", hT[:, :, t * 128:(t + 1) * 128], "hT%d" % t)
                chk("norm1")
                for i in range(2):
                    wt, wr = wload("w_in", 0, 8, O_QA + i * 512, 512)
                    for c in range(4):
                        fm_proj(wt, wr, c * 128, 128, hT, HT_ALL, 512,
                                lambda pp, pr, cc=i * 4 + c: act(qT[:, cc, :], pp[:, :], AF.Copy, [pr], ["qT"]))
                for g in range(2):
                    fm_proj(wkdup[:, :, g, :], "wkdup", 0, 128, hT, HT_ALL, 512,
                            lambda pp, pr, g=g: act(kT[:, g, 128:640], pp[:, :], AF.Copy, [pr], ["kT"]))
                for t in range(4):
                    tm_proj(hT[:, :, t * 128:(t + 1) * 128], "hT%d" % t, wva, "wva", 0, 128,
                            lambda pp, pr, t=t: act(vaug[:, 1 + t, :, 0:64], pp[:, 0:128].rearrange("p (g d) -> p g d", g=2), AF.Copy, [pr], ["vaug"]))
                wt, wr = wload("w_in", 0, 8, O_QG, 512)
                for c in range(4):
                    fm_proj(wt, wr, c * 128, 128, hT, HT_ALL, 512,
                            lambda pp, pr, c=c: act(qgT[:, c, :], pp[:, :], AF.Copy, [pr], ["qgT"], scale=float(128 ** -0.5)))
                fm_proj(walr, "walr", 0, 16, hT, HT_ALL, 512,
                        lambda pp, pr: act(alrT[0:16, :], pp[0:16, :], AF.Copy, [pr], ["alrT"] + X0("alrTB")))
                wt, wr = wload("w_in", 0, 8, O_KG, 512)
                for t in range(4):
                    tm_proj(hT[:, :, t * 128:(t + 1) * 128], "hT%d" % t, wt, wr, 0, 512,
                            lambda pp, pr, t=t: act(kg[:, t, :], pp[:, :], AF.Copy, [pr], ["kg"] + X0("kgB")))
                for i in range(2):
                    wt, wr = wload("w_in", 0, 8, O_VG + i * 512, 512)
                    for t in range(4):
                        tm_proj(hT[:, :, t * 128:(t + 1) * 128], "hT%d" % t, wt, wr, 0, 512,
                                lambda pp, pr, t=t, i=i: act(vg[:, t, i * 512:(i + 1) * 512], pp[:, :], AF.Copy, [pr], ["vg", "qmT", "PmT", "omT"] + X0("vgB")))
                for i in range(2):
                    wt, wr = wload("w_in", 0, 8, O_GG + i * 512, 512)
                    for t in range(4):
                        tm_proj(hT[:, :, t * 128:(t + 1) * 128], "hT%d" % t, wt, wr, 0, 512,
                                lambda pp, pr, t=t, i=i: act(sg[:, t, i * 512:(i + 1) * 512], pp[:, :], AF.Silu, [pr], ["sg"]))
                for i in range(4):
                    wt, wr = wload("w_in", 0, 8, O_GT + i * 512, 512)
                    for c in range(4):
                        cc = i * 4 + c
                        fm_proj(wt, wr, c * 128, 128, hT, HT_ALL, 512,
                                lambda pp, pr, cc=cc: act(gT[:, cc, :], pp[:, :], AF.Sigmoid, [pr, "bgate"], ["gT"] + X0(*PRE_RES), bias=bgate[:, cc:cc + 1]))

                chk("proj")
                for t in range(4):
                    first = (grp == 0 and t == 0)
                    pO = [(psb[4 + i_], "ps%d" % (4 + i_)) for i_ in range(4)]
                    for g in range(2):
                        for r in range(2):
                            s_ = 2 * g + r
                            PTs, PTr = PT[s_ % 2], "PT%d" % (s_ % 2)
                            pA, pAr = psb[2 * r], "ps%d" % (2 * r)
                            pB, pBr = psb[2 * r + 1], "ps%d" % (2 * r + 1)
                            for j in range(4):
                                i = 4 * g + j
                                q_ap = qT[64 * r:64 * r + 64, i, t * 128:(t + 1) * 128]
                                mm(pA[:, j * 128:(j + 1) * 128], kT[64 * r:64 * r + 64, g, t * 128:(t + 1) * 128], q_ap, True, True, ["kT", "qT"], [pAr])
                                mm(pB[:, j * 128:(j + 1) * 128], kT[64 * r:64 * r + 64, g, (t + 1) * 128:(t + 2) * 128], q_ap, True, True, ["kT", "qT"], [pBr])
                            pAv = pA[:, :].rearrange("p (j q) -> p j q", j=4)
                            pBv = pB[:, :].rearrange("p (j q) -> p j q", j=4)
                            kw = dict(scale=0.125)
                            kwp = dict(scale=0.125, bias=flg[0:64, 0:1]) if first else kw
                            kwp2 = dict(scale=0.125, bias=flg[64:128, 0:1]) if first else kw
                            act(PTs[0:64, 0, :, 0:64], pAv[0:64, :, 0:64], AF.Exp, [pAr, "flg"], [PTr], **kwp)
                            act(PTs[64:128, 0, :, :], pAv[64:128, :, :], AF.Exp, [pAr, "flg"], [PTr], **kwp2)
                            act(PTs[0:64, 1, :, :], pBv[0:64, :, :], AF.Exp, [pBr], [PTr], **kw)
                            act(PTs[64:128, 1, :, 64:128], pBv[64:128, :, 64:128], AF.Exp, [pBr], [PTr], **kw)
                            po, por = pO[s_]
                            for j in range(4):
                                col = j * 65
                                mm(po[:, col:col + 65], PTs[:, 0, j, :], vaug[:, t, g, :], True, False, [PTr, "vaug"], [por])
                                mm(po[:, col:col + 65], PTs[:, 1, j, :], vaug[:, t + 1, g, :], False, True, [PTr, "vaug"], [por])
                    for g in range(2):
                        for r in range(2):
                            s_ = 2 * g + r
                            po, por = pO[s_]
                            pov = po[:, 0:260].rearrange("p (j e) -> p j e", j=4)
                            dv = den[:, s_ * 4:(s_ + 1) * 4]
                            ev = exps[:, g * 8:(g + 1) * 8].rearrange("p (j r) -> p r j", r=2)[:, r, :]
                            ov = oa[:, g * 512:(g + 1) * 512].rearrange("p (j r d) -> p r j d", j=4, r=2)[:, r, :, :]
                            S.dve(lambda e, pov=pov, dv=dv, ev=ev: e.tensor_tensor(out=dv, in0=pov[:, :, 64], in1=ev, op=ALU.add),
                                  reads=[por, "exps"], writes=["den"])
                            S.dve(lambda e, dv=dv: e.reciprocal(out=dv, in_=dv), reads=["den"], writes=["den"])
                            S.dve(lambda e, pov=pov, dv=dv, ov=ov: e.tensor_tensor(
                                out=ov, in0=pov[:, :, 0:64], in1=dv.unsqueeze(2).to_broadcast([128, 4, 64]), op=ALU.mult),
                                reads=[por, "den"], writes=["oa"])
                    dump("oa_%d_%d" % (grp, t), oa[:], 1024, ["oa"])
                    for half in range(2):
                        pp, pr = newps()
                        for k in range(4):
                            kk = half * 4 + k
                            tr(pp[:, k * 128:(k + 1) * 128], oa[:, kk * 128:(kk + 1) * 128], ["oa"], [pr])
                        act(oaT[:, half * 4:(half + 1) * 4, t * 128:(t + 1) * 128], pp[:, :].rearrange("p (a b) -> p a b", a=4), AF.Copy, [pr], ["oaT"])
                S.dve(lambda e: e.tensor_copy(out=kT[:, :, 0:128], in_=kT[:, :, 512:640]), reads=["kT"], writes=["kT"])
                S.dve(lambda e: e.tensor_copy(out=vaug[:, 0, :, :], in_=vaug[:, 4, :, :]), reads=["vaug"], writes=["vaug"])

                chk("swa")
                for t in range(4):
                    gla_gate(t * 128, uc64, "uc64", ind2, "ind2")
                    S.dve(lambda e, t=t: e.tensor_tensor(out=kdec[:], in0=kg[:, t, :], in1=Dm[:], op=ALU.mult), reads=["kg", "Dm"], writes=["kdec"])
                    for j in range(2):
                        state_update((64 * j, 64 * j + 64), j, vg[:, t, :], "vg", "r0" if j == 0 else "r1")
                        act(Sbf[j][:], Sst[:], AF.Copy, ["Sst"], ["Sbf%d" % j])
                        pA, pAr = newps()
                        pB, pBr = newps()
                        for h in range(4):
                            pp, pr = (pA, pAr) if h < 2 else (pB, pBr)
                            mm(pp[:, (h % 2) * 256:(h % 2 + 1) * 256], qgT[:, h, t * 128:(t + 1) * 128],
                               Sbf[j][:, h * 256:(h + 1) * 256], True, True, ["qgT", "Sbf%d" % j], [pr])
                        act(og[64 * j:64 * j + 64, 0:512], pA[64 * j:64 * j + 64, :], AF.Copy, [pAr], ["og"])
                        act(og[64 * j:64 * j + 64, 512:1024], pB[64 * j:64 * j + 64, :], AF.Copy, [pBr], ["og"])
                    for h in range(4):
                        act(junk[:, 0:256], og[:, h * 256:(h + 1) * 256], AF.Square, ["og"], ["junk", "ss"], accum_out=ss[:, 4 + h:5 + h])
                    rstd_from_ss(4, 1.0 / 256, ncol=4)
                    for h in range(4):
                        S.dve(lambda e, h=h, t=t: e.scalar_tensor_tensor(
                            out=og[:, h * 256:(h + 1) * 256], in0=og[:, h * 256:(h + 1) * 256], scalar=rs[:, 4 + h:5 + h],
                            in1=sg[:, t, h * 256:(h + 1) * 256], op0=ALU.mult, op1=ALU.mult),
                            reads=["og", "rs", "sg"], writes=["og"])
                    dump("og_%d_%d" % (grp, t), og[:], 1024, ["og"])
                    for half in range(2):
                        pp, pr = newps()
                        for k in range(4):
                            kk = half * 4 + k
                            tr(pp[:, k * 128:(k + 1) * 128], og[:, kk * 128:(kk + 1) * 128], ["og"], [pr])
                        S.dve(lambda e, pp=pp, half=half, t=t: e.tensor_tensor(
                            out=ogT[:, half * 4:(half + 1) * 4, t * 128:(t + 1) * 128], in0=pp[:, :].rearrange("p (a b) -> p a b", a=4),
                            in1=gnw8[:, half * 4:(half + 1) * 4].unsqueeze(2).to_broadcast([128, 4, 128]), op=ALU.mult),
                            reads=[pr, "gnw8"], writes=["ogT"])

                chk("gla")
                for i in range(2):
                    wt, wr = wload("w_attn_o", 0, 8, i * 512, 512)
                    for c in range(4):
                        cc = i * 4 + c
                        fm_proj(wt, wr, c * 128, 128, oaT, "oaT", 512,
                                lambda pp, pr, cc=cc: S.dve(lambda e: e.tensor_tensor(out=m1[:, cc, :], in0=pp[:, :], in1=gT[:, cc, :], op=ALU.mult),
                                                            reads=[pr, "gT"], writes=["sg"]))
                for i in range(2):
                    wt, wr = wload("w_gla_o", 0, 8, i * 512, 512)
                    for c in range(4):
                        cc = i * 4 + c

                        def ev(pp, pr, cc=cc):
                            S.dve(lambda e: e.tensor_tensor(out=mtmp[:], in0=pp[:, :], in1=gT[:, 8 + cc, :], op=ALU.mult),
                                  reads=[pr, "gT"], writes=["Gf"])
                            S.dve(lambda e: e.tensor_tensor(out=mT[:, cc, :], in0=mtmp[:], in1=m1[:, cc, :], op=ALU.add),
                                  reads=["Gf", "sg"], writes=["qT"])
                        fm_proj(wt, wr, c * 128, 128, ogT, "ogT", 512, ev)
                for n in range(2):
                    wt, wr = wload("w_mix_o", 0, 8, n * 512, 512)
                    for t in range(4):
                        tm_proj(mT[:, :, t * 128:(t + 1) * 128], "qT", wt, wr, 0, 512,
                                lambda pp, pr, t=t, n=n: S.dve(lambda e: e.tensor_tensor(
                                    out=xg[:, t, n * 512:(n + 1) * 512], in0=pp[:, :], in1=xg[:, t, n * 512:(n + 1) * 512], op=ALU.add),
                                    reads=[pr, "xg%d" % t], writes=["xg%d" % t]))

                for t in range(4):
                    dump("x1_%d_%d" % (grp, t), xg[:, t, :], 1024, ["xg%d" % t])
                chk("merge")
                for t in range(4):
                    norm_T(xg[:, t, :], "xg%d" % t, wn2, "wn", hT[:, :, t * 128:(t + 1) * 128], "hT%d" % t)
                for c in range(2):
                    fm_proj(wmq, "wmq", c * 128, 128, hT, HT_ALL, 512,
                            lambda pp, pr, c=c: act(qmT[:, c, :], pp[:, :], AF.Copy, [pr], ["qmT", "vg"]))
                for t in range(4):
                    for r in range(2):
                        pp, pr = newps("r0" if r == 0 else "r1")
                        for mt in range(2):
                            for c in range(2):
                                col = (mt * 2 + c) * 128
                                mm(pp[:, col:col + 128], kmT[64 * r:64 * r + 64, c, mt * 128:(mt + 1) * 128],
                                   qmT[64 * r:64 * r + 64, c, t * 128:(t + 1) * 128], True, True, ["kmT", "qmT"], [pr])
                        act(PmT[:, r, :, :, :], pp[:, :].rearrange("p (m c q) -> p m c q", m=2, c=2), AF.Exp, [pr], ["PmT", "vg"], scale=0.125)
                    po, por = newps()
                    for c in range(2):
                        for r in range(2):
                            h = 2 * c + r
                            for mt in range(2):
                                mm(po[:, h * 65:(h + 1) * 65], PmT[:, r, mt, c, :], vmaug[:, mt, h, :], mt == 0, mt == 1, ["PmT", "vmaug"], [por])
                    pov = po[:, 0:260].rearrange("p (h e) -> p h e", h=4)
                    S.dve(lambda e, pov=pov: e.reciprocal(out=den[:, 0:4], in_=pov[:, :, 64]), reads=[por], writes=["den"])
                    S.dve(lambda e, pov=pov: e.tensor_tensor(out=om[:, :].rearrange("p (h d) -> p h d", h=4), in0=pov[:, :, 0:64],
                                                             in1=den[:, 0:4].unsqueeze(2).to_broadcast([128, 4, 64]), op=ALU.mult),
                          reads=[por, "den"], writes=["om"])
                    pp, pr = newps()
                    for k in range(2):
                        tr(pp[:, k * 128:(k + 1) * 128], om[:, k * 128:(k + 1) * 128], ["om"], [pr])
                    act(omT[:, :, t * 128:(t + 1) * 128], pp[:, 0:256].rearrange("p (a b) -> p a b", a=2), AF.Copy, [pr], ["omT", "vg"])
                for t in range(4):
                    for n in range(2):
                        pp, pr = newps()
                        for c in range(2):
                            mm(pp[:, :], omT[:, c, t * 128:(t + 1) * 128], wmo[:, c, n * 512:(n + 1) * 512], c == 0, c == 1, ["omT", "wmo"], [pr])
                        S.dve(lambda e, pp=pp, t=t, n=n: e.tensor_tensor(
                            out=xg[:, t, n * 512:(n + 1) * 512], in0=pp[:, :], in1=xg[:, t, n * 512:(n + 1) * 512], op=ALU.add),
                            reads=[pr, "xg%d" % t], writes=["xg%d" % t])

                for t in range(4):
                    dump("x2_%d_%d" % (grp, t), xg[:, t, :], 1024, ["xg%d" % t])
                chk("xattn")
                for t in range(4):
                    norm_T(xg[:, t, :], "xg%d" % t, wn3, "wn", hT[:, :, t * 128:(t + 1) * 128], "hT%d" % t)
                for i in range(6):
                    ncol = 512 if i < 5 else 256
                    wg_, wgr = wload("w_ffn_gate", 0, 8, i * 512, ncol)
                    wu_, wur = wload("w_ffn_up", 0, 8, i * 512, ncol)
                    for c in range(ncol // 128):
                        fc = i * 4 + c
                        slt, slr = sl[fc % 2], "sl%d" % (fc % 2)
                        fm_proj(wg_, wgr, c * 128, 128, hT, HT_ALL, 512,
                                lambda pp, pr, slt=slt, slr=slr: act(slt[:], pp[:, :], AF.Silu, [pr], [slr]))
                        fm_proj(wu_, wur, c * 128, 128, hT, HT_ALL, 512,
                                lambda pp, pr, slt=slt, slr=slr, fc=fc: S.dve(lambda e: e.tensor_tensor(
                                    out=actT[:, fc, :], in0=pp[:, :], in1=slt[:], op=ALU.mult), reads=[pr, slr], writes=["big"]))
                for n in range(2):
                    pst = [newps() for _ in range(4)]
                    for piece, (k0, nk) in enumerate(((0, 8), (8, 8), (16, 6))):
                        wt, wr = wload("w_ffn_down", k0, nk, n * 512, 512)
                        for t in range(4):
                            pp, pr = pst[t]
                            for k in range(nk):
                                kk = k0 + k
                                mm(pp[:, :], actT[:, kk, t * 128:(t + 1) * 128], wt[:, k, :], kk == 0, kk == 21, ["big", wr], [pr])
                    for t in range(4):
                        pp, pr = pst[t]
                        S.dve(lambda e, pp=pp, t=t, n=n: e.tensor_tensor(
                            out=xg[:, t, n * 512:(n + 1) * 512], in0=pp[:, :], in1=xg[:, t, n * 512:(n + 1) * 512], op=ALU.add),
                            reads=[pr, "xg%d" % t], writes=["xg%d" % t])

                chk("ffn")
                for t in range(4):
                    ys_, ysr = XS[t % 2]
                    act(junk[:], xg[:, t, :], AF.Square, ["xg%d" % t], ["junk", "ss2"], accum_out=ss[:, 2:3])
                    rstd_from_ss(2, 1.0 / D, sres="ss2", rres="rs2")
                    S.dve(lambda e, t=t, ys_=ys_: e.scalar_tensor_tensor(out=ys_[:], in0=xg[:, t, :], scalar=rs[:, 2:3], in1=wfin[:],
                                                                         op0=ALU.mult, op1=ALU.mult),
                          reads=["xg%d" % t, "rs2", "wfin"], writes=[ysr])
                    row0 = (grp * 4 + t) * 128
                    ld("sp", yout[row0:row0 + 128, :], ys_[:], "yout%d" % (t % 2), [], reads=[ysr])

        except _Stop:
            pass
        S.emit(final_wait_keys=[k for k in ["yout0", "yout1"] + dbg_keys if k in S.dma_count])
    return nc


_CONST_CACHE = {}


def _consts():
    if not _CONST_CACHE:
        ident = np.eye(128, dtype=np.float32)
        a = np.arange(128)
        u64 = ((a[:, None] // 64 == a[None, :] // 64) & (a[:, None] > a[None, :])).astype(np.float32) * (-1.0 / 16)
        u128 = (a[:, None] > a[None, :]).astype(np.float32) * (-1.0 / 16)
        ind2 = np.zeros((128, 2), np.float32)
        ind2[:64, 0] = -1.0 / 16
        ind2[64:, 1] = -1.0 / 16
        ind1 = np.full((128, 2), -1.0 / 16, np.float32)
        _CONST_CACHE.update(ident=ident, ucum64=u64, ucum128=u128, ind2=ind2, ind1=ind1)
    return _CONST_CACHE


def make_in_maps(inputs, ncore=NCORE, tok_core=TOK_CORE, npre_tiles=48):
    x = np.asarray(inputs["x"], dtype=np.float32)
    memv = np.asarray(inputs["mem"], dtype=np.float32)
    B, SEQ, _ = x.shape
    per_b = SEQ // tok_core
    c = _consts()
    maps = []
    for core in range(ncore):
        b, j = core // per_b, core % per_b
        npre = npre_tiles * 128
        xa = np.zeros((npre + tok_core, D), np.float32)
        end = (j + 1) * tok_core
        start = max(0, j * tok_core - npre)
        seg = x[b, start:end]
        xa[npre + tok_core - seg.shape[0]:] = seg
        m = dict(xall=xa, mem=np.ascontiguousarray(memv[b]),
                 flagb=np.full((128, 1), 0.0 if j > 0 else -30000.0, np.float32))
        m.update(c)
        for k, v in inputs.items():
            if k in ("x", "mem"):
                continue
            v = np.asarray(v, dtype=np.float32)
            m[k] = np.ascontiguousarray(v[0]) if k != "norm_final_w" else np.ascontiguousarray(v)
        maps.append(m)
    return maps


_NC_CACHE = {}


def kernel(**inputs):
    if "nc" not in _NC_CACHE:
        _NC_CACHE["nc"] = build()
    nc = _NC_CACHE["nc"]
    maps = make_in_maps(inputs)
    res = run_bass_kernel_spmd(nc, maps, core_ids=list(range(NCORE)))
    x = inputs["x"]
    B, SEQ, _ = x.shape
    per_b = SEQ // TOK_CORE
    out = np.empty((B, SEQ, D), np.float32)
    for core in range(NCORE):
        b, j = core // per_b, core % per_b
        out[b, j * TOK_CORE:(j + 1) * TOK_CORE] = res.results[core]["y"]
    return out
```

```python
import contextlib
import numpy as np
import concourse.bass as bass
import concourse.mybir as mybir
from concourse.bass_utils import run_bass_kernel_spmd

F32 = mybir.dt.float32
BF16 = mybir.dt.bfloat16
AF = mybir.ActivationFunctionType
ALU = mybir.AluOpType

D = 1024
NCORE = 8
TOK_CORE = 2048
DFF = 2816
INC = 6416
O_QA, O_KA, O_VA, O_QG, O_KG, O_VG, O_GG, O_AL, O_GT = 0, 1024, 1152, 1280, 1792, 2304, 3328, 4352, 4368


class _Stop(Exception):
    pass


class Sched:
    ENGS = ("pe", "act", "dve", "pool", "sp")

    def __init__(self, nc, same_engine_sync=True):
        self.nc = nc
        self.ops = []
        self.last_write = {}
        self.reads_since = {}
        self.dma_count = {}
        self.same_engine_sync = same_engine_sync

    def _add(self, eng, fn, reads, writes, dma_key=None):
        idx = len(self.ops)
        deps = set()
        for r in reads:
            lw = self.last_write.get(r)
            if lw is not None:
                deps.add(lw)
        for w in writes:
            lw = self.last_write.get(w)
            if lw is not None:
                deps.add(lw)
            for rd in self.reads_since.get(w, ()):
                deps.add(rd)
        for r in reads:
            self.reads_since.setdefault(r, []).append(idx)
        for w in writes:
            self.last_write[w] = idx
            self.reads_since[w] = []
        deps.discard(idx)
        op = dict(eng=eng, fn=fn, deps=deps, dma_key=dma_key, signal=False, idx=idx)
        if dma_key is not None:
            c = self.dma_count.get(dma_key, 0) + 16
            self.dma_count[dma_key] = c
            op["dma_val"] = c
        self.ops.append(op)
        return idx

    def pe(self, fn, reads=(), writes=()):
        return self._add("pe", fn, reads, writes)

    def act(self, fn, reads=(), writes=()):
        return self._add("act", fn, reads, writes)

    def dve(self, fn, reads=(), writes=()):
        return self._add("dve", fn, reads, writes)

    def pool(self, fn, reads=(), writes=()):
        return self._add("pool", fn, reads, writes)

    def dma(self, eng, fn, key, reads=(), writes=()):
        return self._add(eng, fn, reads, writes, dma_key=key)

    def emit(self, final_wait_keys=()):
        nc = self.nc
        ops = self.ops
        need = []
        for op in ops:
            nd = []
            for d in sorted(op["deps"]):
                y = ops[d]
                if y["dma_key"] is None and y["eng"] == op["eng"] and op["dma_key"] is None:
                    if op["eng"] == "pe" or not self.same_engine_sync:
                        continue
                nd.append(d)
                if y["dma_key"] is None:
                    y["signal"] = True
            need.append(nd)
        cnt = {e: 0 for e in self.ENGS}
        for op in ops:
            if op["dma_key"] is None and op["signal"]:
                cnt[op["eng"]] += 1
                op["sig_val"] = cnt[op["eng"]]
        with contextlib.ExitStack() as st:
            esem = {e: st.enter_context(nc.semaphore("s_" + e)) for e in self.ENGS}
            dsem = {k: st.enter_context(nc.semaphore("d_%d" % i)) for i, k in enumerate(self.dma_count)}
            block = st.enter_context(nc.Block())
            per = {e: [] for e in self.ENGS}
            for op, nd in zip(ops, need):
                per[op["eng"]].append((op, nd))

            def body(ename, handle):
                waited = {}
                for op, nd in per[ename]:
                    for d in nd:
                        y = ops[d]
                        if y["dma_key"] is not None:
                            sem, val, k = dsem[y["dma_key"]], y["dma_val"], ("d", y["dma_key"])
                        else:
                            sem, val, k = esem[y["eng"]], y["sig_val"], ("e", y["eng"])
                        if waited.get(k, 0) >= val:
                            continue
                        waited[k] = val
                        handle.wait_ge(sem, val)
                    ins = op["fn"](handle)
                    if op["dma_key"] is not None:
                        ins.then_inc(dsem[op["dma_key"]], 16)
                    elif op["signal"]:
                        ins.then_inc(esem[ename], 1)
                if ename == "sp":
                    for k in final_wait_keys:
                        handle.wait_ge(dsem[k], self.dma_count[k])

            @block.tensor
            def _(e):
                body("pe", e)

            @block.scalar
            def _(e):
                body("act", e)

            @block.vector
            def _(e):
                body("dve", e)

            @block.gpsimd
            def _(e):
                body("pool", e)

            @block.sync
            def _(e):
                body("sp", e)


def build(NPRE=48, NGRP=4, NSLOT=3, same_engine_sync=True, dbg=False, stop_after=None):
    nc = bass.Bass("TRN2", target_bir_lowering=False)
    NT_ALL = NPRE + NGRP * 4

    def din(name, shape):
        return nc.dram_tensor(name, list(shape), F32, kind="ExternalInput").ap()

    xall = din("xall", [NT_ALL * 128, D])
    mem = din("mem", [256, D])
    flagb = din("flagb", [128, 1])
    ident_d = din("ident", [128, 128])
    ucum64_d = din("ucum64", [128, 128])
    ucum128_d = din("ucum128", [128, 128])
    ind2_d = din("ind2", [128, 2])
    ind1_d = din("ind1", [128, 2])
    norm_mix_w = din("norm_mix_w", [D])
    w_in = din("w_in", [D, INC])
    b_gate = din("b_gate", [2 * D])
    attn_sinks = din("attn_sinks", [16])
    gla_gate_w2 = din("gla_gate_w2", [16, 512])
    gla_gate_b = din("gla_gate_b", [512])
    gla_norm_w = din("gla_norm_w", [256])
    w_attn_o = din("w_attn_o", [D, D])
    w_gla_o = din("w_gla_o", [D, D])
    w_mix_o = din("w_mix_o", [D, D])
    norm_mem_q_w = din("norm_mem_q_w", [D])
    norm_mem_kv_w = din("norm_mem_kv_w", [D])
    w_mem_q = din("w_mem_q", [D, 256])
    w_mem_kv = din("w_mem_kv", [D, 512])
    w_mem_o = din("w_mem_o", [256, D])
    norm_ffn_w = din("norm_ffn_w", [D])
    w_ffn_gate = din("w_ffn_gate", [D, DFF])
    w_ffn_up = din("w_ffn_up", [D, DFF])
    w_ffn_down = din("w_ffn_down", [DFF, D])
    norm_final_w = din("norm_final_w", [D])
    yout = nc.dram_tensor("y", [NGRP * 512, D], F32, kind="ExternalOutput").ap()
    WSRC = {"w_in": w_in, "w_attn_o": w_attn_o, "w_gla_o": w_gla_o, "w_mix_o": w_mix_o,
            "w_ffn_gate": w_ffn_gate, "w_ffn_up": w_ffn_up, "w_ffn_down": w_ffn_down, "w_mem_kv": w_mem_kv}
    WB = {}
    for nm_ in ("w_in", "w_attn_o", "w_gla_o", "w_mix_o", "w_ffn_gate", "w_ffn_up", "w_ffn_down"):
        WB[nm_] = nc.dram_tensor("wb_" + nm_, list(WSRC[nm_].shape), BF16, kind="Internal").ap()

    S = Sched(nc, same_engine_sync=same_engine_sync)
    st = contextlib.ExitStack()
    with st:
        def sb(name, shape, dt=F32):
            return st.enter_context(nc.sbuf_tensor("sb_" + name, list(shape), dt))

        psb = [st.enter_context(nc.psum_tensor("ps%d" % i, [128, 512], F32)) for i in range(8)]
        pcnt = {"any": 0, "r0": 0, "r1": 0}
        PSETS = {"any": [0, 1, 2, 3, 4, 5, 6, 7], "r0": [0, 1, 4, 5], "r1": [2, 3, 6, 7]}

        def newps(kind="any"):
            lst = PSETS[kind]
            i = lst[pcnt[kind] % len(lst)]
            pcnt[kind] += 1
            return psb[i], "ps%d" % i

        idt = sb("idt", [128, 128])
        uc64 = sb("uc64", [128, 128])
        uc128 = sb("uc128", [128, 128])
        ind2 = sb("ind2", [128, 2])
        ind1 = sb("ind1", [128, 2])
        flg = sb("flg", [128, 1])
        vst = [sb("vst%d" % i, [16, 128]) for i in range(2)]
        wn1 = sb("wn1", [128, 8])
        wn2 = sb("wn2", [128, 8])
        wn3 = sb("wn3", [128, 8])
        wnkv = sb("wnkv", [128, 8])
        gnw8 = sb("gnw8", [128, 8])
        bgate = sb("bgate", [128, 16])
        exps = sb("exps", [128, 16])
        wfin = sb("wfin", [128, D])
        w2aug = sb("w2aug", [32, 512], BF16)
        wkdup = sb("wkdup", [128, 8, 2, 128], BF16)
        wva = sb("wva", [128, 8, 128], BF16)
        walr = sb("walr", [128, 8, 16], BF16)
        wmq = sb("wmq", [128, 8, 256], BF16)
        wmo = sb("wmo", [128, 2, D], BF16)
        big = sb("big", [128, 8 * 1536], BF16)
        wpre = big[:, :].rearrange("p (k c) -> p k c", k=8)
        actT = big[:, 0:22 * 512].rearrange("p (f t) -> p f t", f=22)
        kmT = sb("kmT", [128, 2, 256], BF16)
        vmaug = sb("vmaug", [128, 2, 4, 65], BF16)
        wsl = [sb("wsl%d" % i, [128, 8, 512], BF16) for i in range(NSLOT)]
        xg = sb("xg", [128, 4, D])
        xs = sb("xs", [128, D])
        junk = sb("junk", [128, D], BF16)
        ss = sb("ss", [128, 8])
        rs = sb("rs", [128, 8])
        hT = sb("hT", [128, 8, 512], BF16)
        qT = sb("qT", [128, 8, 512], BF16)
        kT = sb("kT", [128, 2, 640], BF16)
        vaug = sb("vaug", [128, 5, 2, 65], BF16)
        PT = [sb("PT%d" % i, [128, 2, 4, 128], BF16) for i in range(2)]
        oa = sb("oa", [128, D])
        oaT = sb("oaT", [128, 8, 512], BF16)
        den = sb("den", [128, 16])
        qgT = sb("qgT", [128, 4, 512], BF16)
        alrT = sb("alrT", [32, 512], BF16)
        kg = sb("kg", [128, 4, 512])
        vg = sb("vg", [128, 4, D], BF16)
        sg = sb("sg", [128, 4, D], BF16)
        Gf = sb("Gf", [128, 512])
        Dm = sb("Dm", [128, 512])
        av = sb("av", [128, 8])
        kdec = sb("kdec", [128, 512], BF16)
        Sst = sb("Sst", [128, D])
        Sbf = [sb("Sbf%d" % i, [128, D], BF16) for i in range(2)]
        og = sb("og", [128, D])
        ogT = sb("ogT", [128, 8, 512], BF16)
        gT = sb("gT", [128, 16, 512], BF16)
        m1 = sg[:, :, :].rearrange("p a (b c) -> p (a b) c", b=2)
        mtmp = Gf
        mT = qT
        qmT = vg[:, 0, :].rearrange("p (a b) -> p a b", a=2)
        PmT = vg[:, 1, :].rearrange("p (r m c q) -> p r m c q", r=2, m=2, c=2)
        omT = vg[:, 2, :].rearrange("p (a b) -> p a b", a=2)
        om = sb("om", [128, 256])
        sl = [sb("sl%d" % i, [128, 512], BF16) for i in range(2)]
        yst = xs
        xm = oa
        xs2 = sb("xs2", [128, D])
        av2 = sb("av2", [128, 8])
        XS = [(xs, "xs0"), (xs2, "xs1")]
        xm1 = gT[:, 0:4, :].rearrange("p a b -> p (a b)").bitcast(F32)
        Gf1 = gT[:, 4:6, :].rearrange("p a b -> p (a b)").bitcast(F32)
        Dm1 = gT[:, 6:8, :].rearrange("p a b -> p (a b)").bitcast(F32)
        kdec1 = gT[:, 8, :]
        PRE_RES = ["pxm1", "pGf1", "pDm1", "pkdec1"]
        GB = [dict(Gf=Gf[:], Dm=Dm[:], av=av, kdec=kdec[:], rGf="Gf", rDm="Dm", rav="av", rkdec="kdec"),
              dict(Gf=Gf1, Dm=Dm1, av=av2, kdec=kdec1, rGf="pGf1", rDm="pDm1", rav="av2", rkdec="pkdec1")]

        HT_ALL = ["hT0", "hT1", "hT2", "hT3"]

        def chk(name):
            if stop_after == name:
                raise _Stop()

        dbg_keys = []

        def dump(name, ap, ncols, reads):
            if not dbg:
                return
            d_ = nc.dram_tensor("dbg_" + name, [128, ncols], F32, kind="ExternalOutput").ap()
            k_ = "dbg_" + name
            dbg_keys.append(k_)
            S.dma("sp", lambda e: e.dma_start(out=d_, in_=ap), k_, reads=reads)

        ukey = [0]

        def ld(eng, out_ap, in_ap, key, writes, reads=(), slow=False):
            if key in ("c", "cw"):
                ukey[0] += 1
                if key == "cw":
                    reads = list(reads) + ["cwtok%d" % (ukey[0] % 2)]
                    writes = list(writes) + ["cwtok%d" % (ukey[0] % 2)]
                key = "%s%d" % (key, ukey[0])
            if slow:
                S.dma(eng, lambda e: e.dma_start(out=out_ap, in_=in_ap, allow_slow_non_contiguous=True), key, reads=reads, writes=writes)
            else:
                S.dma(eng, lambda e: e.dma_start(out=out_ap, in_=in_ap), key, reads=reads, writes=writes)

        slot_i = [0]

        def wload(wname, k0, nk, c0, ncols):
            i = slot_i[0] % NSLOT
            slot_i[0] += 1
            W = WB.get(wname, WSRC[wname])
            src = W[k0 * 128:(k0 + nk) * 128, c0:c0 + ncols].rearrange("(k p) c -> p k c", p=128)
            ld("pool", wsl[i][:, 0:nk, 0:ncols], src, "wsl%d" % i, writes=["wsl%d" % i], reads=["wb_" + wname])
            return wsl[i], "wsl%d" % i

        def mm(out_ap, lhsT, rhs, start, stop, reads, writes):
            S.pe(lambda e: e.matmul(out=out_ap, lhsT=lhsT, rhs=rhs, start=start, stop=stop), reads=reads, writes=writes)

        def tr(out_ap, in_ap, reads, writes):
            S.pe(lambda e: e.transpose(out=out_ap, in_=in_ap, identity=idt[:]), reads=list(reads) + ["idt"], writes=writes)

        def act(out_ap, in_ap, func, reads, writes, **kw):
            S.act(lambda e: e.activation(out=out_ap, in_=in_ap, func=func, **kw), reads=reads, writes=writes)

        def rstd_from_ss(col, n_inv, ncol=1, sres="ss", rres="rs"):
            act(rs[:, col:col + ncol], ss[:, col:col + ncol], AF.Ln, [sres], [rres], scale=n_inv, bias=1e-6)
            act(rs[:, col:col + ncol], rs[:, col:col + ncol], AF.Exp, [rres], [rres], scale=-0.5)

        nbuf = [0]

        def norm_T(x_ap, xres, wn, wnres, dst, dstres):
            b_ = nbuf[0] % 2
            nbuf[0] += 1
            xs_, xsr = XS[b_]
            sres, rres = "ssn%d" % b_, "rsn%d" % b_
            act(junk[:], x_ap, AF.Square, [xres], ["junk", sres], accum_out=ss[:, b_:b_ + 1])
            rstd_from_ss(b_, 1.0 / D, sres=sres, rres=rres)
            S.dve(lambda e: e.tensor_scalar(out=xs_[:], in0=x_ap, scalar1=rs[:, b_:b_ + 1], scalar2=None, op0=ALU.mult),
                  reads=[xres, rres], writes=[xsr])
            for half in range(2):
                pp, pr = newps()
                for k in range(4):
                    kk = half * 4 + k
                    tr(pp[:, k * 128:(k + 1) * 128], xs_[:, kk * 128:(kk + 1) * 128], [xsr], [pr])
                S.dve(lambda e, pp=pp, half=half: e.tensor_tensor(
                    out=dst[:, half * 4:(half + 1) * 4, :], in0=pp[:, :].rearrange("p (a b) -> p a b", a=4),
                    in1=wn[:, half * 4:(half + 1) * 4].unsqueeze(2).to_broadcast([128, 4, 128]), op=ALU.mult),
                    reads=[pr, wnres], writes=[dstres])

        def norm_A1(x_ap, xres, b_):
            xs_, xsr = XS[b_]
            sres, rres = "ssn%d" % b_, "rsn%d" % b_
            act(junk[:], x_ap, AF.Square, [xres], ["junk", sres], accum_out=ss[:, b_:b_ + 1])
            rstd_from_ss(b_, 1.0 / D, sres=sres, rres=rres)
            S.dve(lambda e: e.tensor_scalar(out=xs_[:], in0=x_ap, scalar1=rs[:, b_:b_ + 1], scalar2=None, op0=ALU.mult),
                  reads=[xres, rres], writes=[xsr])

        def norm_A2(b_, wn, wnres, dst, dstres):
            xs_, xsr = XS[b_]
            for half in range(2):
                pp, pr = newps()
                for k in range(4):
                    kk = half * 4 + k
                    tr(pp[:, k * 128:(k + 1) * 128], xs_[:, kk * 128:(kk + 1) * 128], [xsr], [pr])
                S.dve(lambda e, pp=pp, half=half: e.tensor_tensor(
                    out=dst[:, half * 4:(half + 1) * 4, :], in0=pp[:, :].rearrange("p (a b) -> p a b", a=4),
                    in1=wn[:, half * 4:(half + 1) * 4].unsqueeze(2).to_broadcast([128, 4, 128]), op=ALU.mult),
                    reads=[pr, wnres], writes=[dstres])

        def norm_group4(wn):
            norm_A1(xg[:, 0, :], "xg0", 0)
            for t in range(4):
                if t + 1 < 4:
                    norm_A1(xg[:, t + 1, :], "xg%d" % (t + 1), (t + 1) % 2)
                norm_A2(t % 2, wn, "wn", hT[:, :, t * 128:(t + 1) * 128], "hT%d" % t)

        def tm_proj(lhs_tile, lhs_res, wt, wres, c0, ncols, evac):
            pp, pr = newps()
            for k in range(8):
                mm(pp[:, 0:ncols], lhs_tile[:, k, :], wt[:, k, c0:c0 + ncols], k == 0, k == 7, [lhs_res, wres], [pr])
            evac(pp, pr)

        def fm_proj(wt, wres, c0, m, rhs, rhs_res, ntok, evac, nk=8):
            pp, pr = newps()
            rr = list(rhs_res) if isinstance(rhs_res, (list, tuple)) else [rhs_res]
            for k in range(nk):
                mm(pp[0:m, 0:ntok], wt[:, k, c0:c0 + m], rhs[:, k, 0:ntok], k == 0, k == nk - 1, rr + [wres], [pr])
            evac(pp, pr)

        def gla_gate(tokc0, umat, ures, indt, indres, B=None, alr_ap=None, alr_res="alrT"):
            B = B or GB[0]
            Gf, Dm, av = B["Gf"], B["Dm"], B["av"]
            rGf, rDm, rav = B["rGf"], B["rDm"], B["rav"]
            alr_ap = alr_ap if alr_ap is not None else alrT[0:17, tokc0:tokc0 + 128]
            pz, pzr = newps()
            mm(pz[:, :], alr_ap, w2aug[0:17, :], True, True, [alr_res, "w2aug"], [pzr])
            act(Gf, pz[:, :], AF.Exp, [pzr], [rGf], scale=-1.0)
            act(Gf, Gf, AF.Ln, [rGf], [rGf], bias=1.0)
            pR, pRr = newps()
            mm(pR[:, :], umat[:], Gf, True, True, [ures, rGf], [pRr])
            pa, par = newps()
            for h in range(4):
                mm(pa[:, 2 * h:2 * h + 2], Gf[:, h * 128:(h + 1) * 128], indt[:], True, True, [rGf, indres], [par])
            act(Dm, pR[:, :], AF.Exp, [pRr], [rDm])
            act(av[:], pa[:, 0:8], AF.Exp, [par], [rav])

        def state_update(krows, j, vsrc, vres, kind, B=None):
            B = B or GB[0]
            kdec, av = B["kdec"], B["av"]
            rkdec, rav = B["rkdec"], B["rav"]
            r0, r1 = krows
            pA, pAr = newps(kind)
            pB, pBr = newps(kind)
            for h in range(4):
                pp, pr = (pA, pAr) if h < 2 else (pB, pBr)
                mm(pp[:, (h % 2) * 256:(h % 2 + 1) * 256], kdec[r0:r1, h * 128:(h + 1) * 128],
                   vsrc[r0:r1, h * 256:(h + 1) * 256], True, True, [rkdec, vres], [pr])
            for h in range(4):
                pp, pr = (pA, pAr) if h < 2 else (pB, pBr)
                S.dve(lambda e, h=h, pp=pp: e.scalar_tensor_tensor(
                    out=Sst[:, h * 256:(h + 1) * 256], in0=Sst[:, h * 256:(h + 1) * 256], scalar=av[:, 2 * h + j:2 * h + j + 1],
                    in1=pp[:, (h % 2) * 256:(h % 2 + 1) * 256], op0=ALU.mult, op1=ALU.add),
                    reads=["Sst", rav, pr], writes=["Sst"])

        try:
            ld("sp", idt[:], ident_d, "c", ["idt"])
            ld("sp", uc64[:], ucum64_d, "c", ["uc64"])
            ld("sp", uc128[:], ucum128_d, "c", ["uc128"])
            ld("sp", ind2[:], ind2_d, "c", ["ind2"])
            ld("sp", ind1[:], ind1_d, "c", ["ind1"])
            ld("sp", flg[:], flagb, "c", ["flg"])
            for vi, (t_, src_, kk_, res_) in enumerate(((wn1, norm_mix_w, 8, "wn"), (wn2, norm_mem_q_w, 8, "wn"), (wn3, norm_ffn_w, 8, "wn"),
                                                      (wnkv, norm_mem_kv_w, 8, "wn"), (bgate, b_gate, 16, "bgate"), (gnw8, gla_norm_w, 2, "gnw8"))):
                stg = vst[vi % 2]
                sres = "vst%d" % (vi % 2)
                ld("sp", stg[0:kk_, :], src_.rearrange("(k p) -> k p", p=128), "c", [sres])
                pp, pr = newps()
                S.pe(lambda e, pp=pp, stg=stg, kk_=kk_: e.transpose(out=pp[:, 0:kk_], in_=stg[0:kk_, :], identity=idt[0:kk_, 0:kk_]),
                     reads=[sres, "idt"], writes=[pr])
                if kk_ == 2:
                    for i in range(4):
                        act(t_[:, 2 * i:2 * i + 2], pp[:, 0:2], AF.Copy, [pr], [res_])
                else:
                    act(t_[:, 0:kk_], pp[:, 0:kk_], AF.Copy, [pr], [res_])
            ld("sp", exps[:], attn_sinks.partition_broadcast(128), "c", ["exps"])
            ld("sp", wfin[:], norm_final_w.partition_broadcast(128), "c", ["wfin"])
            CONSTS = ["idt", "uc64", "uc128", "ind2", "ind1", "flg", "wn", "gnw8", "bgate", "exps", "wfin"]
            act(exps[:], exps[:], AF.Exp, ["exps"], ["exps"])
            ld("pool", w2aug[0:16, :], gla_gate_w2, "cw", ["w2aug"])
            ld("pool", w2aug[16:17, :], gla_gate_b.rearrange("(o c) -> o c", o=1), "cw", ["w2aug"])
            for g in range(2):
                for r in range(2):
                    ld("pool", wkdup[:, :, g, 64 * r:64 * r + 64],
                       w_in[:, O_KA + 64 * g:O_KA + 64 * g + 64].rearrange("(k p) c -> p k c", p=128), "cw", ["wkdup"])
            ld("pool", wva[:], w_in[:, O_VA:O_VA + 128].rearrange("(k p) c -> p k c", p=128), "cw", ["wva"])
            ld("pool", walr[:], w_in[:, O_AL:O_AL + 16].rearrange("(k p) c -> p k c", p=128), "cw", ["walr"])
            ld("pool", wmq[:], w_mem_q.rearrange("(k p) c -> p k c", p=128), "cw", ["wmq"])
            ld("pool", wmo[:], w_mem_o.rearrange("(k p) c -> p k c", p=128), "cw", ["wmo"])
            for i in range(3):
                ld("pool", wpre[:, :, i * 512:(i + 1) * 512],
                   w_in[:, O_KG + i * 512:O_KG + (i + 1) * 512].rearrange("(k p) c -> p k c", p=128), "cw", ["big"])
            S.dve(lambda e: e.memset(Sst[:], 0.0), writes=["Sst"])
            S.dve(lambda e: e.memset(alrT[:], 1.0), writes=["alrT"])
            S.dve(lambda e: e.memset(vaug[:], 1.0), writes=["vaug"])
            S.dve(lambda e: e.memset(vmaug[:], 1.0), writes=["vmaug"])
            for i in range(2):
                S.dve(lambda e, i=i: e.memset(PT[i][:], 0.0), writes=["PT%d" % i])

            for mt in range(2):
                ld("sp", xm[:], mem[mt * 128:(mt + 1) * 128, :], "xm", ["oa"])
                norm_T(xm[:], "oa", wnkv, "wn", hT[:, :, mt * 128:(mt + 1) * 128], "hT%d" % mt)
            wt, wr = wload("w_mem_kv", 0, 8, 0, 512)
            for c in range(2):
                fm_proj(wt, wr, c * 128, 128, hT, ["hT0", "hT1"], 256,
                        lambda pp, pr, c=c: act(kmT[:, c, :], pp[:, 0:256], AF.Copy, [pr], ["kmT"]))
            for mt in range(2):
                tm_proj(hT[:, :, mt * 128:(mt + 1) * 128], "hT%d" % mt, wt, wr, 256, 256,
                        lambda pp, pr, mt=mt: act(vmaug[:, mt, :, 0:64], pp[:, 0:256].rearrange("p (h d) -> p h d", h=4), AF.Copy, [pr], ["vmaug"]))

            for nm_ in ("w_in", "w_attn_o", "w_gla_o", "w_mix_o", "w_ffn_gate", "w_ffn_up", "w_ffn_down"):
                src_, dst_ = WSRC[nm_], WB[nm_]
                for rb in range(src_.shape[0] // 128):
                    S.dma("pool", lambda e, src_=src_, dst_=dst_, rb=rb: e.dma_start(
                        out=dst_[rb * 128:(rb + 1) * 128, :], in_=src_[rb * 128:(rb + 1) * 128, :], max_dma_last_dim=4096),
                        "cast", reads=["casttok"], writes=["casttok", "wb_" + nm_])
            chk("setup")
            def hslot(t):
                return hT[:, :, (t % 4) * 128:(t % 4 + 1) * 128], "hT%d" % (t % 4)

            def bufs(t):
                b = t % 2
                kgr, vgr, alr = (("kg", "vg", "alrT"), ("kgB", "vgB", "alrTB"))[b]
                return b, kgr, vgr, alr, GB[b]

            def pre_A1(t):
                b = t % 2
                xmb, xmr = ((xm[:], "oa"), (xm1, "pxm1"))[b]
                xs_, xsr = XS[b]
                sres, rres = "ssn%d" % b, "rsn%d" % b
                ld("sp", xmb, xall[t * 128:(t + 1) * 128, :], "xm%d" % b, [xmr])
                act(junk[:], xmb, AF.Square, [xmr], ["junk", sres], accum_out=ss[:, b:b + 1])
                rstd_from_ss(b, 1.0 / D, sres=sres, rres=rres)
                S.dve(lambda e: e.tensor_scalar(out=xs_[:], in0=xmb, scalar1=rs[:, b:b + 1], scalar2=None, op0=ALU.mult),
                      reads=[xmr, rres], writes=[xsr])

            def pre_A2(t):
                xs_, xsr = XS[t % 2]
                dst, dstres = hslot(t)
                for half in range(2):
                    pp, pr = newps()
                    for k in range(4):
                        kk = half * 4 + k
                        tr(pp[:, k * 128:(k + 1) * 128], xs_[:, kk * 128:(kk + 1) * 128], [xsr], [pr])
                    S.dve(lambda e, pp=pp, half=half: e.tensor_tensor(
                        out=dst[:, half * 4:(half + 1) * 4, :], in0=pp[:, :].rearrange("p (a b) -> p a b", a=4),
                        in1=wn1[:, half * 4:(half + 1) * 4].unsqueeze(2).to_broadcast([128, 4, 128]), op=ALU.mult),
                        reads=[pr, "wn"], writes=[dstres])

            def pre_B1a(t):
                b, kgr, vgr, alr, B = bufs(t)
                hTt, hres = hslot(t)
                tm_proj(hTt, hres, wpre, "big", 0, 512,
                        lambda pp, pr: act(kg[:, b, :], pp[:, :], AF.Copy, [pr], [kgr]))
                fm_proj(walr, "walr", 0, 16, hTt, hres, 128,
                        lambda pp, pr: act(alrT[0:16, b * 128:(b + 1) * 128], pp[0:16, 0:128], AF.Copy, [pr], [alr]))

            def pre_B1b(t):
                b, kgr, vgr, alr, B = bufs(t)
                hTt, hres = hslot(t)
                for i in range(2):
                    tm_proj(hTt, hres, wpre, "big", 512 + i * 512, 512,
                            lambda pp, pr, i=i: act(vg[:, b, i * 512:(i + 1) * 512], pp[:, :], AF.Copy, [pr], [vgr]))
                if t == NPRE - 1:
                    for g in range(2):
                        fm_proj(wkdup[:, :, g, :], "wkdup", 0, 128, hTt, hres, 128,
                                lambda pp, pr, g=g: act(kT[:, g, 0:128], pp[:, 0:128], AF.Copy, [pr], ["kT"]))
                    tm_proj(hTt, hres, wva, "wva", 0, 128,
                            lambda pp, pr: act(vaug[:, 0, :, 0:64], pp[:, 0:128].rearrange("p (g d) -> p g d", g=2), AF.Copy, [pr], ["vaug"]))

            def pre_B2a(t):
                b, kgr, vgr, alr, B = bufs(t)
                pz, pzr = newps()
                mm(pz[:, :], alrT[0:17, b * 128:(b + 1) * 128], w2aug[0:17, :], True, True, [alr, "w2aug"], [pzr])
                act(B["Gf"], pz[:, :], AF.Exp, [pzr], [B["rGf"]], scale=-1.0)
                act(B["Gf"], B["Gf"], AF.Ln, [B["rGf"]], [B["rGf"]], bias=1.0)

            def pre_B2b(t):
                b, kgr, vgr, alr, B = bufs(t)
                Gf_ = B["Gf"]
                pR, pRr = newps()
                mm(pR[:, :], uc128[:], Gf_, True, True, ["uc128", B["rGf"]], [pRr])
                pa, par = newps()
                for h in range(4):
                    mm(pa[:, 2 * h:2 * h + 2], Gf_[:, h * 128:(h + 1) * 128], ind1[:], True, True, [B["rGf"], "ind1"], [par])
                act(B["Dm"], pR[:, :], AF.Exp, [pRr], [B["rDm"]])
                act(B["av"][:], pa[:, 0:8], AF.Exp, [par], [B["rav"]])
                S.dve(lambda e: e.tensor_tensor(out=B["kdec"], in0=kg[:, b, :], in1=B["Dm"], op=ALU.mult),
                      reads=[kgr, B["rDm"]], writes=[B["rkdec"]])

            def pre_B2c(t):
                b, kgr, vgr, alr, B = bufs(t)
                state_update((0, 128), 0, vg[:, b, :], vgr, "any", B=B)

            for t in range(min(3, NPRE)):
                pre_A1(t)
            for t in range(min(2, NPRE)):
                pre_A2(t)
            pre_B1a(0)
            pre_B1b(0)
            for t in range(NPRE):
                if t + 3 < NPRE:
                    pre_A1(t + 3)
                pre_B2a(t)
                if t + 2 < NPRE:
                    pre_A2(t + 2)
                if t + 1 < NPRE:
                    pre_B1a(t + 1)
                pre_B2b(t)
                if t + 1 < NPRE:
                    pre_B1b(t + 1)
                pre_B2c(t)

            chk("prefix")
            for grp in range(NGRP):
                X0 = (lambda *names: list(names)) if grp == 0 else (lambda *names: [])
                for t in range(4):
                    row0 = (NPRE + grp * 4 + t) * 128
                    ld("sp", xg[:, t, :], xall[row0:row0 + 128, :], "xg%d" % t, ["xg%d" % t])
                norm_group4(wn1)
                chk("norm1")
                for i in range(2):
                    wt, wr = wload("w_in", 0, 8, O_QA + i * 512, 512)
                    for c in range(4):
                        fm_proj(wt, wr, c * 128, 128, hT, HT_ALL, 512,
                                lambda pp, pr, cc=i * 4 + c: act(qT[:, cc, :], pp[:, :], AF.Copy, [pr], ["qT"]))
                for g in range(2):
                    fm_proj(wkdup[:, :, g, :], "wkdup", 0, 128, hT, HT_ALL, 512,
                            lambda pp, pr, g=g: act(kT[:, g, 128:640], pp[:, :], AF.Copy, [pr], ["kT"]))
                for t in range(4):
                    tm_proj(hT[:, :, t * 128:(t + 1) * 128], "hT%d" % t, wva, "wva", 0, 128,
                            lambda pp, pr, t=t: act(vaug[:, 1 + t, :, 0:64], pp[:, 0:128].rearrange("p (g d) -> p g d", g=2), AF.Copy, [pr], ["vaug"]))
                wt, wr = wload("w_in", 0, 8, O_QG, 512)
                for c in range(4):
                    fm_proj(wt, wr, c * 128, 128, hT, HT_ALL, 512,
                            lambda pp, pr, c=c: act(qgT[:, c, :], pp[:, :], AF.Copy, [pr], ["qgT"], scale=float(128 ** -0.5)))
                fm_proj(walr, "walr", 0, 16, hT, HT_ALL, 512,
                        lambda pp, pr: act(alrT[0:16, :], pp[0:16, :], AF.Copy, [pr], ["alrT"] + X0("alrTB")))
                wt, wr = wload("w_in", 0, 8, O_KG, 512)
                for t in range(4):
                    tm_proj(hT[:, :, t * 128:(t + 1) * 128], "hT%d" % t, wt, wr, 0, 512,
                            lambda pp, pr, t=t: act(kg[:, t, :], pp[:, :], AF.Copy, [pr], ["kg"] + X0("kgB")))
                for i in range(2):
                    wt, wr = wload("w_in", 0, 8, O_VG + i * 512, 512)
                    for t in range(4):
                        tm_proj(hT[:, :, t * 128:(t + 1) * 128], "hT%d" % t, wt, wr, 0, 512,
                                lambda pp, pr, t=t, i=i: act(vg[:, t, i * 512:(i + 1) * 512], pp[:, :], AF.Copy, [pr], ["vg", "qmT", "PmT", "omT"] + X0("vgB")))
                for i in range(2):
                    wt, wr = wload("w_in", 0, 8, O_GG + i * 512, 512)
                    for t in range(4):
                        tm_proj(hT[:, :, t * 128:(t + 1) * 128], "hT%d" % t, wt, wr, 0, 512,
                                lambda pp, pr, t=t, i=i: act(sg[:, t, i * 512:(i + 1) * 512], pp[:, :], AF.Silu, [pr], ["sg"]))
                for i in range(4):
                    wt, wr = wload("w_in", 0, 8, O_GT + i * 512, 512)
                    for c in range(4):
                        cc = i * 4 + c
                        fm_proj(wt, wr, c * 128, 128, hT, HT_ALL, 512,
                                lambda pp, pr, cc=cc: act(gT[:, cc, :], pp[:, :], AF.Sigmoid, [pr, "bgate"], ["gT"] + X0(*PRE_RES), bias=bgate[:, cc:cc + 1]))

                chk("proj")
                for t in range(4):
                    first = (grp == 0 and t == 0)
                    pO = [(psb[4 + i_], "ps%d" % (4 + i_)) for i_ in range(4)]
                    for g in range(2):
                        for r in range(2):
                            s_ = 2 * g + r
                            PTs, PTr = PT[s_ % 2], "PT%d" % (s_ % 2)
                            pA, pAr = psb[2 * r], "ps%d" % (2 * r)
                            pB, pBr = psb[2 * r + 1], "ps%d" % (2 * r + 1)
                            for j in range(4):
                                i = 4 * g + j
                                q_ap = qT[64 * r:64 * r + 64, i, t * 128:(t + 1) * 128]
                                mm(pA[:, j * 128:(j + 1) * 128], kT[64 * r:64 * r + 64, g, t * 128:(t + 1) * 128], q_ap, True, True, ["kT", "qT"], [pAr])
                                mm(pB[:, j * 128:(j + 1) * 128], kT[64 * r:64 * r + 64, g, (t + 1) * 128:(t + 2) * 128], q_ap, True, True, ["kT", "qT"], [pBr])
                            pAv = pA[:, :].rearrange("p (j q) -> p j q", j=4)
                            pBv = pB[:, :].rearrange("p (j q) -> p j q", j=4)
                            kw = dict(scale=0.125)
                            kwp = dict(scale=0.125, bias=flg[0:64, 0:1]) if first else kw
                            kwp2 = dict(scale=0.125, bias=flg[64:128, 0:1]) if first else kw
                            act(PTs[0:64, 0, :, 0:64], pAv[0:64, :, 0:64], AF.Exp, [pAr, "flg"], [PTr], **kwp)
                            act(PTs[64:128, 0, :, :], pAv[64:128, :, :], AF.Exp, [pAr, "flg"], [PTr], **kwp2)
                            act(PTs[0:64, 1, :, :], pBv[0:64, :, :], AF.Exp, [pBr], [PTr], **kw)
                            act(PTs[64:128, 1, :, 64:128], pBv[64:128, :, 64:128], AF.Exp, [pBr], [PTr], **kw)
                            po, por = pO[s_]
                            for j in range(4):
                                col = j * 65
                                mm(po[:, col:col + 65], PTs[:, 0, j, :], vaug[:, t, g, :], True, False, [PTr, "vaug"], [por])
                                mm(po[:, col:col + 65], PTs[:, 1, j, :], vaug[:, t + 1, g, :], False, True, [PTr, "vaug"], [por])
                    for g in range(2):
                        for r in range(2):
                            s_ = 2 * g + r
                            po, por = pO[s_]
                            pov = po[:, 0:260].rearrange("p (j e) -> p j e", j=4)
                            dv = den[:, s_ * 4:(s_ + 1) * 4]
                            ev = exps[:, g * 8:(g + 1) * 8].rearrange("p (j r) -> p r j", r=2)[:, r, :]
                            ov = oa[:, g * 512:(g + 1) * 512].rearrange("p (j r d) -> p r j d", j=4, r=2)[:, r, :, :]
                            S.dve(lambda e, pov=pov, dv=dv, ev=ev: e.tensor_tensor(out=dv, in0=pov[:, :, 64], in1=ev, op=ALU.add),
                                  reads=[por, "exps"], writes=["den"])
                            S.dve(lambda e, dv=dv: e.reciprocal(out=dv, in_=dv), reads=["den"], writes=["den"])
                            S.dve(lambda e, pov=pov, dv=dv, ov=ov: e.tensor_tensor(
                                out=ov, in0=pov[:, :, 0:64], in1=dv.unsqueeze(2).to_broadcast([128, 4, 64]), op=ALU.mult),
                                reads=[por, "den"], writes=["oa"])
                    dump("oa_%d_%d" % (grp, t), oa[:], 1024, ["oa"])
                    for half in range(2):
                        pp, pr = newps()
                        for k in range(4):
                            kk = half * 4 + k
                            tr(pp[:, k * 128:(k + 1) * 128], oa[:, kk * 128:(kk + 1) * 128], ["oa"], [pr])
                        act(oaT[:, half * 4:(half + 1) * 4, t * 128:(t + 1) * 128], pp[:, :].rearrange("p (a b) -> p a b", a=4), AF.Copy, [pr], ["oaT"])
                S.dve(lambda e: e.tensor_copy(out=kT[:, :, 0:128], in_=kT[:, :, 512:640]), reads=["kT"], writes=["kT"])
                S.dve(lambda e: e.tensor_copy(out=vaug[:, 0, :, :], in_=vaug[:, 4, :, :]), reads=["vaug"], writes=["vaug"])

                chk("swa")
                for t in range(4):
                    gla_gate(t * 128, uc64, "uc64", ind2, "ind2")
                    S.dve(lambda e, t=t: e.tensor_tensor(out=kdec[:], in0=kg[:, t, :], in1=Dm[:], op=ALU.mult), reads=["kg", "Dm"], writes=["kdec"])
                    for j in range(2):
                        state_update((64 * j, 64 * j + 64), j, vg[:, t, :], "vg", "r0" if j == 0 else "r1")
                        act(Sbf[j][:], Sst[:], AF.Copy, ["Sst"], ["Sbf%d" % j])
                        pA, pAr = newps()
                        pB, pBr = newps()
                        for h in range(4):
                            pp, pr = (pA, pAr) if h < 2 else (pB, pBr)
                            mm(pp[:, (h % 2) * 256:(h % 2 + 1) * 256], qgT[:, h, t * 128:(t + 1) * 128],
                               Sbf[j][:, h * 256:(h + 1) * 256], True, True, ["qgT", "Sbf%d" % j], [pr])
                        act(og[64 * j:64 * j + 64, 0:512], pA[64 * j:64 * j + 64, :], AF.Copy, [pAr], ["og"])
                        act(og[64 * j:64 * j + 64, 512:1024], pB[64 * j:64 * j + 64, :], AF.Copy, [pBr], ["og"])
                    for h in range(4):
                        act(junk[:, 0:256], og[:, h * 256:(h + 1) * 256], AF.Square, ["og"], ["junk", "ss"], accum_out=ss[:, 4 + h:5 + h])
                    rstd_from_ss(4, 1.0 / 256, ncol=4)
                    for h in range(4):
                        S.dve(lambda e, h=h, t=t: e.scalar_tensor_tensor(
                            out=og[:, h * 256:(h + 1) * 256], in0=og[:, h * 256:(h + 1) * 256], scalar=rs[:, 4 + h:5 + h],
                            in1=sg[:, t, h * 256:(h + 1) * 256], op0=ALU.mult, op1=ALU.mult),
                            reads=["og", "rs", "sg"], writes=["og"])
                    dump("og_%d_%d" % (grp, t), og[:], 1024, ["og"])
                    for half in range(2):
                        pp, pr = newps()
                        for k in range(4):
                            kk = half * 4 + k
                            tr(pp[:, k * 128:(k + 1) * 128], og[:, kk * 128:(kk + 1) * 128], ["og"], [pr])
                        S.dve(lambda e, pp=pp, half=half, t=t: e.tensor_tensor(
                            out=ogT[:, half * 4:(half + 1) * 4, t * 128:(t + 1) * 128], in0=pp[:, :].rearrange("p (a b) -> p a b", a=4),
                            in1=gnw8[:, half * 4:(half + 1) * 4].unsqueeze(2).to_broadcast([128, 4, 128]), op=ALU.mult),
                            reads=[pr, "gnw8"], writes=["ogT"])

                chk("gla")
                for i in range(2):
                    wt, wr = wload("w_attn_o", 0, 8, i * 512, 512)
                    for c in range(4):
                        cc = i * 4 + c
                        fm_proj(wt, wr, c * 128, 128, oaT, "oaT", 512,
                                lambda pp, pr, cc=cc: S.dve(lambda e: e.tensor_tensor(out=m1[:, cc, :], in0=pp[:, :], in1=gT[:, cc, :], op=ALU.mult),
                                                            reads=[pr, "gT"], writes=["sg"]))
                for i in range(2):
                    wt, wr = wload("w_gla_o", 0, 8, i * 512, 512)
                    for c in range(4):
                        cc = i * 4 + c

                        def ev(pp, pr, cc=cc):
                            S.dve(lambda e: e.tensor_tensor(out=mtmp[:], in0=pp[:, :], in1=gT[:, 8 + cc, :], op=ALU.mult),
                                  reads=[pr, "gT"], writes=["Gf"])
                            S.dve(lambda e: e.tensor_tensor(out=mT[:, cc, :], in0=mtmp[:], in1=m1[:, cc, :], op=ALU.add),
                                  reads=["Gf", "sg"], writes=["qT"])
                        fm_proj(wt, wr, c * 128, 128, ogT, "ogT", 512, ev)
                for n in range(2):
                    wt, wr = wload("w_mix_o", 0, 8, n * 512, 512)
                    for t in range(4):
                        tm_proj(mT[:, :, t * 128:(t + 1) * 128], "qT", wt, wr, 0, 512,
                                lambda pp, pr, t=t, n=n: S.dve(lambda e: e.tensor_tensor(
                                    out=xg[:, t, n * 512:(n + 1) * 512], in0=pp[:, :], in1=xg[:, t, n * 512:(n + 1) * 512], op=ALU.add),
                                    reads=[pr, "xg%d" % t], writes=["xg%d" % t]))

                for t in range(4):
                    dump("x1_%d_%d" % (grp, t), xg[:, t, :], 1024, ["xg%d" % t])
                chk("merge")
                norm_group4(wn2)
                for c in range(2):
                    fm_proj(wmq, "wmq", c * 128, 128, hT, HT_ALL, 512,
                            lambda pp, pr, c=c: act(qmT[:, c, :], pp[:, :], AF.Copy, [pr], ["qmT", "vg"]))
                for t in range(4):
                    for r in range(2):
                        pp, pr = newps("r0" if r == 0 else "r1")
                        for mt in range(2):
                            for c in range(2):
                                col = (mt * 2 + c) * 128
                                mm(pp[:, col:col + 128], kmT[64 * r:64 * r + 64, c, mt * 128:(mt + 1) * 128],
                                   qmT[64 * r:64 * r + 64, c, t * 128:(t + 1) * 128], True, True, ["kmT", "qmT"], [pr])
                        act(PmT[:, r, :, :, :], pp[:, :].rearrange("p (m c q) -> p m c q", m=2, c=2), AF.Exp, [pr], ["PmT", "vg"], scale=0.125)
                    po, por = newps()
                    for c in range(2):
                        for r in range(2):
                            h = 2 * c + r
                            for mt in range(2):
                                mm(po[:, h * 65:(h + 1) * 65], PmT[:, r, mt, c, :], vmaug[:, mt, h, :], mt == 0, mt == 1, ["PmT", "vmaug"], [por])
                    pov = po[:, 0:260].rearrange("p (h e) -> p h e", h=4)
                    S.dve(lambda e, pov=pov: e.reciprocal(out=den[:, 0:4], in_=pov[:, :, 64]), reads=[por], writes=["den"])
                    S.dve(lambda e, pov=pov: e.tensor_tensor(out=om[:, :].rearrange("p (h d) -> p h d", h=4), in0=pov[:, :, 0:64],
                                                             in1=den[:, 0:4].unsqueeze(2).to_broadcast([128, 4, 64]), op=ALU.mult),
                          reads=[por, "den"], writes=["om"])
                    pp, pr = newps()
                    for k in range(2):
                        tr(pp[:, k * 128:(k + 1) * 128], om[:, k * 128:(k + 1) * 128], ["om"], [pr])
                    act(omT[:, :, t * 128:(t + 1) * 128], pp[:, 0:256].rearrange("p (a b) -> p a b", a=2), AF.Copy, [pr], ["omT", "vg"])
                for t in range(4):
                    for n in range(2):
                        pp, pr = newps()
                        for c in range(2):
                            mm(pp[:, :], omT[:, c, t * 128:(t + 1) * 128], wmo[:, c, n * 512:(n + 1) * 512], c == 0, c == 1, ["omT", "wmo"], [pr])
                        S.dve(lambda e, pp=pp, t=t, n=n: e.tensor_tensor(
                            out=xg[:, t, n * 512:(n + 1) * 512], in0=pp[:, :], in1=xg[:, t, n * 512:(n + 1) * 512], op=ALU.add),
                            reads=[pr, "xg%d" % t], writes=["xg%d" % t])

                for t in range(4):
                    dump("x2_%d_%d" % (grp, t), xg[:, t, :], 1024, ["xg%d" % t])
                chk("xattn")
                norm_group4(wn3)
                for i in range(6):
                    ncol = 512 if i < 5 else 256
                    wg_, wgr = wload("w_ffn_gate", 0, 8, i * 512, ncol)
                    wu_, wur = wload("w_ffn_up", 0, 8, i * 512, ncol)
                    for c in range(ncol // 128):
                        fc = i * 4 + c
                        slt, slr = sl[fc % 2], "sl%d" % (fc % 2)
                        fm_proj(wg_, wgr, c * 128, 128, hT, HT_ALL, 512,
                                lambda pp, pr, slt=slt, slr=slr: act(slt[:], pp[:, :], AF.Silu, [pr], [slr]))
                        fm_proj(wu_, wur, c * 128, 128, hT, HT_ALL, 512,
                                lambda pp, pr, slt=slt, slr=slr, fc=fc: S.dve(lambda e: e.tensor_tensor(
                                    out=actT[:, fc, :], in0=pp[:, :], in1=slt[:], op=ALU.mult), reads=[pr, slr], writes=["big"]))
                for n in range(2):
                    pst = [newps() for _ in range(4)]
                    for piece, (k0, nk) in enumerate(((0, 8), (8, 8), (16, 6))):
                        wt, wr = wload("w_ffn_down", k0, nk, n * 512, 512)
                        for t in range(4):
                            pp, pr = pst[t]
                            for k in range(nk):
                                kk = k0 + k
                                mm(pp[:, :], actT[:, kk, t * 128:(t + 1) * 128], wt[:, k, :], kk == 0, kk == 21, ["big", wr], [pr])
                    for t in range(4):
                        pp, pr = pst[t]
                        S.dve(lambda e, pp=pp, t=t, n=n: e.tensor_tensor(
                            out=xg[:, t, n * 512:(n + 1) * 512], in0=pp[:, :], in1=xg[:, t, n * 512:(n + 1) * 512], op=ALU.add),
                            reads=[pr, "xg%d" % t], writes=["xg%d" % t])

                chk("ffn")
                for t in range(4):
                    ys_, ysr = XS[t % 2]
                    act(junk[:], xg[:, t, :], AF.Square, ["xg%d" % t], ["junk", "ss2"], accum_out=ss[:, 2:3])
                    rstd_from_ss(2, 1.0 / D, sres="ss2", rres="rs2")
                    S.dve(lambda e, t=t, ys_=ys_: e.scalar_tensor_tensor(out=ys_[:], in0=xg[:, t, :], scalar=rs[:, 2:3], in1=wfin[:],
                                                                         op0=ALU.mult, op1=ALU.mult),
                          reads=["xg%d" % t, "rs2", "wfin"], writes=[ysr])
                    row0 = (grp * 4 + t) * 128
                    ld("sp", yout[row0:row0 + 128, :], ys_[:], "yout%d" % (t % 2), [], reads=[ysr])

        except _Stop:
            pass
        S.emit(final_wait_keys=[k for k in ["yout0", "yout1"] + dbg_keys if k in S.dma_count])
    return nc


_CONST_CACHE = {}


def _consts():
    if not _CONST_CACHE:
        ident = np.eye(128, dtype=np.float32)
        a = np.arange(128)
        u64 = ((a[:, None] // 64 == a[None, :] // 64) & (a[:, None] > a[None, :])).astype(np.float32) * (-1.0 / 16)
        u128 = (a[:, None] > a[None, :]).astype(np.float32) * (-1.0 / 16)
        ind2 = np.zeros((128, 2), np.float32)
        ind2[:64, 0] = -1.0 / 16
        ind2[64:, 1] = -1.0 / 16
        ind1 = np.full((128, 2), -1.0 / 16, np.float32)
        _CONST_CACHE.update(ident=ident, ucum64=u64, ucum128=u128, ind2=ind2, ind1=ind1)
    return _CONST_CACHE


def make_in_maps(inputs, ncore=NCORE, tok_core=TOK_CORE, npre_tiles=48):
    x = np.asarray(inputs["x"], dtype=np.float32)
    memv = np.asarray(inputs["mem"], dtype=np.float32)
    B, SEQ, _ = x.shape
    per_b = SEQ // tok_core
    c = _consts()
    maps = []
    for core in range(ncore):
        b, j = core // per_b, core % per_b
        npre = npre_tiles * 128
        xa = np.zeros((npre + tok_core, D), np.float32)
        end = (j + 1) * tok_core
        start = max(0, j * tok_core - npre)
        seg = x[b, start:end]
        xa[npre + tok_core - seg.shape[0]:] = seg
        m = dict(xall=xa, mem=np.ascontiguousarray(memv[b]),
                 flagb=np.full((128, 1), 0.0 if j > 0 else -30000.0, np.float32))
        m.update(c)
        for k, v in inputs.items():
            if k in ("x", "mem"):
                continue
            v = np.asarray(v, dtype=np.float32)
            m[k] = np.ascontiguousarray(v[0]) if k != "norm_final_w" else np.ascontiguousarray(v)
        maps.append(m)
    return maps


_NC_CACHE = {}


def kernel(**inputs):
    if "nc" not in _NC_CACHE:
        _NC_CACHE["nc"] = build()
    nc = _NC_CACHE["nc"]
    maps = make_in_maps(inputs)
    res = run_bass_kernel_spmd(nc, maps, core_ids=list(range(NCORE)))
    x = inputs["x"]
    B, SEQ, _ = x.shape
    per_b = SEQ // TOK_CORE
    out = np.empty((B, SEQ, D), np.float32)
    for core in range(NCORE):
        b, j = core // per_b, core % per_b
        out[b, j * TOK_CORE:(j + 1) * TOK_CORE] = res.results[core]["y"]
    return out
```
